# Optimizing a Trainium2 kernel written in Bass

```python
import jax, jax.numpy as jnp
from jax import lax
import numpy as np

D_MODEL = 2048
BATCH = 4
SEQ = 4096
DEPTH = 4

CHUNK = 64
EPS = 1e-6
ROPE_BASE = 10000.0
MLA_HEADS = 16
MLA_NOPE = 128
MLA_ROPE = 64
MLA_V = 128
Q_LORA = 512
KV_LORA = 512
Q_BLOCK = 128
RET_HEADS = 8
RET_QK = 256
RET_V = 256
RET_GN_EPS = 1e-5
GM_GROUPS = 4
GM_WIDTH = 2048
GM_BLOCK = 128
BRANCH_W = 2048
N_BRANCH = 3
D_FF = 5504
IN_SIZES = (Q_LORA, KV_LORA, MLA_ROPE, RET_HEADS * RET_QK, RET_HEADS * RET_QK, RET_HEADS * RET_V, RET_HEADS * RET_V, GM_WIDTH, GM_WIDTH)
IN_SPLITS = tuple(int(v) for v in np.cumsum(IN_SIZES)[:-1])
D_IN = int(sum(IN_SIZES))

kernel_name = 'hybrid_mla_retention_gmlp_macaron'


def rms_norm(x, g):
    x32 = x.astype(jnp.float32)
    y = x32 * lax.rsqrt(jnp.mean(x32 * x32, axis=-1, keepdims=True) + EPS)
    return (y * g.astype(jnp.float32)).astype(x.dtype)


def layer_norm(x, g, b):
    x32 = x.astype(jnp.float32)
    mu = jnp.mean(x32, axis=-1, keepdims=True)
    var = jnp.mean(jnp.square(x32 - mu), axis=-1, keepdims=True)
    y = (x32 - mu) * lax.rsqrt(var + EPS)
    return (y * g.astype(jnp.float32) + b.astype(jnp.float32)).astype(x.dtype)


def rope_tables(pos, dim):
    inv = ROPE_BASE ** (-jnp.arange(0, dim, 2, dtype=jnp.float32) / dim)
    ang = pos.astype(jnp.float32)[..., None] * inv
    return jnp.cos(ang), jnp.sin(ang)


def apply_rope(x, cos, sin):
    x32 = x.astype(jnp.float32)
    x1, x2 = jnp.split(x32, 2, axis=-1)
    c = cos[:, :, None, :]
    s = sin[:, :, None, :]
    return jnp.concatenate([x1 * c - x2 * s, x2 * c + x1 * s], axis=-1).astype(x.dtype)


def swiglu(h, wi, wo):
    a, b = jnp.split(h @ wi, 2, axis=-1)
    return (jax.nn.silu(a) * b) @ wo


def mla_branch(z_q, z_kv, z_kr, q_norm_g, w_uq, kv_norm_g, w_ukv, cos_r, sin_r):
    B, S, _ = z_q.shape
    q = (rms_norm(z_q, q_norm_g) @ w_uq).reshape(B, S, MLA_HEADS, MLA_NOPE + MLA_ROPE)
    q_nope = q[..., :MLA_NOPE]
    q_rope = apply_rope(q[..., MLA_NOPE:], cos_r, sin_r)
    kv = (rms_norm(z_kv, kv_norm_g) @ w_ukv).reshape(B, S, MLA_HEADS, MLA_NOPE + MLA_V)
    k_nope = kv[..., :MLA_NOPE]
    v = kv[..., MLA_NOPE:]
    k_rope = apply_rope(z_kr[:, :, None, :], cos_r, sin_r)[:, :, 0, :]
    scale = (MLA_NOPE + MLA_ROPE) ** -0.5
    nb = S // Q_BLOCK
    key_chunk = jnp.arange(S) // CHUNK

    def block(args):
        qn, qr, bi = args
        s = jnp.einsum('bqhd,bkhd->bhqk', qn, k_nope) + jnp.einsum('bqhr,bkr->bhqk', qr, k_rope)
        s = s.astype(jnp.float32) * scale
        q_chunk = (bi * Q_BLOCK + jnp.arange(Q_BLOCK)) // CHUNK
        mask = key_chunk[None, :] <= q_chunk[:, None]
        p = jax.nn.softmax(jnp.where(mask, s, -1e30), axis=-1).astype(v.dtype)
        return jnp.einsum('bhqk,bkhe->bqhe', p, v)

    qn_b = q_nope.reshape(B, nb, Q_BLOCK, MLA_HEADS, MLA_NOPE).transpose(1, 0, 2, 3, 4)
    qr_b = q_rope.reshape(B, nb, Q_BLOCK, MLA_HEADS, MLA_ROPE).transpose(1, 0, 2, 3, 4)
    o = lax.map(block, (qn_b, qr_b, jnp.arange(nb)))
    return o.transpose(1, 0, 2, 3, 4).reshape(B, S, MLA_HEADS * MLA_V)


def retention_branch(z_q, z_k, z_v, z_g, cos_k, sin_k):
    B, S, _ = z_q.shape
    f32 = jnp.float32
    q = apply_rope(z_q.reshape(B, S, RET_HEADS, RET_QK), cos_k, sin_k).astype(f32)
    k = apply_rope(z_k.reshape(B, S, RET_HEADS, RET_QK), cos_k, sin_k).astype(f32) * RET_QK ** -0.5
    v = z_v.reshape(B, S, RET_HEADS, RET_V).astype(f32)
    log_g = jnp.log1p(-(2.0 ** (-5.0 - jnp.arange(RET_HEADS, dtype=f32))))
    idx = jnp.arange(CHUNK, dtype=f32)
    decay_intra = jnp.exp(jnp.abs(idx[:, None] - idx[None, :])[None] * log_g[:, None, None])
    q_decay = jnp.exp((idx[:, None] + 1.0) * log_g[None, :])
    k_decay = jnp.exp((CHUNK - 1.0 - idx)[:, None] * log_g[None, :])
    chunk_decay = jnp.exp(CHUNK * log_g)
    nc = S // CHUNK

    def to_chunks(t):
        return t.reshape(B, nc, CHUNK, RET_HEADS, t.shape[-1]).transpose(1, 0, 2, 3, 4)

    def step(state, inp):
        qc, kc, vc = inp
        s = jnp.einsum('bihd,bjhd->bhij', qc, kc) * decay_intra
        o = jnp.einsum('bhij,bjhe->bihe', s, vc)
        o = o + jnp.einsum('bihd,bhde->bihe', qc * q_decay[None, :, :, None], state)
        state = state * chunk_decay[None, :, None, None] + jnp.einsum('bjhd,bjhe->bhde', kc * k_decay[None, :, :, None], vc)
        return state, o

    state0 = jnp.zeros((B, RET_HEADS, RET_QK, RET_V), f32)
    _, o = lax.scan(step, state0, (to_chunks(q), to_chunks(k), to_chunks(v)))
    o = o.transpose(1, 0, 2, 3, 4).reshape(B, S, RET_HEADS, RET_V)
    mu = jnp.mean(o, axis=-1, keepdims=True)
    var = jnp.mean(jnp.square(o - mu), axis=-1, keepdims=True)
    o = ((o - mu) * lax.rsqrt(var + RET_GN_EPS)).reshape(B, S, RET_HEADS * RET_V)
    return (jax.nn.silu(z_g.astype(f32)) * o).astype(z_g.dtype)


def gmlp_branch(z_u, z_v, ln_g, ln_b, w_s, b_s):
    B, S, _ = z_u.shape
    u = jax.nn.gelu(z_u)
    v = layer_norm(jax.nn.gelu(z_v), ln_g, ln_b)
    nb = S // GM_BLOCK
    vb = v.reshape(B, nb, GM_BLOCK, GM_GROUPS, GM_WIDTH // GM_GROUPS)
    pc = jnp.arange(GM_BLOCK) // CHUNK
    mask = pc[:, None] >= pc[None, :]
    w = jnp.where(mask[None], w_s, 0.0).astype(v.dtype)
    mixed = jnp.einsum('gij,bnjgc->bnigc', w, vb) + b_s.T[None, None, :, :, None]
    return u * mixed.reshape(B, S, GM_WIDTH)


def token_mixer(h, w_in, q_norm_g, w_uq, kv_norm_g, w_ukv, gm_ln_g, gm_ln_b, gm_w_s, gm_b_s, w_gate, b_gate, w_br, w_o, cos_r, sin_r, cos_k, sin_k):
    B, S, D = h.shape
    z = h @ w_in
    zq, zkv, zkr, rq, rk, rv, rg, gu, gv = jnp.split(z, IN_SPLITS, axis=-1)
    y_a = mla_branch(zq, zkv, zkr, q_norm_g, w_uq, kv_norm_g, w_ukv, cos_r, sin_r)
    y_b = retention_branch(rq, rk, rv, rg, cos_k, sin_k)
    y_c = gmlp_branch(gu, gv, gm_ln_g, gm_ln_b, gm_w_s, gm_b_s)
    gates = jax.nn.sigmoid(h @ w_gate + b_gate).reshape(B, S, N_BRANCH, D)
    merged = gates[:, :, 0] * (y_a @ w_br[0]) + gates[:, :, 1] * (y_b @ w_br[1]) + gates[:, :, 2] * (y_c @ w_br[2])
    return merged @ w_o


def setup_inputs(seed: int = 0) -> dict:
    key = jax.random.key(seed)
    k = jax.random.split(key, 26)
    f32 = jnp.float32
    L = DEPTH

    def w(kk, shape, fan_in):
        return jax.random.normal(kk, shape, f32) * fan_in ** -0.5

    def gain(kk, shape):
        return 1.0 + 0.05 * jax.random.normal(kk, shape, f32)

    def small(kk, shape):
        return 0.01 * jax.random.normal(kk, shape, f32)

    x = jax.random.normal(k[0], (BATCH, SEQ, D_MODEL), f32)
    offset = jax.random.randint(k[1], (BATCH, 1), 0, 1024) * CHUNK
    pos = (offset + jnp.arange(SEQ)[None, :]).astype(jnp.int32)
    return {
        'x': x,
        'pos': pos,
        'ffn1_pre_g': gain(k[2], (L, D_MODEL)),
        'ffn1_wi': w(k[3], (L, D_MODEL, 2 * D_FF), D_MODEL),
        'ffn1_wo': w(k[4], (L, D_FF, D_MODEL), D_FF),
        'ffn1_post_g': gain(k[5], (L, D_MODEL)),
        'mix_pre_g': gain(k[6], (L, D_MODEL)),
        'w_in': w(k[7], (L, D_MODEL, D_IN), D_MODEL),
        'q_norm_g': gain(k[8], (L, Q_LORA)),
        'w_uq': w(k[9], (L, Q_LORA, MLA_HEADS * (MLA_NOPE + MLA_ROPE)), Q_LORA),
        'kv_norm_g': gain(k[10], (L, KV_LORA)),
        'w_ukv': w(k[11], (L, KV_LORA, MLA_HEADS * (MLA_NOPE + MLA_V)), KV_LORA),
        'gm_ln_g': gain(k[12], (L, GM_WIDTH)),
        'gm_ln_b': small(k[13], (L, GM_WIDTH)),
        'gm_w_s': w(k[14], (L, GM_GROUPS, GM_BLOCK, GM_BLOCK), GM_BLOCK),
        'gm_b_s': 1.0 + 0.1 * jax.random.normal(k[15], (L, GM_GROUPS, GM_BLOCK), f32),
        'w_gate': w(k[16], (L, D_MODEL, N_BRANCH * D_MODEL), D_MODEL),
        'b_gate': small(k[17], (L, N_BRANCH * D_MODEL)),
        'w_br': w(k[18], (L, N_BRANCH, BRANCH_W, D_MODEL), BRANCH_W),
        'w_o': w(k[19], (L, D_MODEL, D_MODEL), D_MODEL),
        'mix_post_g': gain(k[20], (L, D_MODEL)),
        'ffn2_pre_g': gain(k[21], (L, D_MODEL)),
        'ffn2_wi': w(k[22], (L, D_MODEL, 2 * D_FF), D_MODEL),
        'ffn2_wo': w(k[23], (L, D_FF, D_MODEL), D_FF),
        'ffn2_post_g': gain(k[24], (L, D_MODEL)),
    }


def reference(x, pos, ffn1_pre_g, ffn1_wi, ffn1_wo, ffn1_post_g, mix_pre_g, w_in, q_norm_g, w_uq, kv_norm_g, w_ukv, gm_ln_g, gm_ln_b, gm_w_s, gm_b_s, w_gate, b_gate, w_br, w_o, mix_post_g, ffn2_pre_g, ffn2_wi, ffn2_wo, ffn2_post_g):
    cos_r, sin_r = rope_tables(pos, MLA_ROPE)
    cos_k, sin_k = rope_tables(pos, RET_QK)
    for l in range(DEPTH):
        x = x + 0.5 * rms_norm(swiglu(rms_norm(x, ffn1_pre_g[l]), ffn1_wi[l], ffn1_wo[l]), ffn1_post_g[l])
        m = token_mixer(rms_norm(x, mix_pre_g[l]), w_in[l], q_norm_g[l], w_uq[l], kv_norm_g[l], w_ukv[l], gm_ln_g[l], gm_ln_b[l], gm_w_s[l], gm_b_s[l], w_gate[l], b_gate[l], w_br[l], w_o[l], cos_r, sin_r, cos_k, sin_k)
        x = x + rms_norm(m, mix_post_g[l])
        x = x + 0.5 * rms_norm(swiglu(rms_norm(x, ffn2_pre_g[l]), ffn2_wi[l], ffn2_wo[l]), ffn2_post_g[l])
    return x
```

```python
import numpy as np
from contextlib import ExitStack
import concourse.bass as bass
import concourse.mybir as mybir
from concourse.bass_utils import run_bass_kernel_spmd

F32, BF16, I32 = mybir.dt.float32, mybir.dt.bfloat16, mybir.dt.int32
ALU = mybir.AluOpType
AF = mybir.ActivationFunctionType

D = 2048
DFF = 5504
NHC = 43
TT = 512
EPS = 1e-6
NV_L = 152
C_MASKA, C_DT, C_QDEC, C_KDEC, C_MASKG, C_INV, C_INVROW, NC_TOT = 0, 2048, 3072, 4096, 4104, 4232, 4240, 4368
V_F1PRE, V_F1POST, V_MIXPRE, V_QN, V_KVN, V_BG, V_MIXPOST, V_F2PRE, V_F2POST = 0, 16, 32, 48, 52, 56, 104, 120, 136


class Tk:
    __slots__ = ("name", "w", "r")

    def __init__(self, name):
        self.name = name
        self.w = None
        self.r = {}


class Prog:
    ENG = ("pe", "act", "dve", "pool", "sp")

    def __init__(self, ndma=12):
        self.ops = {e: [] for e in self.ENG}
        self.cnt = {}
        self.seen = {e: {} for e in self.ENG}
        self.rr = {e: 0 for e in self.ENG}
        self.ndma = ndma

    def op(self, eng, fn, reads=(), writes=(), dma=False):
        need = {}

        def add(k, v):
            if need.get(k, 0) < v:
                need[k] = v

        for t in reads:
            if t.w is not None:
                add(*t.w)
        for t in writes:
            if t.w is not None:
                add(*t.w)
            for k, v in t.r.items():
                add(k, v)
        own = "c_" + eng
        waits = []
        seen = self.seen[eng]
        for k, v in need.items():
            if k == own and eng == "pe":
                continue
            if seen.get(k, 0) >= v:
                continue
            seen[k] = v
            waits.append((k, v))
        if dma:
            j = self.rr[eng]
            self.rr[eng] = (j + 1) % self.ndma
            k = "d_%s_%d" % (eng, j)
            prev = self.cnt.get(k, 0)
            if prev and seen.get(k, 0) < prev:
                seen[k] = prev
                waits.append((k, prev))
            self.cnt[k] = prev + 16
            tok = (k, prev + 16)
            inc = 16
        else:
            k = own
            self.cnt[k] = self.cnt.get(k, 0) + 1
            tok = (k, self.cnt[k])
            inc = 1
        self.ops[eng].append((waits, fn, k, inc))
        for t in writes:
            t.w = tok
            t.r = {}
        for t in reads:
            if t.r.get(k, 0) < tok[1]:
                t.r[k] = tok[1]
        return tok

    def barrier(self, engines=None):
        for e in (engines or self.ENG):
            waits = []
            seen = self.seen[e]
            for k, v in self.cnt.items():
                if e == "pe" and k == "c_pe":
                    continue
                if seen.get(k, 0) < v:
                    seen[k] = v
                    waits.append((k, v))
            if waits:
                self.ops[e].append((waits, None, None, 0))

    def replay(self, eng, e, sems):
        for waits, fn, k, inc in self.ops[eng]:
            for wk, wv in waits:
                e.wait_ge(sems[wk], wv)
            if fn is not None:
                ins = fn(e)
                ins.then_inc(sems[k], inc)


class Arena:
    def __init__(self, tensor, nbytes):
        self.t32 = tensor
        self.t16 = tensor.bitcast(BF16)
        self.off = 0
        self.nbytes = nbytes

    def reset(self, off=0):
        self.off = off

    def alloc(self, cols, dt, parts=128):
        esz = 4 if dt in (F32, I32) else 2
        nb = (cols * esz + 31) // 32 * 32
        o = self.off
        assert o + nb <= self.nbytes, ("arena overflow", o, nb, self.nbytes)
        self.off = o + nb
        if dt == F32:
            return self.t32[0:parts, o // 4:o // 4 + cols]
        if dt == I32:
            return self.t32.bitcast(I32)[0:parts, o // 4:o // 4 + cols]
        return self.t16[0:parts, o // 2:o // 2 + cols]


def build_program(S, L, stages=("f1", "mix", "f2"), dbg=False):
    NT = S // TT
    nc = bass.Bass("TRN2", target_bir_lowering=False)
    P = Prog()
    di = {}

    def dram_in(name, shape, dt=F32):
        di[name] = nc.dram_tensor(name, list(shape), dt, kind="ExternalInput")
        return di[name]

    xT = dram_in("xT", [D, S])
    vecs_d = dram_in("vecs", [128, NV_L * L])
    f1wi = dram_in("f1wi", [L * NHC, 128, 16 * 256])
    f1wo = dram_in("f1wo", [L * 16, 128, DFF])
    f2wi = dram_in("f2wi", [L * NHC, 128, 16 * 256])
    f2wo = dram_in("f2wo", [L * 16, 128, DFF])
    yT = nc.dram_tensor("yT", [D, S], F32, kind="ExternalOutput")
    NBLK = S // 128
    if "mix" in stages:
        consts_d = dram_in("consts", [128, NC_TOT])
        posb_d = dram_in("posb", [128, S], I32)
        post_d = dram_in("post", [128, NBLK], I32)
        winfm = dram_in("winfm", [L * 73, 128, 2048])
        wintm = dram_in("wintm", [L * 12, 128, 16 * 512])
        wuq_d = dram_in("wuq", [L, 128, 4 * 16 * 256])
        wukv_d = dram_in("wukv", [L, 128, 4 * 4096])
        wg_d = dram_in("wg", [L * 48, 128, 2048])
        wbr_d = dram_in("wbr", [L * 48, 128, 2048])
        wo2_d = dram_in("wo2", [L * 16, 128, 2048])
        lng_d = dram_in("lng", [L, 128, 2048])
        lnb_d = dram_in("lnb", [L, 128, 2048])
        wst_d = dram_in("wst", [L, 128, 512])
        gmb_d = dram_in("gmb", [L, 128, 2048])

        def scr(name, shape, dt=BF16):
            return nc.dram_tensor(name, list(shape), dt).ap()
        cosk_s, sink_s = scr("cosk_s", [128, S], F32), scr("sink_s", [128, S], F32)
        c64_s, s64_s = scr("c64_s", [64, S], F32), scr("s64_s", [64, S], F32)
        costm_s, sintm_s = scr("costm_s", [S, 128], F32), scr("sintm_s", [S, 128], F32)
        qn_s, qr_s, kT_s = scr("qn_s", [16, 128, S]), scr("qr_s", [16, 64, S]), scr("kT_s", [16, 128, S])
        kr_s = scr("kr_s", [64, S])
        v_s = scr("v_s", [S, 2048])
        rq_s, rkT_s = scr("rq_s", [16, 128, S]), scr("rkT_s", [16, 128, S])
        rktm_s, rvtm_s, gvtm_s = scr("rktm_s", [S, 2048]), scr("rvtm_s", [S, 2048]), scr("gvtm_s", [S, 2048])
        rg_s, gu_s = scr("rg_s", [16, 128, S]), scr("gu_s", [16, 128, S])
        ya_s, yb_s, yc_s = scr("ya_s", [16, 128, S]), scr("yb_s", [16, 128, S]), scr("yc_s", [16, 128, S])

    xT_v = xT.ap().rearrange("(kc p) s -> p kc s", p=128)
    yT_v = yT.ap().rearrange("(kc p) s -> p kc s", p=128)

    ARENA_BYTES = 190 * 1024
    with ExitStack() as es:
        arena_t = es.enter_context(nc.sbuf_tensor("arena", [128, ARENA_BYTES // 4], F32))
        ar = Arena(arena_t, ARENA_BYTES)
        ps = [es.enter_context(nc.psum_tensor("ps%d" % i, [128, 512], F32)) for i in range(8)]
        Tps = [Tk("ps%d" % i) for i in range(8)]

        ones = ar.alloc(128, F32)
        Tones = Tk("ones")
        vecs = ar.alloc(NV_L * L, F32)
        Tvecs = Tk("vecs")
        base_off = ar.off

        P.op("pool", lambda e: e.memset(ones, 1.0), writes=[Tones])
        P.op("sp", lambda e: e.dma_start(out=vecs, in_=vecs_d.ap()), writes=[Tvecs], dma=True)

        TY = [Tk("y%d" % t) for t in range(NT)]
        state = {"first": False}
        for t in range(NT):
            P.op("sp", lambda e, t=t: e.dma_start(out=yT.ap()[:, t * TT:(t + 1) * TT], in_=xT.ap()[:, t * TT:(t + 1) * TT]),
                 writes=[TY[t]], dma=True)

        def vcol(l, base, kc):
            c = l * NV_L + base + kc
            return vecs[:, c:c + 1]

        def ffn_phase(l, wi_d, wo_d, vpre, vpost):
            ar.reset(base_off)
            X = ar.alloc(16 * TT, F32).rearrange("p (k t) -> p k t", k=16)
            O = ar.alloc(16 * TT, F32).rearrange("p (k t) -> p k t", k=16)
            H = ar.alloc(16 * TT, BF16).rearrange("p (k t) -> p k t", k=16)
            A = ar.alloc(NHC * TT, BF16).rearrange("p (k t) -> p k t", k=NHC)
            WA = [ar.alloc(4096, BF16) for _ in range(3)]
            WB = [ar.alloc(DFF, BF16) for _ in range(2)]
            SQ = [ar.alloc(TT, F32) for _ in range(2)]
            SA = [ar.alloc(TT, F32) for _ in range(2)]
            RS = ar.alloc(TT, F32)
            TMP = [ar.alloc(TT, F32) for _ in range(2)]
            TX, TO, TH, TA, TRS = Tk("X"), Tk("O"), Tk("H"), Tk("A"), Tk("RS")
            TWA = [Tk("WA%d" % i) for i in range(3)]
            TWB = [Tk("WB%d" % i) for i in range(2)]
            TSQ = [Tk("SQ%d" % i) for i in range(2)]
            TSA = [Tk("SA%d" % i) for i in range(2)]
            TTMP = [Tk("TMP%d" % i) for i in range(2)]
            first = state["first"]
            state["first"] = False
            src_v = xT_v if first else yT_v
            for t in range(NT):
                tsl = slice(t * TT, (t + 1) * TT)
                P.op("sp", lambda e, tsl=tsl: e.dma_start(out=X, in_=src_v[:, :, tsl]),
                     reads=([] if first else [TY[t]]), writes=[TX], dma=True)
                rms_stats(X, TX, SQ, TSQ, ps[6], Tps[6], RS, TRS, 16, D)
                for kc in range(16):
                    P.op("dve", lambda e, kc=kc: e.scalar_tensor_tensor(
                        out=H[:, kc, :], in0=X[:, kc, :], scalar=vcol(l, vpre, kc), in1=RS,
                        op0=ALU.mult, op1=ALU.mult), reads=[TX, TRS, Tvecs], writes=[TH])
                for hc in range(NHC):
                    ws = hc % 3
                    P.op("pool", lambda e, ws=ws, hc=hc: e.dma_start(out=WA[ws], in_=wi_d.ap()[l * NHC + hc]),
                         writes=[TWA[ws]], dma=True)
                    ia, ib = (hc % 2) * 2, (hc % 2) * 2 + 1

                    def mm(e, ws=ws, ia=ia, ib=ib):
                        for kc in range(16):
                            e.matmul(ps[ia][:], lhsT=WA[ws][:, kc * 256:kc * 256 + 128], rhs=H[:, kc, :],
                                     start=(kc == 0), stop=(kc == 15))
                        for kc in range(16):
                            ins = e.matmul(ps[ib][:], lhsT=WA[ws][:, kc * 256 + 128:kc * 256 + 256], rhs=H[:, kc, :],
                                           start=(kc == 0), stop=(kc == 15))
                        return ins
                    P.op("pe", mm, reads=[TWA[ws], TH], writes=[Tps[ia], Tps[ib]])
                    P.op("act", lambda e, hc=hc, ia=ia: e.activation(out=SA[hc % 2], in_=ps[ia][:], func=AF.Silu),
                         reads=[Tps[ia]], writes=[TSA[hc % 2]])
                    P.op("dve", lambda e, hc=hc, ib=ib: e.tensor_tensor(out=A[:, hc, :], in0=SA[hc % 2], in1=ps[ib][:],
                                                                         op=ALU.mult),
                         reads=[TSA[hc % 2], Tps[ib]], writes=[TA])
                for mc in range(16):
                    ws = mc % 2
                    P.op("pool", lambda e, ws=ws, mc=mc: e.dma_start(out=WB[ws], in_=wo_d.ap()[l * 16 + mc]),
                         writes=[TWB[ws]], dma=True)
                    io = 4 + mc % 2

                    def mm2(e, ws=ws, io=io):
                        for kc in range(NHC):
                            ins = e.matmul(ps[io][:], lhsT=WB[ws][:, kc * 128:(kc + 1) * 128], rhs=A[:, kc, :],
                                           start=(kc == 0), stop=(kc == NHC - 1))
                        return ins
                    P.op("pe", mm2, reads=[TWB[ws], TA], writes=[Tps[io]])
                    P.op("act", lambda e, mc=mc, io=io: e.activation(out=O[:, mc, :], in_=ps[io][:], func=AF.Copy),
                         reads=[Tps[io]], writes=[TO])
                rms_stats(O, TO, SQ, TSQ, ps[6], Tps[6], RS, TRS, 16, D)
                for mc in range(16):
                    tm = mc % 2
                    P.op("dve", lambda e, mc=mc, tm=tm: e.scalar_tensor_tensor(
                        out=TMP[tm], in0=O[:, mc, :], scalar=vcol(l, vpost, mc), in1=RS,
                        op0=ALU.mult, op1=ALU.mult), reads=[TO, TRS, Tvecs], writes=[TTMP[tm]])
                    P.op("dve", lambda e, mc=mc, tm=tm: e.scalar_tensor_tensor(
                        out=X[:, mc, :], in0=TMP[tm], scalar=0.5, in1=X[:, mc, :],
                        op0=ALU.mult, op1=ALU.add), reads=[TTMP[tm], TX], writes=[TX])
                P.op("sp", lambda e, tsl=tsl: e.dma_start(out=yT_v[:, :, tsl], in_=X),
                     reads=[TX], writes=[TY[t]], dma=True)
            P.barrier()

        def rms_stats(X, TX, SQ, TSQ, pss, Tpss, RS, TRS, nchunk, dim):
            for kc in range(nchunk):
                s = kc % 2
                P.op("act", lambda e, kc=kc, s=s: e.activation(out=SQ[s], in_=X[:, kc, :], func=AF.Square),
                     reads=[TX], writes=[TSQ[s]])
                P.op("pe", lambda e, kc=kc, s=s: e.matmul(pss[:], lhsT=ones, rhs=SQ[s], start=(kc == 0),
                                                          stop=(kc == nchunk - 1)),
                     reads=[TSQ[s], Tones], writes=[Tpss])
            P.op("act", lambda e: e.activation(out=RS, in_=pss[:], func=AF.Sqrt, bias=EPS, scale=1.0 / dim),
                 reads=[Tpss], writes=[TRS])
            P.op("dve", lambda e: e.reciprocal(out=RS, in_=RS), reads=[TRS], writes=[TRS])


        def dma(eng, out, in_, reads=(), writes=()):
            P.op(eng, lambda e: e.dma_start(out=out, in_=in_), reads, writes, dma=True)

        def act(out, in_, func, reads, writes, **kw):
            P.op("act", lambda e: e.activation(out=out, in_=in_, func=func, **kw), reads, writes)

        def tt(eng, out, a, b, op, reads, writes):
            P.op(eng, lambda e: e.tensor_tensor(out=out, in0=a, in1=b, op=op), reads, writes)

        def ts(eng, out, a, s1, s2, op0, op1, reads, writes):
            if s2 is None:
                P.op(eng, lambda e: e.tensor_scalar(out=out, in0=a, scalar1=s1, scalar2=None, op0=op0), reads, writes)
            else:
                P.op(eng, lambda e: e.tensor_scalar(out=out, in0=a, scalar1=s1, scalar2=s2, op0=op0, op1=op1), reads, writes)

        def stt(eng, out, a, s, b, op0, op1, reads, writes):
            P.op(eng, lambda e: e.scalar_tensor_tensor(out=out, in0=a, scalar=s, in1=b, op0=op0, op1=op1), reads, writes)

        def mmg(out, pairs, reads, writes):
            n = len(pairs)

            def f(e):
                for i, (a, b) in enumerate(pairs):
                    ins = e.matmul(out, lhsT=a, rhs=b, start=(i == 0), stop=(i == n - 1))
                return ins
            P.op("pe", f, reads, writes)

        class Bf:
            def __init__(self, cols, dt, parts=128, r=None, **kw):
                self.ap = ar.alloc(cols, dt, parts)
                if r is not None:
                    self.ap = self.ap.rearrange(r, **kw)
                self.tk = Tk("b")

        def load_xh(t, X, H, RS, SQ, l, first_src=False):
            tsl = slice(t * TT, (t + 1) * TT)
            dma("sp", X.ap, yT_v[:, :, tsl], [TY[t]], [X.tk])
            rms_stats(X.ap, X.tk, [s.ap for s in SQ], [s.tk for s in SQ], ps[6], Tps[6], RS.ap, RS.tk, 16, D)
            for kc in range(16):
                stt("dve", H.ap[:, kc, :], X.ap[:, kc, :], vcol(l, V_MIXPRE, kc), RS.ap, ALU.mult, ALU.mult,
                    [X.tk, RS.tk, Tvecs], [H.tk])

        def gelu_tanh(eng_sets, out, xin, tmp1, tmp2, reads, writes, shape_tk):
            t1, t2 = tmp1, tmp2
            tt("dve", t1.ap, xin, xin, ALU.mult, reads, [t1.tk])
            ts("dve", t1.ap, t1.ap, 0.044715, 1.0, ALU.mult, ALU.add, [t1.tk], [t1.tk])
            tt("dve", t1.ap, t1.ap, xin, ALU.mult, reads + [t1.tk], [t1.tk])
            act(t2.ap, t1.ap, AF.Sigmoid, [t1.tk], [t2.tk], scale=1.5957691216)
            tt("dve", out, t2.ap, xin, ALU.mult, reads + [t2.tk], writes)

        def rope_tables_phase():
            ar.reset(base_off)
            CI = Bf(8, F32)
            IR = Bf(128, F32)
            dma("sp", CI.ap, consts_d.ap()[:, C_INV:C_INV + 8], [], [CI.tk])
            dma("sp", IR.ap, consts_d.ap()[:, C_INVROW:C_INVROW + 128], [], [IR.tk])
            PI_ = Bf(TT, I32)
            PF = Bf(TT, F32)
            ANG = Bf(TT, F32)
            KI = Bf(TT, I32)
            KF, RR, NR = Bf(TT, F32), Bf(TT, F32), Bf(TT, F32)
            OUT = [Bf(TT, F32) for _ in range(2)]
            HALF_PI = float(np.pi / 2)
            C1 = 6.28125
            C2 = float(2 * np.pi - 6.28125)

            def sincos(ang, n, w, osin, ocos):
                kf, rr, nr, ki = KF.ap[0:n, 0:w], RR.ap[0:n, 0:w], NR.ap[0:n, 0:w], KI.ap[0:n, 0:w]
                ts("dve", kf, ang, float(1.0 / (2 * np.pi)), None, ALU.mult, None, [ANG.tk], [KF.tk])
                P.op("dve", lambda e: e.tensor_copy(out=ki, in_=kf), [KF.tk], [KI.tk])
                P.op("dve", lambda e: e.tensor_copy(out=kf, in_=ki), [KI.tk], [KF.tk])
                stt("dve", rr, kf, -C1, ang, ALU.mult, ALU.add, [KF.tk, ANG.tk], [RR.tk])
                stt("dve", rr, kf, -C2, rr, ALU.mult, ALU.add, [KF.tk, RR.tk], [RR.tk])
                ts("dve", rr, rr, -3.14159, 3.14159, ALU.max, ALU.min, [RR.tk], [RR.tk])
                act(osin.ap[0:n, 0:w], rr, AF.Sin, [RR.tk], [osin.tk])
                ts("dve", nr, rr, -1.0, None, ALU.mult, None, [RR.tk], [NR.tk])
                tt("dve", nr, nr, rr, ALU.max, [NR.tk, RR.tk], [NR.tk])
                ts("dve", nr, nr, -1.0, HALF_PI, ALU.mult, ALU.add, [NR.tk], [NR.tk])
                act(ocos.ap[0:n, 0:w], nr, AF.Sin, [NR.tk], [ocos.tk])

            for t in range(NT):
                tsl = slice(t * TT, (t + 1) * TT)
                dma("sp", PI_.ap, posb_d.ap()[:, tsl], [], [PI_.tk])
                P.op("dve", lambda e: e.tensor_copy(out=PF.ap, in_=PI_.ap), [PI_.tk], [PF.tk])
                for which in range(2):
                    n = 128 if which == 0 else 64
                    ts("dve", ANG.ap[0:n], PF.ap[0:n], CI.ap[0:n, which:which + 1], None, ALU.mult, None,
                       [PF.tk, CI.tk], [ANG.tk])
                    sincos(ANG.ap[0:n], n, TT, OUT[0], OUT[1])
                    if which == 0:
                        dma("sp", sink_s[:, tsl], OUT[0].ap, [OUT[0].tk], [])
                        dma("sp", cosk_s[:, tsl], OUT[1].ap, [OUT[1].tk], [])
                    else:
                        ts("dve", OUT[0].ap[0:64], OUT[0].ap[0:64], CI.ap[0:64, 2:3], None, ALU.mult, None,
                           [OUT[0].tk, CI.tk], [OUT[0].tk])
                        dma("sp", s64_s[:, tsl], OUT[0].ap[0:64], [OUT[0].tk], [])
                        dma("sp", c64_s[:, tsl], OUT[1].ap[0:64], [OUT[1].tk], [])
            PT_ = Bf(NBLK, I32)
            PTF = Bf(NBLK, F32)
            dma("sp", PT_.ap, post_d.ap(), [], [PT_.tk])
            P.op("dve", lambda e: e.tensor_copy(out=PTF.ap, in_=PT_.ap), [PT_.tk], [PTF.tk])
            for n in range(NBLK):
                ts("dve", ANG.ap[:, 0:128], IR.ap, PTF.ap[:, n:n + 1], None, ALU.mult, None, [IR.tk, PTF.tk], [ANG.tk])
                sincos(ANG.ap[:, 0:128], 128, 128, OUT[0], OUT[1])
                dma("sp", sintm_s[n * 128:(n + 1) * 128, :], OUT[0].ap[:, 0:128], [OUT[0].tk], [])
                dma("sp", costm_s[n * 128:(n + 1) * 128, :], OUT[1].ap[:, 0:128], [OUT[1].tk], [])
            P.barrier()

        def m1a_phase(l):
            ar.reset(base_off)
            X = Bf(16 * TT, F32, r="p (k t) -> p k t", k=16)
            H = Bf(16 * TT, BF16, r="p (k t) -> p k t", k=16)
            WUQ = Bf(4 * 16 * 256, BF16, r="p (k h c) -> p k h c", k=4, h=16)
            WUKV = Bf(4 * 4096, BF16, r="p (k h c) -> p k h c", k=4, h=16)
            W = [Bf(2048, BF16) for _ in range(2)]
            ZQ = Bf(4 * TT, F32, r="p (k t) -> p k t", k=4)
            ZKV = Bf(4 * TT, F32, r="p (k t) -> p k t", k=4)
            ZQN = Bf(4 * TT, BF16, r="p (k t) -> p k t", k=4)
            ZKVN = Bf(4 * TT, BF16, r="p (k t) -> p k t", k=4)
            SQ = [Bf(TT, F32) for _ in range(2)]
            RS = Bf(TT, F32)
            CK, SK, C64, S64 = Bf(TT, F32), Bf(TT, F32), Bf(TT, F32), Bf(TT, F32)
            R0, R1 = [Bf(TT, F32) for _ in range(2)], [Bf(TT, F32) for _ in range(2)]
            T0, T1, T2, T3 = Bf(TT, F32), Bf(TT, F32), Bf(TT, F32), Bf(TT, F32)
            ST = [Bf(TT, BF16) for _ in range(4)]
            VT = [Bf(2048, BF16) for _ in range(1)]
            dma("pool", WUQ.ap, wuq_d.ap()[l].rearrange("p (k h c) -> p k h c", k=4, h=16), [], [WUQ.tk])
            dma("pool", WUKV.ap, wukv_d.ap()[l].rearrange("p (k h c) -> p k h c", k=4, h=16), [], [WUKV.tk])
            cnt = {"w": 0, "st": 0, "ps": 0}

            def wtile(idx):
                w = W[cnt["w"] % 2]
                cnt["w"] += 1
                dma("pool", w.ap, winfm.ap()[l * 73 + idx], [], [w.tk])
                return w

            def stage():
                s = ST[cnt["st"] % 4]
                cnt["st"] += 1
                return s

            def proj(w, lo, hi, pi, npart=128):
                mmg(ps[pi][0:npart, :], [(w.ap[:, kc * 128 + lo:kc * 128 + hi], H.ap[:, kc, :]) for kc in range(16)],
                    [w.tk, H.tk], [Tps[pi]])

            for t in range(NT):
                tsl = slice(t * TT, (t + 1) * TT)
                load_xh(t, X, H, RS, SQ, l)
                dma("sp", CK.ap, cosk_s[:, tsl], [], [CK.tk])
                dma("sp", SK.ap, sink_s[:, tsl], [], [SK.tk])
                dma("sp", C64.ap[0:64], c64_s[:, tsl], [], [C64.tk])
                dma("sp", S64.ap[0:64], s64_s[:, tsl], [], [S64.tk])
                for i in range(8):
                    w = wtile(i)
                    pi = i % 2
                    proj(w, 0, 128, pi)
                    dst = ZQ if i < 4 else ZKV
                    act(dst.ap[:, i % 4, :], ps[pi][:], AF.Copy, [Tps[pi]], [dst.tk])
                for (Z, ZN, vb) in ((ZQ, ZQN, V_QN), (ZKV, ZKVN, V_KVN)):
                    rms_stats(Z.ap, Z.tk, [s.ap for s in SQ], [s.tk for s in SQ], ps[6], Tps[6], RS.ap, RS.tk, 4, 512)
                    for kc in range(4):
                        stt("dve", ZN.ap[:, kc, :], Z.ap[:, kc, :], vcol(l, vb, kc), RS.ap, ALU.mult, ALU.mult,
                            [Z.tk, RS.tk, Tvecs], [ZN.tk])

                def rope64(pa, pb, scale, out_ap, out_tk):
                    act(R0[0].ap[0:64], ps[pa][0:64, :], AF.Copy, [Tps[pa]], [R0[0].tk], scale=scale)
                    act(R1[0].ap[0:64], ps[pb][0:64, :], AF.Copy, [Tps[pb]], [R1[0].tk], scale=scale)
                    tt("dve", T0.ap[0:64], R0[0].ap[0:64], C64.ap[0:64], ALU.mult, [R0[0].tk, C64.tk], [T0.tk])
                    tt("dve", T1.ap[0:64], R1[0].ap[0:64], S64.ap[0:64], ALU.mult, [R1[0].tk, S64.tk], [T1.tk])
                    tt("dve", out_ap, T0.ap[0:64], T1.ap[0:64], ALU.add, [T0.tk, T1.tk], [out_tk])

                w = wtile(8)
                proj(w, 0, 64, 0, 64)
                proj(w, 64, 128, 1, 64)
                s = stage()
                rope64(0, 1, 1.0, s.ap[0:64], s.tk)
                dma("sp", kr_s[:, tsl], s.ap[0:64], [s.tk], [])
                for kind, base, dst_s, scale in (("rq", 9, rq_s, 1.0), ("rk", 25, rkT_s, 1.0 / 16.0)):
                    for h in range(8):
                        pr = h % 2
                        for c in range(2):
                            w = wtile(base + h * 2 + c)
                            proj(w, 0, 128, 2 * pr + c)
                        act(R0[pr].ap, ps[2 * pr][:], AF.Copy, [Tps[2 * pr]], [R0[pr].tk], scale=scale)
                        act(R1[pr].ap, ps[2 * pr + 1][:], AF.Copy, [Tps[2 * pr + 1]], [R1[pr].tk], scale=scale)
                        s0, s1 = stage(), stage()
                        tt("dve", T0.ap, R0[pr].ap, CK.ap, ALU.mult, [R0[pr].tk, CK.tk], [T0.tk])
                        tt("dve", T1.ap, R1[pr].ap, SK.ap, ALU.mult, [R1[pr].tk, SK.tk], [T1.tk])
                        tt("dve", s0.ap, T0.ap, T1.ap, ALU.subtract, [T0.tk, T1.tk], [s0.tk])
                        tt("pool", T2.ap, R1[pr].ap, CK.ap, ALU.mult, [R1[pr].tk, CK.tk], [T2.tk])
                        tt("pool", T3.ap, R0[pr].ap, SK.ap, ALU.mult, [R0[pr].tk, SK.tk], [T3.tk])
                        tt("pool", s1.ap, T2.ap, T3.ap, ALU.add, [T2.tk, T3.tk], [s1.tk])
                        dma("sp", dst_s[2 * h][:, tsl], s0.ap, [s0.tk], [])
                        dma("sp", dst_s[2 * h + 1][:, tsl], s1.ap, [s1.tk], [])
                for i in range(16):
                    w = wtile(41 + i)
                    pi = i % 2
                    proj(w, 0, 128, pi)
                    s = stage()
                    act(s.ap, ps[pi][:], AF.Silu, [Tps[pi]], [s.tk])
                    dma("sp", rg_s[i][:, tsl], s.ap, [s.tk], [])
                for i in range(16):
                    w = wtile(57 + i)
                    pi = i % 2
                    proj(w, 0, 128, pi)
                    act(R0[pi].ap, ps[pi][:], AF.Copy, [Tps[pi]], [R0[pi].tk])
                    s = stage()
                    gelu_tanh(None, s.ap, R0[pi].ap, T0, T1, [R0[pi].tk], [s.tk], None)
                    dma("sp", gu_s[i][:, tsl], s.ap, [s.tk], [])
                QSC = 192.0 ** -0.5
                for h in range(16):
                    pi = 2 + h % 2
                    mmg(ps[pi][:], [(WUQ.ap[:, kc, h, 0:128], ZQN.ap[:, kc, :]) for kc in range(4)],
                        [WUQ.tk, ZQN.tk], [Tps[pi]])
                    s = stage()
                    act(s.ap, ps[pi][:], AF.Copy, [Tps[pi]], [s.tk], scale=QSC)
                    dma("sp", qn_s[h][:, tsl], s.ap, [s.tk], [])
                    mmg(ps[0][0:64, :], [(WUQ.ap[:, kc, h, 128:192], ZQN.ap[:, kc, :]) for kc in range(4)],
                        [WUQ.tk, ZQN.tk], [Tps[0]])
                    mmg(ps[1][0:64, :], [(WUQ.ap[:, kc, h, 192:256], ZQN.ap[:, kc, :]) for kc in range(4)],
                        [WUQ.tk, ZQN.tk], [Tps[1]])
                    s = stage()
                    rope64(0, 1, QSC, s.ap[0:64], s.tk)
                    dma("sp", qr_s[h][:, tsl], s.ap[0:64], [s.tk], [])
                    pi = 4 + h % 2
                    mmg(ps[pi][:], [(WUKV.ap[:, kc, h, 0:128], ZKVN.ap[:, kc, :]) for kc in range(4)],
                        [WUKV.tk, ZKVN.tk], [Tps[pi]])
                    s = stage()
                    act(s.ap, ps[pi][:], AF.Copy, [Tps[pi]], [s.tk])
                    dma("sp", kT_s[h][:, tsl], s.ap, [s.tk], [])
                for blk in range(4):
                    vt = VT[0]
                    for j in range(4):
                        pi = 2 + j % 2
                        mmg(ps[pi][:], [(ZKVN.ap[:, kc, blk * 128:(blk + 1) * 128], WUKV.ap[:, kc, 4 * j:4 * j + 4, 128:256])
                                        for kc in range(4)], [WUKV.tk, ZKVN.tk], [Tps[pi]])
                        act(vt.ap[:, j * 512:(j + 1) * 512], ps[pi][:], AF.Copy, [Tps[pi]], [vt.tk])
                    r0 = t * TT + blk * 128
                    dma("sp", v_s[r0:r0 + 128, :], vt.ap, [vt.tk], [])
            P.barrier()

        def m1b_phase(l):
            ar.reset(base_off)
            X = Bf(16 * TT, F32, r="p (k t) -> p k t", k=16)
            H = Bf(16 * TT, BF16, r="p (k t) -> p k t", k=16)
            W = [Bf(16 * 512, BF16) for _ in range(2)]
            SQ = [Bf(TT, F32) for _ in range(2)]
            RS = Bf(TT, F32)
            GV = Bf(4 * 2048, F32, r="p (b c) -> p b c", b=4)
            LNG, LNB = Bf(2048, F32), Bf(2048, F32)
            CT, STm = Bf(4 * 128, F32, r="p (b f) -> p b f", b=4), Bf(4 * 128, F32, r="p (b f) -> p b f", b=4)
            KS = Bf(512, F32)
            T0, T1, T2 = Bf(512, F32), Bf(512, F32), Bf(512, F32)
            OS = [Bf(512, BF16) for _ in range(3)]
            GO = [Bf(2048, BF16) for _ in range(2)]
            STAT = Bf(8, F32)
            LNscr = Bf(2048, F32)
            dma("sp", LNG.ap, lng_d.ap()[l], [], [LNG.tk])
            dma("sp", LNB.ap, lnb_d.ap()[l], [], [LNB.tk])
            c = {"w": 0, "o": 0}
            for t in range(NT):
                load_xh(t, X, H, RS, SQ, l)
                r_t = t * TT
                dma("sp", CT.ap, costm_s[r_t:r_t + TT, :].rearrange("(b p) f -> p b f", p=128), [], [CT.tk])
                dma("sp", STm.ap, sintm_s[r_t:r_t + TT, :].rearrange("(b p) f -> p b f", p=128), [], [STm.tk])
                for wi_ in range(12):
                    w = W[c["w"] % 2]
                    c["w"] += 1
                    dma("pool", w.ap, wintm.ap()[l * 12 + wi_], [], [w.tk])
                    kind, j = wi_ // 4, wi_ % 4
                    for blk in range(4):
                        pi = blk % 2
                        mmg(ps[pi][:], [(H.ap[:, kc, blk * 128:(blk + 1) * 128], w.ap[:, kc * 512:(kc + 1) * 512])
                                        for kc in range(16)], [w.tk, H.tk], [Tps[pi]])
                        r0 = r_t + blk * 128
                        if kind == 1:
                            o = OS[c["o"] % 3]
                            c["o"] += 1
                            act(o.ap, ps[pi][:], AF.Copy, [Tps[pi]], [o.tk])
                            dma("sp", rvtm_s[r0:r0 + 128, j * 512:(j + 1) * 512], o.ap, [o.tk], [])
                        elif kind == 0:
                            act(KS.ap, ps[pi][:], AF.Copy, [Tps[pi]], [KS.tk], scale=1.0 / 16.0)
                            o = OS[c["o"] % 3]
                            c["o"] += 1
                            K4 = KS.ap.rearrange("p (h x f) -> p h x f", h=2, x=2)
                            O4 = o.ap.rearrange("p (h x f) -> p h x f", h=2, x=2)
                            for hh in range(2):
                                x1, x2 = K4[:, hh, 0, :], K4[:, hh, 1, :]
                                cb, sb = CT.ap[:, blk, :], STm.ap[:, blk, :]
                                tt("dve", T0.ap[:, 0:128], x1, cb, ALU.mult, [KS.tk, CT.tk], [T0.tk])
                                tt("dve", T1.ap[:, 0:128], x2, sb, ALU.mult, [KS.tk, STm.tk], [T1.tk])
                                tt("dve", O4[:, hh, 0, :], T0.ap[:, 0:128], T1.ap[:, 0:128], ALU.subtract, [T0.tk, T1.tk], [o.tk])
                                tt("pool", T2.ap[:, 0:128], x2, cb, ALU.mult, [KS.tk, CT.tk], [T2.tk])
                                tt("pool", T2.ap[:, 128:256], x1, sb, ALU.mult, [KS.tk, STm.tk], [T2.tk])
                                tt("pool", O4[:, hh, 1, :], T2.ap[:, 0:128], T2.ap[:, 128:256], ALU.add, [T2.tk], [o.tk])
                            dma("sp", rktm_s[r0:r0 + 128, j * 512:(j + 1) * 512], o.ap, [o.tk], [])
                        else:
                            act(KS.ap, ps[pi][:], AF.Copy, [Tps[pi]], [KS.tk])
                            gelu_tanh(None, GV.ap[:, blk, j * 512:(j + 1) * 512], KS.ap, T0, T1, [KS.tk], [GV.tk], None)
                for blk in range(4):
                    g = GV.ap[:, blk, :]
                    go = GO[blk % 2]
                    P.op("dve", lambda e, g=g: e.reduce_sum(out=STAT.ap[:, 0:1], in_=g, axis=mybir.AxisListType.X),
                         [GV.tk], [STAT.tk])
                    tt("dve", LNscr.ap, g, g, ALU.mult, [GV.tk], [LNscr.tk])
                    P.op("dve", lambda e: e.reduce_sum(out=STAT.ap[:, 1:2], in_=LNscr.ap, axis=mybir.AxisListType.X),
                         [LNscr.tk], [STAT.tk])
                    ts("dve", STAT.ap[:, 2:3], STAT.ap[:, 0:1], 1.0 / 2048, None, ALU.mult, None, [STAT.tk], [STAT.tk])
                    ts("dve", STAT.ap[:, 3:4], STAT.ap[:, 1:2], 1.0 / 2048, None, ALU.mult, None, [STAT.tk], [STAT.tk])
                    tt("dve", STAT.ap[:, 4:5], STAT.ap[:, 2:3], STAT.ap[:, 2:3], ALU.mult, [STAT.tk], [STAT.tk])
                    tt("dve", STAT.ap[:, 5:6], STAT.ap[:, 3:4], STAT.ap[:, 4:5], ALU.subtract, [STAT.tk], [STAT.tk])
                    act(STAT.ap[:, 6:7], STAT.ap[:, 5:6], AF.Sqrt, [STAT.tk], [STAT.tk], bias=EPS, scale=1.0)
                    P.op("dve", lambda e: e.reciprocal(out=STAT.ap[:, 6:7], in_=STAT.ap[:, 6:7]), [STAT.tk], [STAT.tk])
                    ts("dve", g, g, STAT.ap[:, 2:3], STAT.ap[:, 6:7], ALU.subtract, ALU.mult, [GV.tk, STAT.tk], [GV.tk])
                    tt("dve", g, g, LNG.ap, ALU.mult, [GV.tk, LNG.tk], [GV.tk])
                    tt("dve", go.ap, g, LNB.ap, ALU.add, [GV.tk, LNB.tk], [go.tk])
                    r0 = r_t + blk * 128
                    dma("sp", gvtm_s[r0:r0 + 128, :], go.ap, [go.tk], [])
            P.barrier()

        LOGG = [float(np.log1p(-(2.0 ** (-5.0 - h)))) for h in range(8)]

        def m2_phase(l):
            ar.reset(base_off)
            MASKA = Bf(2048, F32, r="p (j q) -> p j q", j=4)
            DTc = Bf(1024, F32, r="p (h i) -> p h i", h=8)
            QDEC = Bf(1024, F32, r="p (h i) -> p h i", h=8)
            KDEC = Bf(8, F32)
            MASKG = Bf(128, F32)
            cd = consts_d.ap()
            dma("sp", MASKA.ap, cd[:, C_MASKA:C_MASKA + 2048].rearrange("p (j q) -> p j q", j=4), [], [MASKA.tk])
            dma("sp", DTc.ap, cd[:, C_DT:C_DT + 1024].rearrange("p (h i) -> p h i", h=8), [], [DTc.tk])
            dma("sp", QDEC.ap, cd[:, C_QDEC:C_QDEC + 1024].rearrange("p (h i) -> p h i", h=8), [], [QDEC.tk])
            dma("sp", KDEC.ap, cd[:, C_KDEC:C_KDEC + 8], [], [KDEC.tk])
            dma("sp", MASKG.ap, cd[:, C_MASKG:C_MASKG + 128], [], [MASKG.tk])
            ONESB = Bf(128, BF16)
            P.op("pool", lambda e: e.memset(ONESB.ap, 1.0), [], [ONESB.tk])
            KR = Bf(S, BF16)
            dma("sp", KR.ap[0:64], kr_s, [], [KR.tk])
            QN = [Bf(TT, BF16) for _ in range(2)]
            QR = [Bf(TT, BF16) for _ in range(2)]
            KT = [Bf(TT, BF16) for _ in range(2)]
            VB = [Bf(512, BF16, r="p (j e) -> p j e", j=4) for _ in range(2)]
            PT = [Bf(TT, BF16) for _ in range(3)]
            RDEN = Bf(TT, F32)
            YA = [Bf(TT, BF16) for _ in range(2)]
            SF = Bf(8 * 512, F32, r="p (h c e) -> p h c e", h=8, c=2)
            SB = Bf(8 * 512, BF16, r="p (h c e) -> p h c e", h=8, c=2)
            SFtk = [Tk("sf") for _ in range(8)]
            SBtk = [Tk("sb") for _ in range(8)]
            P.op("pool", lambda e: e.memset(SF.ap, 0.0), [], SFtk)
            P.op("pool", lambda e: e.memset(SB.ap, 0.0), [], SBtk)
            RQ = [Bf(2 * TT, BF16, r="p (c t) -> p c t", c=2) for _ in range(2)]
            RKT = [Bf(2 * TT, BF16, r="p (c t) -> p c t", c=2) for _ in range(2)]
            RG = [Bf(2 * TT, BF16, r="p (c t) -> p c t", c=2) for _ in range(2)]
            RKM = [Bf(4 * 256, BF16, r="p (b d) -> p b d", b=4) for _ in range(2)]
            RVM = [Bf(4 * 256, BF16, r="p (b d) -> p b d", b=4) for _ in range(2)]
            PTR = [Bf(128, BF16) for _ in range(2)]
            QD = [Bf(256, BF16, r="p (c i) -> p c i", c=2) for _ in range(2)]
            KD = [Bf(256, BF16) for _ in range(2)]
            RO = Bf(2 * TT, F32, r="p (c t) -> p c t", c=2)
            SQ2 = [Bf(TT, F32) for _ in range(2)]
            MEAN, MSQ, VAR, RSTD = Bf(TT, F32), Bf(TT, F32), Bf(TT, F32), Bf(TT, F32)
            YB = [Bf(TT, BF16) for _ in range(2)]
            WSTf = Bf(512, F32, r="p (g i) -> p g i", g=4)
            WST = Bf(512, BF16, r="p (g i) -> p g i", g=4)
            GMB = Bf(2048, F32, r="p (g t) -> p g t", g=4)
            GVt = Bf(4 * 2048, BF16, r="p (b c) -> p b c", b=4)
            GU = Bf(16 * TT, BF16, r="p (k t) -> p k t", k=16)
            TMPF = [Bf(TT, F32) for _ in range(2)]
            YC = [Bf(TT, BF16) for _ in range(2)]
            dma("sp", WSTf.ap, wst_d.ap()[l].rearrange("p (g i) -> p g i", g=4), [], [WSTf.tk])
            dma("sp", GMB.ap, gmb_d.ap()[l].rearrange("p (g t) -> p g t", g=4), [], [GMB.tk])
            for g in range(4):
                tt("dve", WST.ap[:, g, :], WSTf.ap[:, g, :], MASKG.ap, ALU.mult, [WSTf.tk, MASKG.tk], [WST.tk])
            c = {"kv": 0, "pt": 0}
            for t in range(NT):
                tsl = slice(t * TT, (t + 1) * TT)
                nk = 4 * (t + 1)
                for h in range(16):
                    b = h % 2
                    dma("sp", QN[b].ap, qn_s[h][:, tsl], [], [QN[b].tk])
                    dma("sp", QR[b].ap[0:64], qr_s[h][:, tsl], [], [QR[b].tk])
                    for kb in range(t + 1):
                        kbuf, vbuf = KT[c["kv"] % 2], VB[c["kv"] % 2]
                        c["kv"] += 1
                        dma("sp", kbuf.ap, kT_s[h][:, kb * 512:(kb + 1) * 512], [], [kbuf.tk])
                        dma("sp", vbuf.ap, v_s[kb * 512:(kb + 1) * 512, h * 128:(h + 1) * 128].rearrange(
                            "(j p) e -> p j e", p=128), [], [vbuf.tk])
                        for j in range(4):
                            kt = kb * 4 + j
                            pi = kt % 2
                            mmg(ps[pi][:], [(kbuf.ap[:, j * 128:(j + 1) * 128], QN[b].ap),
                                            (KR.ap[0:64, kt * 128:(kt + 1) * 128], QR[b].ap[0:64])],
                                [kbuf.tk, QN[b].tk, KR.tk, QR[b].tk], [Tps[pi]])
                            pt = PT[c["pt"] % 3]
                            c["pt"] += 1
                            act(pt.ap, ps[pi][:], AF.Exp, [Tps[pi]], [pt.tk])
                            if kb == t:
                                tt("dve", pt.ap, pt.ap, MASKA.ap[:, j, :], ALU.mult, [pt.tk, MASKA.tk], [pt.tk])
                            P.op("pe", lambda e, b=b, pt=pt, kt=kt: e.matmul(ps[2 + b][:], lhsT=ONESB.ap, rhs=pt.ap,
                                                                             start=(kt == 0), stop=(kt == nk - 1)),
                                 [ONESB.tk, pt.tk], [Tps[2 + b]])
                            P.op("pe", lambda e, b=b, pt=pt, kt=kt, vbuf=vbuf, j=j: e.matmul(
                                ps[4 + b][:], lhsT=vbuf.ap[:, j, :], rhs=pt.ap, start=(kt == 0), stop=(kt == nk - 1)),
                                 [vbuf.tk, pt.tk], [Tps[4 + b]])
                    P.op("dve", lambda e, b=b: e.reciprocal(out=RDEN.ap, in_=ps[2 + b][:]), [Tps[2 + b]], [RDEN.tk])
                    ya = YA[b]
                    tt("dve", ya.ap, ps[4 + b][:], RDEN.ap, ALU.mult, [Tps[4 + b], RDEN.tk], [ya.tk])
                    dma("sp", ya_s[h][:, tsl], ya.ap, [ya.tk], [])
                r_t = t * TT
                for h in range(8):
                    b = h % 2
                    cdh = float(np.exp(128.0 * LOGG[h]))
                    dma("sp", RQ[b].ap, rq_s[2 * h:2 * h + 2, :, tsl].rearrange("c p t -> p c t"), [], [RQ[b].tk])
                    dma("sp", RKT[b].ap, rkT_s[2 * h:2 * h + 2, :, tsl].rearrange("c p t -> p c t"), [], [RKT[b].tk])
                    dma("sp", RG[b].ap, rg_s[2 * h:2 * h + 2, :, tsl].rearrange("c p t -> p c t"), [], [RG[b].tk])
                    dma("sp", RKM[b].ap, rktm_s[r_t:r_t + TT, h * 256:(h + 1) * 256].rearrange("(b p) d -> p b d", p=128),
                        [], [RKM[b].tk])
                    dma("sp", RVM[b].ap, rvtm_s[r_t:r_t + TT, h * 256:(h + 1) * 256].rearrange("(b p) d -> p b d", p=128),
                        [], [RVM[b].tk])
                    for blk in range(4):
                        bs = slice(blk * 128, (blk + 1) * 128)
                        mmg(ps[0][:, 0:128], [(RKT[b].ap[:, cc, bs], RQ[b].ap[:, cc, bs]) for cc in range(2)],
                            [RKT[b].tk, RQ[b].tk], [Tps[0]])
                        ptr = PTR[blk % 2]
                        tt("dve", ptr.ap, ps[0][:, 0:128], DTc.ap[:, h, :], ALU.mult, [Tps[0], DTc.tk], [ptr.tk])
                        qd = QD[blk % 2]
                        for cc in range(2):
                            tt("pool", qd.ap[:, cc, :], RQ[b].ap[:, cc, bs], QDEC.ap[:, h, :], ALU.mult,
                               [RQ[b].tk, QDEC.tk], [qd.tk])
                        for ec in range(2):
                            es_ = slice(ec * 128, (ec + 1) * 128)
                            pairs = [(RVM[b].ap[:, blk, es_], ptr.ap)] + [(SB.ap[:, h, cc, es_], qd.ap[:, cc, :])
                                                                          for cc in range(2)]
                            mmg(ps[1 + ec][:, 0:128], pairs, [RVM[b].tk, ptr.tk, SBtk[h], qd.tk], [Tps[1 + ec]])
                            act(RO.ap[:, ec, bs], ps[1 + ec][:, 0:128], AF.Copy, [Tps[1 + ec]], [RO.tk])
                        kd = KD[blk % 2]
                        ts("pool", kd.ap, RKM[b].ap[:, blk, :], KDEC.ap[:, h:h + 1], None, ALU.mult, None,
                           [RKM[b].tk, KDEC.tk], [kd.tk])
                        for cc in range(2):
                            mmg(ps[3 + cc][:, 0:256], [(kd.ap[:, cc * 128:(cc + 1) * 128], RVM[b].ap[:, blk, :])],
                                [kd.tk, RVM[b].tk], [Tps[3 + cc]])
                            stt("dve", SF.ap[:, h, cc, :], SF.ap[:, h, cc, :], cdh, ps[3 + cc][:, 0:256], ALU.mult, ALU.add,
                                [SFtk[h], Tps[3 + cc]], [SFtk[h]])
                            act(SB.ap[:, h, cc, :], SF.ap[:, h, cc, :], AF.Copy, [SFtk[h]], [SBtk[h]])
                    mmg(ps[5][:], [(ones, RO.ap[:, cc, :]) for cc in range(2)], [Tones, RO.tk], [Tps[5]])
                    for cc in range(2):
                        act(SQ2[cc].ap, RO.ap[:, cc, :], AF.Square, [RO.tk], [SQ2[cc].tk])
                    mmg(ps[6][:], [(ones, SQ2[cc].ap) for cc in range(2)], [Tones, SQ2[0].tk, SQ2[1].tk], [Tps[6]])
                    ts("dve", MEAN.ap, ps[5][:], 1.0 / 256, None, ALU.mult, None, [Tps[5]], [MEAN.tk])
                    ts("dve", MSQ.ap, ps[6][:], 1.0 / 256, None, ALU.mult, None, [Tps[6]], [MSQ.tk])
                    tt("dve", VAR.ap, MEAN.ap, MEAN.ap, ALU.mult, [MEAN.tk], [VAR.tk])
                    tt("dve", VAR.ap, MSQ.ap, VAR.ap, ALU.subtract, [MSQ.tk, VAR.tk], [VAR.tk])
                    act(RSTD.ap, VAR.ap, AF.Sqrt, [VAR.tk], [RSTD.tk], bias=1e-5, scale=1.0)
                    P.op("dve", lambda e: e.reciprocal(out=RSTD.ap, in_=RSTD.ap), [RSTD.tk], [RSTD.tk])
                    for cc in range(2):
                        tt("dve", SQ2[cc].ap, RO.ap[:, cc, :], MEAN.ap, ALU.subtract, [RO.tk, MEAN.tk], [SQ2[cc].tk])
                        tt("dve", SQ2[cc].ap, SQ2[cc].ap, RSTD.ap, ALU.mult, [SQ2[cc].tk, RSTD.tk], [SQ2[cc].tk])
                        yb = YB[cc]
                        tt("pool", yb.ap, SQ2[cc].ap, RG[b].ap[:, cc, :], ALU.mult, [SQ2[cc].tk, RG[b].tk], [yb.tk])
                        dma("sp", yb_s[2 * h + cc][:, tsl], yb.ap, [yb.tk], [])
                dma("sp", GVt.ap, gvtm_s[r_t:r_t + TT, :].rearrange("(b p) c -> p b c", p=128), [], [GVt.tk])
                dma("sp", GU.ap, gu_s[:, :, tsl].rearrange("k p t -> p k t"), [], [GU.tk])
                for cc in range(16):
                    g = cc // 4
                    pi = cc % 2

                    def f(e, cc=cc, g=g, pi=pi):
                        for blk in range(4):
                            ins = e.matmul(ps[pi][:, blk * 128:(blk + 1) * 128], lhsT=GVt.ap[:, blk, cc * 128:(cc + 1) * 128],
                                           rhs=WST.ap[:, g, :], start=True, stop=True)
                        return ins
                    P.op("pe", f, [GVt.tk, WST.tk], [Tps[pi]])
                    tf = TMPF[cc % 2]
                    tt("dve", tf.ap, ps[pi][:], GMB.ap[:, g, :], ALU.add, [Tps[pi], GMB.tk], [tf.tk])
                    yc = YC[cc % 2]
                    tt("dve", yc.ap, tf.ap, GU.ap[:, cc, :], ALU.mult, [tf.tk, GU.tk], [yc.tk])
                    dma("sp", yc_s[cc][:, tsl], yc.ap, [yc.tk], [])
            P.barrier()

        def m3_phase(l):
            ar.reset(base_off)
            X = Bf(16 * TT, F32, r="p (k t) -> p k t", k=16)
            H = Bf(16 * TT, BF16, r="p (k t) -> p k t", k=16)
            SQ = [Bf(TT, F32) for _ in range(2)]
            RS = Bf(TT, F32)
            YB3 = [Bf(16 * TT, BF16, r="p (k t) -> p k t", k=16) for _ in range(3)]
            M = Bf(16 * TT, F32, r="p (k t) -> p k t", k=16)
            MB = Bf(16 * TT, BF16, r="p (k t) -> p k t", k=16)
            W = [Bf(2048, BF16) for _ in range(4)]
            G = [Bf(TT, F32) for _ in range(2)]
            TM = [Bf(TT, F32) for _ in range(2)]
            srcs = (ya_s, yb_s, yc_s)
            c = {"w": 0, "n": 0}

            def wtile(d, idx):
                w = W[c["w"] % 4]
                c["w"] += 1
                dma("pool", w.ap, d.ap()[idx], [], [w.tk])
                return w

            for t in range(NT):
                tsl = slice(t * TT, (t + 1) * TT)
                load_xh(t, X, H, RS, SQ, l)
                for b in range(3):
                    dma("sp", YB3[b].ap, srcs[b][:, :, tsl].rearrange("k p t -> p k t"), [], [YB3[b].tk])
                for mc in range(16):
                    for b in range(3):
                        n = c["n"]
                        c["n"] += 1
                        wg = wtile(wg_d, l * 48 + b * 16 + mc)
                        wb = wtile(wbr_d, l * 48 + b * 16 + mc)
                        pg, pb = (2 * n) % 4, (2 * n + 1) % 4
                        mmg(ps[pg][:], [(wg.ap[:, kc * 128:(kc + 1) * 128], H.ap[:, kc, :]) for kc in range(16)],
                            [wg.tk, H.tk], [Tps[pg]])
                        mmg(ps[pb][:], [(wb.ap[:, kc * 128:(kc + 1) * 128], YB3[b].ap[:, kc, :]) for kc in range(16)],
                            [wb.tk, YB3[b].tk], [Tps[pb]])
                        g = G[n % 2]
                        act(g.ap, ps[pg][:], AF.Sigmoid, [Tps[pg], Tvecs], [g.tk], bias=vcol(l, V_BG, b * 16 + mc), scale=1.0)
                        if b == 0:
                            tt("dve", M.ap[:, mc, :], g.ap, ps[pb][:], ALU.mult, [g.tk, Tps[pb]], [M.tk])
                        else:
                            tm = TM[n % 2]
                            tt("dve", tm.ap, g.ap, ps[pb][:], ALU.mult, [g.tk, Tps[pb]], [tm.tk])
                            tt("dve", M.ap[:, mc, :], M.ap[:, mc, :], tm.ap, ALU.add, [M.tk, tm.tk], [M.tk])
                    act(MB.ap[:, mc, :], M.ap[:, mc, :], AF.Copy, [M.tk], [MB.tk])
                for mc in range(16):
                    w = wtile(wo2_d, l * 16 + mc)
                    pi = 4 + mc % 2
                    mmg(ps[pi][:], [(w.ap[:, kc * 128:(kc + 1) * 128], MB.ap[:, kc, :]) for kc in range(16)],
                        [w.tk, MB.tk], [Tps[pi]])
                    act(M.ap[:, mc, :], ps[pi][:], AF.Copy, [Tps[pi]], [M.tk])
                rms_stats(M.ap, M.tk, [s.ap for s in SQ], [s.tk for s in SQ], ps[6], Tps[6], RS.ap, RS.tk, 16, D)
                for mc in range(16):
                    tm = TM[mc % 2]
                    stt("dve", tm.ap, M.ap[:, mc, :], vcol(l, V_MIXPOST, mc), RS.ap, ALU.mult, ALU.mult,
                        [M.tk, RS.tk, Tvecs], [tm.tk])
                    tt("dve", X.ap[:, mc, :], X.ap[:, mc, :], tm.ap, ALU.add, [X.tk, tm.tk], [X.tk])
                dma("sp", yT_v[:, :, tsl], X.ap, [X.tk], [TY[t]])
            P.barrier()

        if "mix" in stages:
            rope_tables_phase()

        for l in range(L):
            if "f1" in stages:
                ffn_phase(l, f1wi, f1wo, V_F1PRE, V_F1POST)
            if "mix" in stages:
                m1a_phase(l)
                m1b_phase(l)
                m2_phase(l)
                m3_phase(l)
            if "f2" in stages:
                ffn_phase(l, f2wi, f2wo, V_F2PRE, V_F2POST)
        P.barrier(["sp"])

        sems = {k: es.enter_context(nc.semaphore(k)) for k in P.cnt.keys()}
        with nc.Block() as block:
            @block.tensor
            def _(e):
                P.replay("pe", e, sems)

            @block.scalar
            def _(e):
                P.replay("act", e, sems)

            @block.vector
            def _(e):
                P.replay("dve", e, sems)

            @block.gpsimd
            def _(e):
                P.replay("pool", e, sems)

            @block.sync
            def _(e):
                P.replay("sp", e, sems)
    return nc


def prep_shared(inp, L):
    out = {}
    vec = np.zeros((128, NV_L * L), np.float32)

    def put(l, base, v):
        n = v.shape[0] // 128
        vec[:, l * NV_L + base:l * NV_L + base + n] = v.reshape(n, 128).T

    for l in range(L):
        put(l, V_F1PRE, inp["ffn1_pre_g"][l])
        put(l, V_F1POST, inp["ffn1_post_g"][l])
        put(l, V_MIXPRE, inp["mix_pre_g"][l])
        put(l, V_QN, inp["q_norm_g"][l])
        put(l, V_KVN, inp["kv_norm_g"][l])
        put(l, V_BG, inp["b_gate"][l])
        put(l, V_MIXPOST, inp["mix_post_g"][l])
        put(l, V_F2PRE, inp["ffn2_pre_g"][l])
        put(l, V_F2POST, inp["ffn2_post_g"][l])
    out["vecs"] = vec

    def wi_tiles(w):
        r = np.empty((L, NHC, 128, 16, 2, 128), np.float32)
        for l in range(L):
            r[l] = w[l].reshape(16, 128, 2, NHC, 128).transpose(3, 1, 0, 2, 4)
        return r.reshape(L * NHC, 128, 4096)

    def wo_tiles(w):
        r = np.empty((L, 16, 128, NHC, 128), np.float32)
        for l in range(L):
            r[l] = w[l].reshape(NHC, 128, 16, 128).transpose(2, 1, 0, 3)
        return r.reshape(L * 16, 128, DFF)

    out["f1wi"] = wi_tiles(inp["ffn1_wi"])
    out["f1wo"] = wo_tiles(inp["ffn1_wo"])
    out["f2wi"] = wi_tiles(inp["ffn2_wi"])
    out["f2wo"] = wo_tiles(inp["ffn2_wo"])
    return out


def tiles128(W):
    K, M = W.shape
    return np.ascontiguousarray(W.reshape(K // 128, 128, M // 128, 128).transpose(2, 1, 0, 3)).reshape(M // 128, 128, K)


def make_consts():
    c = np.zeros((128, NC_TOT), np.float32)
    kp = np.arange(128)[:, None]
    qf = np.arange(512)[None, :]
    for j in range(4):
        c[:, C_MASKA + j * 512:C_MASKA + (j + 1) * 512] = (((j * 128 + kp) // 64) <= (qf // 64)).astype(np.float32)
    i = np.arange(128)[None, :]
    jj = np.arange(128)[:, None]
    for h in range(8):
        lg = np.log1p(-(2.0 ** (-5.0 - h)))
        c[:, C_DT + h * 128:C_DT + (h + 1) * 128] = np.exp(np.abs(i - jj) * lg) * ((i // 64) >= (jj // 64))
        c[:, C_QDEC + h * 128:C_QDEC + (h + 1) * 128] = np.exp((i + 1.0) * lg)
        c[:, C_KDEC + h] = np.exp((127.0 - np.arange(128)) * lg)
    c[:, C_MASKG:C_MASKG + 128] = ((i // 64) >= (jj // 64)).astype(np.float32)
    inv_k = (np.float32(10000.0) ** (-np.arange(0, 256, 2, dtype=np.float32) / np.float32(256))).astype(np.float32)
    inv_r = (np.float32(10000.0) ** (-np.arange(0, 64, 2, dtype=np.float32) / np.float32(64))).astype(np.float32)
    c[:, C_INV] = inv_k
    c[:, C_INV + 1] = np.tile(inv_r, 4)
    c[:, C_INV + 2] = np.where((np.arange(128) % 64) < 32, -1.0, 1.0)
    c[:, C_INVROW:C_INVROW + 128] = inv_k[None, :]
    return c


def prep_mixer(inp, L):
    out = {"consts": make_consts()}
    SPL = np.cumsum([0, 512, 512, 64, 2048, 2048, 2048, 2048, 2048, 2048])
    winfm = np.empty((L, 73, 128, 2048), np.float32)
    wintm = np.empty((L, 12, 128, 16 * 512), np.float32)
    wuq = np.empty((L, 128, 4, 16, 256), np.float32)
    wukv = np.empty((L, 128, 4 * 4096), np.float32)
    swap = (np.arange(64) + 32) % 64
    for l in range(L):
        w = inp["w_in"][l]
        winfm[l, 0:4] = tiles128(w[:, SPL[0]:SPL[1]])
        winfm[l, 4:8] = tiles128(w[:, SPL[1]:SPL[2]])
        kr = w[:, SPL[2]:SPL[3]]
        winfm[l, 8:9] = tiles128(np.concatenate([kr, kr[:, swap]], 1))
        winfm[l, 9:25] = tiles128(w[:, SPL[3]:SPL[4]])
        winfm[l, 25:41] = tiles128(w[:, SPL[4]:SPL[5]])
        winfm[l, 41:57] = tiles128(w[:, SPL[6]:SPL[7]])
        winfm[l, 57:73] = tiles128(w[:, SPL[7]:SPL[8]])
        for i, (lo) in enumerate([SPL[4], SPL[5], SPL[8]]):
            blk = w[:, lo:lo + 2048].reshape(16, 128, 4, 512).transpose(2, 1, 0, 3)
            wintm[l, i * 4:(i + 1) * 4] = blk.reshape(4, 128, 16 * 512)
        q = inp["w_uq"][l].reshape(4, 128, 16, 192).transpose(1, 0, 2, 3)
        wuq[l, :, :, :, 0:192] = q
        wuq[l, :, :, :, 192:256] = q[:, :, :, 128 + swap]
        wukv[l] = inp["w_ukv"][l].reshape(4, 128, 4096).transpose(1, 0, 2).reshape(128, 4 * 4096)
    out["winfm"] = winfm.reshape(L * 73, 128, 2048)
    out["wintm"] = wintm.reshape(L * 12, 128, 16 * 512)
    out["wuq"] = wuq.reshape(L, 128, 4 * 16 * 256)
    out["wukv"] = wukv
    out["wg"] = np.concatenate([tiles128(inp["w_gate"][l]) for l in range(L)], 0)
    out["wbr"] = np.concatenate([tiles128(inp["w_br"][l, b]) for l in range(L) for b in range(3)], 0)
    out["wo2"] = np.concatenate([tiles128(inp["w_o"][l]) for l in range(L)], 0)
    out["lng"] = np.ascontiguousarray(np.broadcast_to(inp["gm_ln_g"][:, None, :], (L, 128, 2048)))
    out["lnb"] = np.ascontiguousarray(np.broadcast_to(inp["gm_ln_b"][:, None, :], (L, 128, 2048)))
    out["wst"] = np.ascontiguousarray(inp["gm_w_s"].transpose(0, 3, 1, 2)).reshape(L, 128, 512)
    gb = np.broadcast_to(inp["gm_b_s"][:, None, :, None, :], (L, 128, 4, 4, 128))
    out["gmb"] = np.ascontiguousarray(gb).reshape(L, 128, 2048)
    return out


def run(inp, S, L, n_seq, stages=("f1", "mix", "f2")):
    nc = build_program(S, L, stages)
    shared = prep_shared(inp, L)
    if "mix" in stages:
        shared.update(prep_mixer(inp, L))
    in_maps = []
    for b in range(n_seq):
        m = dict(shared)
        m["xT"] = np.ascontiguousarray(inp["x"][b, :S].T)
        if "mix" in stages:
            p = np.asarray(inp["pos"][b, :S]).astype(np.int32)
            m["posb"] = np.ascontiguousarray(np.broadcast_to(p[None, :], (128, S)))
            m["post"] = np.ascontiguousarray(p.reshape(S // 128, 128).T)
        in_maps.append(m)
    res = run_bass_kernel_spmd(nc, in_maps, core_ids=list(range(n_seq)))
    return np.stack([np.ascontiguousarray(r["yT"].T) for r in res.results], 0)


def kernel(**inputs):
    inp = {k: np.asarray(v) for k, v in inputs.items()}
    B, S, _ = inp["x"].shape
    L = inp["ffn1_pre_g"].shape[0]
    return run(inp, S, L, B).astype(np.float32)
```

```python
import numpy as np
from contextlib import ExitStack
import concourse.bass as bass
import concourse.mybir as mybir
from concourse.bass_utils import run_bass_kernel_spmd

F32, BF16, I32 = mybir.dt.float32, mybir.dt.bfloat16, mybir.dt.int32
ALU = mybir.AluOpType
AF = mybir.ActivationFunctionType

D = 2048
DFF = 5504
NHC = 43
TT = 512
EPS = 1e-6
NV_L = 152
C_MASKA, C_DT, C_QDEC, C_KDEC, C_MASKG, C_INV, C_INVROW, NC_TOT = 0, 2048, 3072, 4096, 4104, 4232, 4240, 4368
V_F1PRE, V_F1POST, V_MIXPRE, V_QN, V_KVN, V_BG, V_MIXPOST, V_F2PRE, V_F2POST = 0, 16, 32, 48, 52, 56, 104, 120, 136


class Tk:
    __slots__ = ("name", "w", "r")

    def __init__(self, name):
        self.name = name
        self.w = None
        self.r = {}


class Prog:
    ENG = ("pe", "act", "dve", "pool", "sp")

    def __init__(self, ndma=12):
        self.ops = {e: [] for e in self.ENG}
        self.cnt = {}
        self.seen = {e: {} for e in self.ENG}
        self.rr = {e: 0 for e in self.ENG}
        self.ndma = ndma

    def op(self, eng, fn, reads=(), writes=(), dma=False):
        need = {}

        def add(k, v):
            if need.get(k, 0) < v:
                need[k] = v

        for t in reads:
            if t.w is not None:
                add(*t.w)
        for t in writes:
            if t.w is not None:
                add(*t.w)
            for k, v in t.r.items():
                add(k, v)
        own = "c_" + eng
        waits = []
        seen = self.seen[eng]
        for k, v in need.items():
            if k == own and eng == "pe":
                continue
            if seen.get(k, 0) >= v:
                continue
            seen[k] = v
            waits.append((k, v))
        if dma:
            j = self.rr[eng]
            self.rr[eng] = (j + 1) % self.ndma
            k = "d_%s_%d" % (eng, j)
            prev = self.cnt.get(k, 0)
            if prev and seen.get(k, 0) < prev:
                seen[k] = prev
                waits.append((k, prev))
            self.cnt[k] = prev + 16
            tok = (k, prev + 16)
            inc = 16
        else:
            k = own
            self.cnt[k] = self.cnt.get(k, 0) + 1
            tok = (k, self.cnt[k])
            inc = 1
        self.ops[eng].append((waits, fn, k, inc))
        for t in writes:
            t.w = tok
            t.r = {}
        for t in reads:
            if t.r.get(k, 0) < tok[1]:
                t.r[k] = tok[1]
        return tok

    def barrier(self, engines=None):
        for e in (engines or self.ENG):
            waits = []
            seen = self.seen[e]
            for k, v in self.cnt.items():
                if e == "pe" and k == "c_pe":
                    continue
                if seen.get(k, 0) < v:
                    seen[k] = v
                    waits.append((k, v))
            if waits:
                self.ops[e].append((waits, None, None, 0))

    def replay(self, eng, e, sems):
        for waits, fn, k, inc in self.ops[eng]:
            for wk, wv in waits:
                e.wait_ge(sems[wk], wv)
            if fn is not None:
                ins = fn(e)
                ins.then_inc(sems[k], inc)


class Arena:
    def __init__(self, tensor, nbytes):
        self.t32 = tensor
        self.t16 = tensor.bitcast(BF16)
        self.off = 0
        self.nbytes = nbytes

    def reset(self, off=0):
        self.off = off

    def alloc(self, cols, dt, parts=128):
        esz = 4 if dt in (F32, I32) else 2
        nb = (cols * esz + 31) // 32 * 32
        o = self.off
        assert o + nb <= self.nbytes, ("arena overflow", o, nb, self.nbytes)
        self.off = o + nb
        if dt == F32:
            return self.t32[0:parts, o // 4:o // 4 + cols]
        if dt == I32:
            return self.t32.bitcast(I32)[0:parts, o // 4:o // 4 + cols]
        return self.t16[0:parts, o // 2:o // 2 + cols]


def build_program(S, L, stages=("f1", "mix", "f2"), dbg=False):
    NT = S // TT
    nc = bass.Bass("TRN2", target_bir_lowering=False)
    P = Prog()
    di = {}

    def dram_in(name, shape, dt=F32):
        di[name] = nc.dram_tensor(name, list(shape), dt, kind="ExternalInput")
        return di[name]

    xT = dram_in("xT", [D, S])
    vecs_d = dram_in("vecs", [128, NV_L * L])
    f1wi = dram_in("f1wi", [L * NHC, 128, 16 * 256])
    f1wo = dram_in("f1wo", [L * 16, 128, DFF])
    f2wi = dram_in("f2wi", [L * NHC, 128, 16 * 256])
    f2wo = dram_in("f2wo", [L * 16, 128, DFF])
    yT = nc.dram_tensor("yT", [D, S], F32, kind="ExternalOutput")
    NBLK = S // 128
    if "mix" in stages:
        consts_d = dram_in("consts", [128, NC_TOT])
        posb_d = dram_in("posb", [128, S], I32)
        post_d = dram_in("post", [128, NBLK], I32)
        winfm = dram_in("winfm", [L * 73, 128, 2048])
        wintm = dram_in("wintm", [L * 12, 128, 16 * 512])
        wuq_d = dram_in("wuq", [L, 128, 4 * 16 * 256])
        wukv_d = dram_in("wukv", [L, 128, 4 * 4096])
        wg_d = dram_in("wg", [L * 48, 128, 2048])
        wbr_d = dram_in("wbr", [L * 48, 128, 2048])
        wo2_d = dram_in("wo2", [L * 16, 128, 2048])
        lng_d = dram_in("lng", [L, 128, 2048])
        lnb_d = dram_in("lnb", [L, 128, 2048])
        wst_d = dram_in("wst", [L, 128, 512])
        gmb_d = dram_in("gmb", [L, 128, 2048])

        def scr(name, shape, dt=BF16):
            return nc.dram_tensor(name, list(shape), dt).ap()
        cosk_s, sink_s = scr("cosk_s", [128, S], F32), scr("sink_s", [128, S], F32)
        c64_s, s64_s = scr("c64_s", [64, S], F32), scr("s64_s", [64, S], F32)
        costm_s, sintm_s = scr("costm_s", [S, 128], F32), scr("sintm_s", [S, 128], F32)
        qn_s, qr_s, kT_s = scr("qn_s", [16, 128, S]), scr("qr_s", [16, 64, S]), scr("kT_s", [16, 128, S])
        kr_s = scr("kr_s", [64, S])
        v_s = scr("v_s", [S, 2048])
        rq_s, rkT_s = scr("rq_s", [16, 128, S]), scr("rkT_s", [16, 128, S])
        rktm_s, rvtm_s, gvtm_s = scr("rktm_s", [S, 2048]), scr("rvtm_s", [S, 2048]), scr("gvtm_s", [S, 2048])
        rg_s, gu_s = scr("rg_s", [16, 128, S]), scr("gu_s", [16, 128, S])
        ya_s, yb_s, yc_s = scr("ya_s", [16, 128, S]), scr("yb_s", [16, 128, S]), scr("yc_s", [16, 128, S])

    xT_v = xT.ap().rearrange("(kc p) s -> p kc s", p=128)
    yT_v = yT.ap().rearrange("(kc p) s -> p kc s", p=128)

    ARENA_BYTES = 190 * 1024
    with ExitStack() as es:
        arena_t = es.enter_context(nc.sbuf_tensor("arena", [128, ARENA_BYTES // 4], F32))
        ar = Arena(arena_t, ARENA_BYTES)
        ps = [es.enter_context(nc.psum_tensor("ps%d" % i, [128, 512], F32)) for i in range(8)]
        Tps = [Tk("ps%d" % i) for i in range(8)]

        ones = ar.alloc(128, F32)
        Tones = Tk("ones")
        vecs = ar.alloc(NV_L * L, F32)
        Tvecs = Tk("vecs")
        base_off = ar.off

        P.op("pool", lambda e: e.memset(ones, 1.0), writes=[Tones])
        P.op("sp", lambda e: e.dma_start(out=vecs, in_=vecs_d.ap()), writes=[Tvecs], dma=True)

        TY = [[Tk("y%d_%d" % (t, k)) for k in range(16)] for t in range(NT)]
        state = {"first": False}
        for t in range(NT):
            P.op("sp", lambda e, t=t: e.dma_start(out=yT.ap()[:, t * TT:(t + 1) * TT], in_=xT.ap()[:, t * TT:(t + 1) * TT]),
                 writes=TY[t], dma=True)

        def vcol(l, base, kc):
            c = l * NV_L + base + kc
            return vecs[:, c:c + 1]

        def ffn_phase(l, wi_d, wo_d, vpre, vpost):
            ar.reset(base_off)
            O = Bf(16 * TT, F32, r="p (k t) -> p k t", k=16)
            HH = [Bf(16 * TT, BF16, r="p (k t) -> p k t", k=16) for _ in range(2)]
            A = Bf(NHC * TT, BF16, r="p (k t) -> p k t", k=NHC)
            XC = [Bf(TT, F32) for _ in range(4)]
            XR = [Bf(TT, F32) for _ in range(4)]
            RS2 = Bf(TT, F32)
            SQ = [Bf(TT, F32) for _ in range(2)]
            SA = [Bf(TT, F32) for _ in range(2)]
            RS = Bf(TT, F32)
            TMP = [Bf(TT, F32) for _ in range(2)]
            WA = [Bf(4096, BF16) for _ in range(3)]
            WB = [Bf(DFF, BF16) for _ in range(2)]
            c = {"a": 0, "b": 0}
            load_h_chunked(0, HH[0], RS, SQ, XC, l, vpre)
            for t in range(NT):
                tsl = slice(t * TT, (t + 1) * TT)
                H = HH[t % 2]
                for hc in range(NHC):
                    wa = WA[c["a"] % 3]
                    c["a"] += 1
                    dma("pool", wa.ap, wi_d.ap()[l * NHC + hc], [], [wa.tk])
                    ia, ib = (hc % 2) * 2, (hc % 2) * 2 + 1

                    def mm(e, wa=wa, ia=ia, ib=ib, H=H):
                        for kc in range(16):
                            e.matmul(ps[ia][:], lhsT=wa.ap[:, kc * 256:kc * 256 + 128], rhs=H.ap[:, kc, :],
                                     start=(kc == 0), stop=(kc == 15))
                        for kc in range(16):
                            ins = e.matmul(ps[ib][:], lhsT=wa.ap[:, kc * 256 + 128:kc * 256 + 256], rhs=H.ap[:, kc, :],
                                           start=(kc == 0), stop=(kc == 15))
                        return ins
                    P.op("pe", mm, [wa.tk, H.tk], [Tps[ia], Tps[ib]])
                    sa = SA[hc % 2]
                    act(sa.ap, ps[ia][:], AF.Silu, [Tps[ia]], [sa.tk])
                    tt("dve", A.ap[:, hc, :], sa.ap, ps[ib][:], ALU.mult, [sa.tk, Tps[ib]], [A.tk])
                if t + 1 < NT:
                    load_h_chunked(t + 1, HH[(t + 1) % 2], RS, SQ, XC, l, vpre)
                for mc in range(16):
                    wb = WB[c["b"] % 2]
                    c["b"] += 1
                    dma("pool", wb.ap, wo_d.ap()[l * 16 + mc], [], [wb.tk])
                    io = 4 + mc % 2
                    mmg(ps[io][:], [(wb.ap[:, kc * 128:(kc + 1) * 128], A.ap[:, kc, :]) for kc in range(NHC)],
                        [wb.tk, A.tk], [Tps[io]])
                    act(O.ap[:, mc, :], ps[io][:], AF.Copy, [Tps[io]], [O.tk])
                rms_stats(O.ap, O.tk, [s.ap for s in SQ], [s.tk for s in SQ], ps[7], Tps[7], RS2.ap, RS2.tk, 16, D)
                for mc in range(16):
                    xc = XR[mc % 4]
                    dma("sp", xc.ap, yT_v[:, mc, tsl], [TY[t][mc]], [xc.tk])
                    tm = TMP[mc % 2]
                    stt("dve", tm.ap, O.ap[:, mc, :], vcol(l, vpost, mc), RS2.ap, ALU.mult, ALU.mult,
                        [O.tk, RS2.tk, Tvecs], [tm.tk])
                    stt("dve", xc.ap, tm.ap, 0.5, xc.ap, ALU.mult, ALU.add, [tm.tk, xc.tk], [xc.tk])
                    dma("sp", yT_v[:, mc, tsl], xc.ap, [xc.tk], [TY[t][mc]])
            P.barrier()

        def rms_stats(X, TX, SQ, TSQ, pss, Tpss, RS, TRS, nchunk, dim):
            for kc in range(nchunk):
                s = kc % 2
                P.op("act", lambda e, kc=kc, s=s: e.activation(out=SQ[s], in_=X[:, kc, :], func=AF.Square),
                     reads=[TX], writes=[TSQ[s]])
                P.op("pe", lambda e, kc=kc, s=s: e.matmul(pss[:], lhsT=ones, rhs=SQ[s], start=(kc == 0),
                                                          stop=(kc == nchunk - 1)),
                     reads=[TSQ[s], Tones], writes=[Tpss])
            P.op("act", lambda e: e.activation(out=RS, in_=pss[:], func=AF.Sqrt, bias=EPS, scale=1.0 / dim),
                 reads=[Tpss], writes=[TRS])
            P.op("dve", lambda e: e.reciprocal(out=RS, in_=RS), reads=[TRS], writes=[TRS])


        def dma(eng, out, in_, reads=(), writes=()):
            P.op(eng, lambda e: e.dma_start(out=out, in_=in_), reads, writes, dma=True)

        def act(out, in_, func, reads, writes, **kw):
            P.op("act", lambda e: e.activation(out=out, in_=in_, func=func, **kw), reads, writes)

        def tt(eng, out, a, b, op, reads, writes):
            P.op(eng, lambda e: e.tensor_tensor(out=out, in0=a, in1=b, op=op), reads, writes)

        def ts(eng, out, a, s1, s2, op0, op1, reads, writes):
            if s2 is None:
                P.op(eng, lambda e: e.tensor_scalar(out=out, in0=a, scalar1=s1, scalar2=None, op0=op0), reads, writes)
            else:
                P.op(eng, lambda e: e.tensor_scalar(out=out, in0=a, scalar1=s1, scalar2=s2, op0=op0, op1=op1), reads, writes)

        def stt(eng, out, a, s, b, op0, op1, reads, writes):
            P.op(eng, lambda e: e.scalar_tensor_tensor(out=out, in0=a, scalar=s, in1=b, op0=op0, op1=op1), reads, writes)

        def mmg(out, pairs, reads, writes):
            n = len(pairs)

            def f(e):
                for i, (a, b) in enumerate(pairs):
                    ins = e.matmul(out, lhsT=a, rhs=b, start=(i == 0), stop=(i == n - 1))
                return ins
            P.op("pe", f, reads, writes)

        class Bf:
            def __init__(self, cols, dt, parts=128, r=None, **kw):
                self.ap = ar.alloc(cols, dt, parts)
                if r is not None:
                    self.ap = self.ap.rearrange(r, **kw)
                self.tk = Tk("b")

        def load_h_chunked(t, H, RS, SQ, XC, l, vbase):
            tsl = slice(t * TT, (t + 1) * TT)
            n = len(XC)
            for kc in range(16):
                xc = XC[kc % n]
                dma("sp", xc.ap, yT_v[:, kc, tsl], [TY[t][kc]], [xc.tk])
                s = SQ[kc % 2]
                act(s.ap, xc.ap, AF.Square, [xc.tk], [s.tk])
                P.op("pe", lambda e, kc=kc, s=s: e.matmul(ps[6][:], lhsT=ones, rhs=s.ap, start=(kc == 0), stop=(kc == 15)),
                     [s.tk, Tones], [Tps[6]])
            act(RS.ap, ps[6][:], AF.Sqrt, [Tps[6]], [RS.tk], bias=EPS, scale=1.0 / D)
            P.op("dve", lambda e: e.reciprocal(out=RS.ap, in_=RS.ap), [RS.tk], [RS.tk])
            for kc in range(16):
                xc = XC[(16 + kc) % n]
                dma("sp", xc.ap, yT_v[:, kc, tsl], [TY[t][kc]], [xc.tk])
                stt("dve", H.ap[:, kc, :], xc.ap, vcol(l, vbase, kc), RS.ap, ALU.mult, ALU.mult,
                    [xc.tk, RS.tk, Tvecs], [H.tk])

        def load_xh(t, X, H, RS, SQ, l, first_src=False):
            tsl = slice(t * TT, (t + 1) * TT)
            dma("sp", X.ap, yT_v[:, :, tsl], TY[t], [X.tk])
            rms_stats(X.ap, X.tk, [s.ap for s in SQ], [s.tk for s in SQ], ps[6], Tps[6], RS.ap, RS.tk, 16, D)
            for kc in range(16):
                stt("dve", H.ap[:, kc, :], X.ap[:, kc, :], vcol(l, V_MIXPRE, kc), RS.ap, ALU.mult, ALU.mult,
                    [X.tk, RS.tk, Tvecs], [H.tk])

        def gelu_tanh(eng_sets, out, xin, tmp1, tmp2, reads, writes, shape_tk):
            t1, t2 = tmp1, tmp2
            tt("dve", t1.ap, xin, xin, ALU.mult, reads, [t1.tk])
            ts("dve", t1.ap, t1.ap, 0.044715, 1.0, ALU.mult, ALU.add, [t1.tk], [t1.tk])
            tt("dve", t1.ap, t1.ap, xin, ALU.mult, reads + [t1.tk], [t1.tk])
            act(t2.ap, t1.ap, AF.Sigmoid, [t1.tk], [t2.tk], scale=1.5957691216)
            tt("dve", out, t2.ap, xin, ALU.mult, reads + [t2.tk], writes)

        def rope_tables_phase():
            ar.reset(base_off)
            CI = Bf(8, F32)
            IR = Bf(128, F32)
            dma("sp", CI.ap, consts_d.ap()[:, C_INV:C_INV + 8], [], [CI.tk])
            dma("sp", IR.ap, consts_d.ap()[:, C_INVROW:C_INVROW + 128], [], [IR.tk])
            PI_ = Bf(TT, I32)
            PF = Bf(TT, F32)
            ANG = Bf(TT, F32)
            KI = Bf(TT, I32)
            KF, RR, NR = Bf(TT, F32), Bf(TT, F32), Bf(TT, F32)
            OUT = [Bf(TT, F32) for _ in range(2)]
            HALF_PI = float(np.pi / 2)
            C1 = 6.28125
            C2 = float(2 * np.pi - 6.28125)

            def sincos(ang, n, w, osin, ocos):
                kf, rr, nr, ki = KF.ap[0:n, 0:w], RR.ap[0:n, 0:w], NR.ap[0:n, 0:w], KI.ap[0:n, 0:w]
                ts("dve", kf, ang, float(1.0 / (2 * np.pi)), None, ALU.mult, None, [ANG.tk], [KF.tk])
                P.op("dve", lambda e: e.tensor_copy(out=ki, in_=kf), [KF.tk], [KI.tk])
                P.op("dve", lambda e: e.tensor_copy(out=kf, in_=ki), [KI.tk], [KF.tk])
                stt("dve", rr, kf, -C1, ang, ALU.mult, ALU.add, [KF.tk, ANG.tk], [RR.tk])
                stt("dve", rr, kf, -C2, rr, ALU.mult, ALU.add, [KF.tk, RR.tk], [RR.tk])
                ts("dve", rr, rr, -3.14159, 3.14159, ALU.max, ALU.min, [RR.tk], [RR.tk])
                act(osin.ap[0:n, 0:w], rr, AF.Sin, [RR.tk], [osin.tk])
                ts("dve", nr, rr, -1.0, None, ALU.mult, None, [RR.tk], [NR.tk])
                tt("dve", nr, nr, rr, ALU.max, [NR.tk, RR.tk], [NR.tk])
                ts("dve", nr, nr, -1.0, HALF_PI, ALU.mult, ALU.add, [NR.tk], [NR.tk])
                act(ocos.ap[0:n, 0:w], nr, AF.Sin, [NR.tk], [ocos.tk])

            for t in range(NT):
                tsl = slice(t * TT, (t + 1) * TT)
                dma("sp", PI_.ap, posb_d.ap()[:, tsl], [], [PI_.tk])
                P.op("dve", lambda e: e.tensor_copy(out=PF.ap, in_=PI_.ap), [PI_.tk], [PF.tk])
                for which in range(2):
                    n = 128 if which == 0 else 64
                    ts("dve", ANG.ap[0:n], PF.ap[0:n], CI.ap[0:n, which:which + 1], None, ALU.mult, None,
                       [PF.tk, CI.tk], [ANG.tk])
                    sincos(ANG.ap[0:n], n, TT, OUT[0], OUT[1])
                    if which == 0:
                        dma("sp", sink_s[:, tsl], OUT[0].ap, [OUT[0].tk], [])
                        dma("sp", cosk_s[:, tsl], OUT[1].ap, [OUT[1].tk], [])
                    else:
                        ts("dve", OUT[0].ap[0:64], OUT[0].ap[0:64], CI.ap[0:64, 2:3], None, ALU.mult, None,
                           [OUT[0].tk, CI.tk], [OUT[0].tk])
                        dma("sp", s64_s[:, tsl], OUT[0].ap[0:64], [OUT[0].tk], [])
                        dma("sp", c64_s[:, tsl], OUT[1].ap[0:64], [OUT[1].tk], [])
            PT_ = Bf(NBLK, I32)
            PTF = Bf(NBLK, F32)
            dma("sp", PT_.ap, post_d.ap(), [], [PT_.tk])
            P.op("dve", lambda e: e.tensor_copy(out=PTF.ap, in_=PT_.ap), [PT_.tk], [PTF.tk])
            for n in range(NBLK):
                ts("dve", ANG.ap[:, 0:128], IR.ap, PTF.ap[:, n:n + 1], None, ALU.mult, None, [IR.tk, PTF.tk], [ANG.tk])
                sincos(ANG.ap[:, 0:128], 128, 128, OUT[0], OUT[1])
                dma("sp", sintm_s[n * 128:(n + 1) * 128, :], OUT[0].ap[:, 0:128], [OUT[0].tk], [])
                dma("sp", costm_s[n * 128:(n + 1) * 128, :], OUT[1].ap[:, 0:128], [OUT[1].tk], [])
            P.barrier()

        def m1a_phase(l):
            ar.reset(base_off)
            XC = [Bf(TT, F32) for _ in range(4)]
            H = Bf(16 * TT, BF16, r="p (k t) -> p k t", k=16)
            WUQ = Bf(4 * 16 * 256, BF16, r="p (k h c) -> p k h c", k=4, h=16)
            WUKV = Bf(4 * 4096, BF16, r="p (k h c) -> p k h c", k=4, h=16)
            W = [Bf(2048, BF16) for _ in range(8)]
            ZQ = Bf(4 * TT, F32, r="p (k t) -> p k t", k=4)
            ZKV = Bf(4 * TT, F32, r="p (k t) -> p k t", k=4)
            ZQN = Bf(4 * TT, BF16, r="p (k t) -> p k t", k=4)
            ZKVN = Bf(4 * TT, BF16, r="p (k t) -> p k t", k=4)
            SQ = [Bf(TT, F32) for _ in range(2)]
            RS = Bf(TT, F32)
            CK, SK, C64, S64 = Bf(TT, F32), Bf(TT, F32), Bf(TT, F32), Bf(TT, F32)
            R0, R1 = [Bf(TT, F32) for _ in range(2)], [Bf(TT, F32) for _ in range(2)]
            T0, T1, T2, T3 = Bf(TT, F32), Bf(TT, F32), Bf(TT, F32), Bf(TT, F32)
            ST = [Bf(TT, BF16) for _ in range(4)]
            VT = [Bf(2048, BF16) for _ in range(1)]
            dma("pool", WUQ.ap, wuq_d.ap()[l].rearrange("p (k h c) -> p k h c", k=4, h=16), [], [WUQ.tk])
            dma("pool", WUKV.ap, wukv_d.ap()[l].rearrange("p (k h c) -> p k h c", k=4, h=16), [], [WUKV.tk])
            cnt = {"w": 0, "st": 0, "ps": 0}

            def wtile(idx):
                w = W[cnt["w"] % 8]
                cnt["w"] += 1
                dma("pool", w.ap, winfm.ap()[l * 73 + idx], [], [w.tk])
                return w

            def stage():
                s = ST[cnt["st"] % 4]
                cnt["st"] += 1
                return s

            def proj(w, lo, hi, pi, npart=128):
                mmg(ps[pi][0:npart, :], [(w.ap[:, kc * 128 + lo:kc * 128 + hi], H.ap[:, kc, :]) for kc in range(16)],
                    [w.tk, H.tk], [Tps[pi]])

            for t in range(NT):
                tsl = slice(t * TT, (t + 1) * TT)
                load_h_chunked(t, H, RS, SQ, XC, l, V_MIXPRE)
                dma("sp", CK.ap, cosk_s[:, tsl], [], [CK.tk])
                dma("sp", SK.ap, sink_s[:, tsl], [], [SK.tk])
                dma("sp", C64.ap[0:64], c64_s[:, tsl], [], [C64.tk])
                dma("sp", S64.ap[0:64], s64_s[:, tsl], [], [S64.tk])
                for i in range(8):
                    w = wtile(i)
                    pi = i % 2
                    proj(w, 0, 128, pi)
                    dst = ZQ if i < 4 else ZKV
                    act(dst.ap[:, i % 4, :], ps[pi][:], AF.Copy, [Tps[pi]], [dst.tk])
                for (Z, ZN, vb) in ((ZQ, ZQN, V_QN), (ZKV, ZKVN, V_KVN)):
                    rms_stats(Z.ap, Z.tk, [s.ap for s in SQ], [s.tk for s in SQ], ps[6], Tps[6], RS.ap, RS.tk, 4, 512)
                    for kc in range(4):
                        stt("dve", ZN.ap[:, kc, :], Z.ap[:, kc, :], vcol(l, vb, kc), RS.ap, ALU.mult, ALU.mult,
                            [Z.tk, RS.tk, Tvecs], [ZN.tk])

                def rope64(pa, pb, scale, out_ap, out_tk):
                    act(R0[0].ap[0:64], ps[pa][0:64, :], AF.Copy, [Tps[pa]], [R0[0].tk], scale=scale)
                    act(R1[0].ap[0:64], ps[pb][0:64, :], AF.Copy, [Tps[pb]], [R1[0].tk], scale=scale)
                    tt("dve", T0.ap[0:64], R0[0].ap[0:64], C64.ap[0:64], ALU.mult, [R0[0].tk, C64.tk], [T0.tk])
                    tt("dve", T1.ap[0:64], R1[0].ap[0:64], S64.ap[0:64], ALU.mult, [R1[0].tk, S64.tk], [T1.tk])
                    tt("dve", out_ap, T0.ap[0:64], T1.ap[0:64], ALU.add, [T0.tk, T1.tk], [out_tk])

                w = wtile(8)
                proj(w, 0, 64, 0, 64)
                proj(w, 64, 128, 1, 64)
                s = stage()
                rope64(0, 1, 1.0, s.ap[0:64], s.tk)
                dma("sp", kr_s[:, tsl], s.ap[0:64], [s.tk], [])
                for kind, base, dst_s, scale in (("rq", 9, rq_s, 1.0), ("rk", 25, rkT_s, 1.0 / 16.0)):
                    for h in range(8):
                        pr = h % 2
                        for c in range(2):
                            w = wtile(base + h * 2 + c)
                            proj(w, 0, 128, 2 * pr + c)
                        act(R0[pr].ap, ps[2 * pr][:], AF.Copy, [Tps[2 * pr]], [R0[pr].tk], scale=scale)
                        act(R1[pr].ap, ps[2 * pr + 1][:], AF.Copy, [Tps[2 * pr + 1]], [R1[pr].tk], scale=scale)
                        s0, s1 = stage(), stage()
                        tt("dve", T0.ap, R0[pr].ap, CK.ap, ALU.mult, [R0[pr].tk, CK.tk], [T0.tk])
                        tt("dve", T1.ap, R1[pr].ap, SK.ap, ALU.mult, [R1[pr].tk, SK.tk], [T1.tk])
                        tt("dve", s0.ap, T0.ap, T1.ap, ALU.subtract, [T0.tk, T1.tk], [s0.tk])
                        tt("dve", T2.ap, R1[pr].ap, CK.ap, ALU.mult, [R1[pr].tk, CK.tk], [T2.tk])
                        tt("dve", T3.ap, R0[pr].ap, SK.ap, ALU.mult, [R0[pr].tk, SK.tk], [T3.tk])
                        tt("dve", s1.ap, T2.ap, T3.ap, ALU.add, [T2.tk, T3.tk], [s1.tk])
                        dma("sp", dst_s[2 * h][:, tsl], s0.ap, [s0.tk], [])
                        dma("sp", dst_s[2 * h + 1][:, tsl], s1.ap, [s1.tk], [])
                for i in range(16):
                    w = wtile(41 + i)
                    pi = i % 2
                    proj(w, 0, 128, pi)
                    s = stage()
                    act(s.ap, ps[pi][:], AF.Silu, [Tps[pi]], [s.tk])
                    dma("sp", rg_s[i][:, tsl], s.ap, [s.tk], [])
                for i in range(16):
                    w = wtile(57 + i)
                    pi = i % 2
                    proj(w, 0, 128, pi)
                    act(R0[pi].ap, ps[pi][:], AF.Copy, [Tps[pi]], [R0[pi].tk])
                    s = stage()
                    gelu_tanh(None, s.ap, R0[pi].ap, T0, T1, [R0[pi].tk], [s.tk], None)
                    dma("sp", gu_s[i][:, tsl], s.ap, [s.tk], [])
                QSC = 192.0 ** -0.5
                for h in range(16):
                    pi = 2 + h % 2
                    mmg(ps[pi][:], [(WUQ.ap[:, kc, h, 0:128], ZQN.ap[:, kc, :]) for kc in range(4)],
                        [WUQ.tk, ZQN.tk], [Tps[pi]])
                    s = stage()
                    act(s.ap, ps[pi][:], AF.Copy, [Tps[pi]], [s.tk], scale=QSC)
                    dma("sp", qn_s[h][:, tsl], s.ap, [s.tk], [])
                    mmg(ps[0][0:64, :], [(WUQ.ap[:, kc, h, 128:192], ZQN.ap[:, kc, :]) for kc in range(4)],
                        [WUQ.tk, ZQN.tk], [Tps[0]])
                    mmg(ps[1][0:64, :], [(WUQ.ap[:, kc, h, 192:256], ZQN.ap[:, kc, :]) for kc in range(4)],
                        [WUQ.tk, ZQN.tk], [Tps[1]])
                    s = stage()
                    rope64(0, 1, QSC, s.ap[0:64], s.tk)
                    dma("sp", qr_s[h][:, tsl], s.ap[0:64], [s.tk], [])
                    pi = 4 + h % 2
                    mmg(ps[pi][:], [(WUKV.ap[:, kc, h, 0:128], ZKVN.ap[:, kc, :]) for kc in range(4)],
                        [WUKV.tk, ZKVN.tk], [Tps[pi]])
                    s = stage()
                    act(s.ap, ps[pi][:], AF.Copy, [Tps[pi]], [s.tk])
                    dma("sp", kT_s[h][:, tsl], s.ap, [s.tk], [])
                for blk in range(4):
                    vt = VT[0]
                    for j in range(4):
                        pi = 2 + j % 2
                        mmg(ps[pi][:], [(ZKVN.ap[:, kc, blk * 128:(blk + 1) * 128], WUKV.ap[:, kc, 4 * j:4 * j + 4, 128:256])
                                        for kc in range(4)], [WUKV.tk, ZKVN.tk], [Tps[pi]])
                        act(vt.ap[:, j * 512:(j + 1) * 512], ps[pi][:], AF.Copy, [Tps[pi]], [vt.tk])
                    r0 = t * TT + blk * 128
                    dma("sp", v_s[r0:r0 + 128, :], vt.ap, [vt.tk], [])
            P.barrier()

        def m1b_phase(l):
            ar.reset(base_off)
            XC = [Bf(TT, F32) for _ in range(4)]
            H = Bf(16 * TT, BF16, r="p (k t) -> p k t", k=16)
            W = [Bf(16 * 512, BF16) for _ in range(4)]
            SQ = [Bf(TT, F32) for _ in range(2)]
            RS = Bf(TT, F32)
            GV = Bf(4 * 2048, F32, r="p (b c) -> p b c", b=4)
            LNG, LNB = Bf(2048, F32), Bf(2048, F32)
            CT, STm = Bf(4 * 128, F32, r="p (b f) -> p b f", b=4), Bf(4 * 128, F32, r="p (b f) -> p b f", b=4)
            KS = Bf(512, F32)
            T0, T1, T2 = Bf(512, F32), Bf(512, F32), Bf(512, F32)
            OS = [Bf(512, BF16) for _ in range(3)]
            GO = [Bf(2048, BF16) for _ in range(2)]
            STAT = Bf(8, F32)
            LNscr = Bf(2048, F32)
            dma("sp", LNG.ap, lng_d.ap()[l], [], [LNG.tk])
            dma("sp", LNB.ap, lnb_d.ap()[l], [], [LNB.tk])
            c = {"w": 0, "o": 0}
            for t in range(NT):
                load_h_chunked(t, H, RS, SQ, XC, l, V_MIXPRE)
                r_t = t * TT
                dma("sp", CT.ap, costm_s[r_t:r_t + TT, :].rearrange("(b p) f -> p b f", p=128), [], [CT.tk])
                dma("sp", STm.ap, sintm_s[r_t:r_t + TT, :].rearrange("(b p) f -> p b f", p=128), [], [STm.tk])
                for wi_ in range(12):
                    w = W[c["w"] % 4]
                    c["w"] += 1
                    dma("pool", w.ap, wintm.ap()[l * 12 + wi_], [], [w.tk])
                    kind, j = wi_ // 4, wi_ % 4
                    for blk in range(4):
                        pi = blk % 2
                        mmg(ps[pi][:], [(H.ap[:, kc, blk * 128:(blk + 1) * 128], w.ap[:, kc * 512:(kc + 1) * 512])
                                        for kc in range(16)], [w.tk, H.tk], [Tps[pi]])
                        r0 = r_t + blk * 128
                        if kind == 1:
                            o = OS[c["o"] % 3]
                            c["o"] += 1
                            act(o.ap, ps[pi][:], AF.Copy, [Tps[pi]], [o.tk])
                            dma("sp", rvtm_s[r0:r0 + 128, j * 512:(j + 1) * 512], o.ap, [o.tk], [])
                        elif kind == 0:
                            act(KS.ap, ps[pi][:], AF.Copy, [Tps[pi]], [KS.tk], scale=1.0 / 16.0)
                            o = OS[c["o"] % 3]
                            c["o"] += 1
                            K4 = KS.ap.rearrange("p (h x f) -> p h x f", h=2, x=2)
                            O4 = o.ap.rearrange("p (h x f) -> p h x f", h=2, x=2)
                            for hh in range(2):
                                x1, x2 = K4[:, hh, 0, :], K4[:, hh, 1, :]
                                cb, sb = CT.ap[:, blk, :], STm.ap[:, blk, :]
                                tt("dve", T0.ap[:, 0:128], x1, cb, ALU.mult, [KS.tk, CT.tk], [T0.tk])
                                tt("dve", T1.ap[:, 0:128], x2, sb, ALU.mult, [KS.tk, STm.tk], [T1.tk])
                                tt("dve", O4[:, hh, 0, :], T0.ap[:, 0:128], T1.ap[:, 0:128], ALU.subtract, [T0.tk, T1.tk], [o.tk])
                                tt("dve", T2.ap[:, 0:128], x2, cb, ALU.mult, [KS.tk, CT.tk], [T2.tk])
                                tt("dve", T2.ap[:, 128:256], x1, sb, ALU.mult, [KS.tk, STm.tk], [T2.tk])
                                tt("dve", O4[:, hh, 1, :], T2.ap[:, 0:128], T2.ap[:, 128:256], ALU.add, [T2.tk], [o.tk])
                            dma("sp", rktm_s[r0:r0 + 128, j * 512:(j + 1) * 512], o.ap, [o.tk], [])
                        else:
                            act(KS.ap, ps[pi][:], AF.Copy, [Tps[pi]], [KS.tk])
                            gelu_tanh(None, GV.ap[:, blk, j * 512:(j + 1) * 512], KS.ap, T0, T1, [KS.tk], [GV.tk], None)
                for blk in range(4):
                    g = GV.ap[:, blk, :]
                    go = GO[blk % 2]
                    P.op("dve", lambda e, g=g: e.reduce_sum(out=STAT.ap[:, 0:1], in_=g, axis=mybir.AxisListType.X),
                         [GV.tk], [STAT.tk])
                    tt("dve", LNscr.ap, g, g, ALU.mult, [GV.tk], [LNscr.tk])
                    P.op("dve", lambda e: e.reduce_sum(out=STAT.ap[:, 1:2], in_=LNscr.ap, axis=mybir.AxisListType.X),
                         [LNscr.tk], [STAT.tk])
                    ts("dve", STAT.ap[:, 2:3], STAT.ap[:, 0:1], 1.0 / 2048, None, ALU.mult, None, [STAT.tk], [STAT.tk])
                    ts("dve", STAT.ap[:, 3:4], STAT.ap[:, 1:2], 1.0 / 2048, None, ALU.mult, None, [STAT.tk], [STAT.tk])
                    tt("dve", STAT.ap[:, 4:5], STAT.ap[:, 2:3], STAT.ap[:, 2:3], ALU.mult, [STAT.tk], [STAT.tk])
                    tt("dve", STAT.ap[:, 5:6], STAT.ap[:, 3:4], STAT.ap[:, 4:5], ALU.subtract, [STAT.tk], [STAT.tk])
                    act(STAT.ap[:, 6:7], STAT.ap[:, 5:6], AF.Sqrt, [STAT.tk], [STAT.tk], bias=EPS, scale=1.0)
                    P.op("dve", lambda e: e.reciprocal(out=STAT.ap[:, 6:7], in_=STAT.ap[:, 6:7]), [STAT.tk], [STAT.tk])
                    ts("dve", g, g, STAT.ap[:, 2:3], STAT.ap[:, 6:7], ALU.subtract, ALU.mult, [GV.tk, STAT.tk], [GV.tk])
                    tt("dve", g, g, LNG.ap, ALU.mult, [GV.tk, LNG.tk], [GV.tk])
                    tt("dve", go.ap, g, LNB.ap, ALU.add, [GV.tk, LNB.tk], [go.tk])
                    r0 = r_t + blk * 128
                    dma("sp", gvtm_s[r0:r0 + 128, :], go.ap, [go.tk], [])
            P.barrier()

        LOGG = [float(np.log1p(-(2.0 ** (-5.0 - h)))) for h in range(8)]

        def m2_phase(l):
            ar.reset(base_off)
            MASKA = Bf(2048, F32, r="p (j q) -> p j q", j=4)
            DTc = Bf(1024, F32, r="p (h i) -> p h i", h=8)
            QDEC = Bf(1024, F32, r="p (h i) -> p h i", h=8)
            KDEC = Bf(8, F32)
            MASKG = Bf(128, F32)
            cd = consts_d.ap()
            dma("sp", MASKA.ap, cd[:, C_MASKA:C_MASKA + 2048].rearrange("p (j q) -> p j q", j=4), [], [MASKA.tk])
            dma("sp", DTc.ap, cd[:, C_DT:C_DT + 1024].rearrange("p (h i) -> p h i", h=8), [], [DTc.tk])
            dma("sp", QDEC.ap, cd[:, C_QDEC:C_QDEC + 1024].rearrange("p (h i) -> p h i", h=8), [], [QDEC.tk])
            dma("sp", KDEC.ap, cd[:, C_KDEC:C_KDEC + 8], [], [KDEC.tk])
            dma("sp", MASKG.ap, cd[:, C_MASKG:C_MASKG + 128], [], [MASKG.tk])
            ONESB = Bf(128, BF16)
            P.op("pool", lambda e: e.memset(ONESB.ap, 1.0), [], [ONESB.tk])
            KR = Bf(S, BF16)
            dma("sp", KR.ap[0:64], kr_s, [], [KR.tk])
            QN = [Bf(TT, BF16) for _ in range(2)]
            QR = [Bf(TT, BF16) for _ in range(2)]
            KT = [Bf(TT, BF16) for _ in range(2)]
            VB = [Bf(512, BF16, r="p (j e) -> p j e", j=4) for _ in range(2)]
            PT = [Bf(TT, BF16) for _ in range(3)]
            RDEN = Bf(TT, F32)
            YA = [Bf(TT, BF16) for _ in range(2)]
            SF = Bf(8 * 512, F32, r="p (h c e) -> p h c e", h=8, c=2)
            SB = Bf(8 * 512, BF16, r="p (h c e) -> p h c e", h=8, c=2)
            SFtk = [Tk("sf") for _ in range(8)]
            SBtk = [Tk("sb") for _ in range(8)]
            P.op("pool", lambda e: e.memset(SF.ap, 0.0), [], SFtk)
            P.op("pool", lambda e: e.memset(SB.ap, 0.0), [], SBtk)
            RQ = [Bf(2 * TT, BF16, r="p (c t) -> p c t", c=2) for _ in range(2)]
            RKT = [Bf(2 * TT, BF16, r="p (c t) -> p c t", c=2) for _ in range(2)]
            RG = [Bf(2 * TT, BF16, r="p (c t) -> p c t", c=2) for _ in range(2)]
            RKM = [Bf(4 * 256, BF16, r="p (b d) -> p b d", b=4) for _ in range(2)]
            RVM = [Bf(4 * 256, BF16, r="p (b d) -> p b d", b=4) for _ in range(2)]
            PTR = [Bf(128, BF16) for _ in range(2)]
            QD = [Bf(256, BF16, r="p (c i) -> p c i", c=2) for _ in range(2)]
            KD = [Bf(256, BF16) for _ in range(2)]
            RO = Bf(2 * TT, F32, r="p (c t) -> p c t", c=2)
            SQ2 = [Bf(TT, F32) for _ in range(2)]
            MEAN, MSQ, VAR, RSTD = Bf(TT, F32), Bf(TT, F32), Bf(TT, F32), Bf(TT, F32)
            YB = [Bf(TT, BF16) for _ in range(2)]
            WSTf = Bf(512, F32, r="p (g i) -> p g i", g=4)
            WST = Bf(512, BF16, r="p (g i) -> p g i", g=4)
            GMB = Bf(2048, F32, r="p (g t) -> p g t", g=4)
            GVt = Bf(4 * 2048, BF16, r="p (b c) -> p b c", b=4)
            GU = Bf(16 * TT, BF16, r="p (k t) -> p k t", k=16)
            TMPF = [Bf(TT, F32) for _ in range(2)]
            YC = [Bf(TT, BF16) for _ in range(2)]
            dma("sp", WSTf.ap, wst_d.ap()[l].rearrange("p (g i) -> p g i", g=4), [], [WSTf.tk])
            dma("sp", GMB.ap, gmb_d.ap()[l].rearrange("p (g t) -> p g t", g=4), [], [GMB.tk])
            for g in range(4):
                tt("dve", WST.ap[:, g, :], WSTf.ap[:, g, :], MASKG.ap, ALU.mult, [WSTf.tk, MASKG.tk], [WST.tk])
            c = {"kv": 0, "pt": 0}
            for t in range(NT):
                tsl = slice(t * TT, (t + 1) * TT)
                nk = 4 * (t + 1)
                for h in range(16):
                    b = h % 2
                    dma("sp", QN[b].ap, qn_s[h][:, tsl], [], [QN[b].tk])
                    dma("sp", QR[b].ap[0:64], qr_s[h][:, tsl], [], [QR[b].tk])
                    for kb in range(t + 1):
                        kbuf, vbuf = KT[c["kv"] % 2], VB[c["kv"] % 2]
                        c["kv"] += 1
                        dma("sp", kbuf.ap, kT_s[h][:, kb * 512:(kb + 1) * 512], [], [kbuf.tk])
                        dma("sp", vbuf.ap, v_s[kb * 512:(kb + 1) * 512, h * 128:(h + 1) * 128].rearrange(
                            "(j p) e -> p j e", p=128), [], [vbuf.tk])
                        for j in range(4):
                            kt = kb * 4 + j
                            pi = kt % 2
                            mmg(ps[pi][:], [(kbuf.ap[:, j * 128:(j + 1) * 128], QN[b].ap),
                                            (KR.ap[0:64, kt * 128:(kt + 1) * 128], QR[b].ap[0:64])],
                                [kbuf.tk, QN[b].tk, KR.tk, QR[b].tk], [Tps[pi]])
                            pt = PT[c["pt"] % 3]
                            c["pt"] += 1
                            act(pt.ap, ps[pi][:], AF.Exp, [Tps[pi]], [pt.tk])
                            if kb == t:
                                tt("dve", pt.ap, pt.ap, MASKA.ap[:, j, :], ALU.mult, [pt.tk, MASKA.tk], [pt.tk])
                            P.op("pe", lambda e, b=b, pt=pt, kt=kt: e.matmul(ps[2 + b][:], lhsT=ONESB.ap, rhs=pt.ap,
                                                                             start=(kt == 0), stop=(kt == nk - 1)),
                                 [ONESB.tk, pt.tk], [Tps[2 + b]])
                            P.op("pe", lambda e, b=b, pt=pt, kt=kt, vbuf=vbuf, j=j: e.matmul(
                                ps[4 + b][:], lhsT=vbuf.ap[:, j, :], rhs=pt.ap, start=(kt == 0), stop=(kt == nk - 1)),
                                 [vbuf.tk, pt.tk], [Tps[4 + b]])
                    P.op("dve", lambda e, b=b: e.reciprocal(out=RDEN.ap, in_=ps[2 + b][:]), [Tps[2 + b]], [RDEN.tk])
                    ya = YA[b]
                    tt("dve", ya.ap, ps[4 + b][:], RDEN.ap, ALU.mult, [Tps[4 + b], RDEN.tk], [ya.tk])
                    dma("sp", ya_s[h][:, tsl], ya.ap, [ya.tk], [])
                r_t = t * TT
                for h in range(8):
                    b = h % 2
                    cdh = float(np.exp(128.0 * LOGG[h]))
                    dma("sp", RQ[b].ap, rq_s[2 * h:2 * h + 2, :, tsl].rearrange("c p t -> p c t"), [], [RQ[b].tk])
                    dma("sp", RKT[b].ap, rkT_s[2 * h:2 * h + 2, :, tsl].rearrange("c p t -> p c t"), [], [RKT[b].tk])
                    dma("sp", RG[b].ap, rg_s[2 * h:2 * h + 2, :, tsl].rearrange("c p t -> p c t"), [], [RG[b].tk])
                    dma("sp", RKM[b].ap, rktm_s[r_t:r_t + TT, h * 256:(h + 1) * 256].rearrange("(b p) d -> p b d", p=128),
                        [], [RKM[b].tk])
                    dma("sp", RVM[b].ap, rvtm_s[r_t:r_t + TT, h * 256:(h + 1) * 256].rearrange("(b p) d -> p b d", p=128),
                        [], [RVM[b].tk])
                    for blk in range(4):
                        bs = slice(blk * 128, (blk + 1) * 128)
                        mmg(ps[0][:, 0:128], [(RKT[b].ap[:, cc, bs], RQ[b].ap[:, cc, bs]) for cc in range(2)],
                            [RKT[b].tk, RQ[b].tk], [Tps[0]])
                        ptr = PTR[blk % 2]
                        tt("dve", ptr.ap, ps[0][:, 0:128], DTc.ap[:, h, :], ALU.mult, [Tps[0], DTc.tk], [ptr.tk])
                        qd = QD[blk % 2]
                        for cc in range(2):
                            tt("pool", qd.ap[:, cc, :], RQ[b].ap[:, cc, bs], QDEC.ap[:, h, :], ALU.mult,
                               [RQ[b].tk, QDEC.tk], [qd.tk])
                        for ec in range(2):
                            es_ = slice(ec * 128, (ec + 1) * 128)
                            pairs = [(RVM[b].ap[:, blk, es_], ptr.ap)] + [(SB.ap[:, h, cc, es_], qd.ap[:, cc, :])
                                                                          for cc in range(2)]
                            mmg(ps[1 + ec][:, 0:128], pairs, [RVM[b].tk, ptr.tk, SBtk[h], qd.tk], [Tps[1 + ec]])
                            act(RO.ap[:, ec, bs], ps[1 + ec][:, 0:128], AF.Copy, [Tps[1 + ec]], [RO.tk])
                        kd = KD[blk % 2]
                        ts("pool", kd.ap, RKM[b].ap[:, blk, :], KDEC.ap[:, h:h + 1], None, ALU.mult, None,
                           [RKM[b].tk, KDEC.tk], [kd.tk])
                        for cc in range(2):
                            mmg(ps[3 + cc][:, 0:256], [(kd.ap[:, cc * 128:(cc + 1) * 128], RVM[b].ap[:, blk, :])],
                                [kd.tk, RVM[b].tk], [Tps[3 + cc]])
                            stt("dve", SF.ap[:, h, cc, :], SF.ap[:, h, cc, :], cdh, ps[3 + cc][:, 0:256], ALU.mult, ALU.add,
                                [SFtk[h], Tps[3 + cc]], [SFtk[h]])
                            act(SB.ap[:, h, cc, :], SF.ap[:, h, cc, :], AF.Copy, [SFtk[h]], [SBtk[h]])
                    mmg(ps[5][:], [(ones, RO.ap[:, cc, :]) for cc in range(2)], [Tones, RO.tk], [Tps[5]])
                    for cc in range(2):
                        act(SQ2[cc].ap, RO.ap[:, cc, :], AF.Square, [RO.tk], [SQ2[cc].tk])
                    mmg(ps[6][:], [(ones, SQ2[cc].ap) for cc in range(2)], [Tones, SQ2[0].tk, SQ2[1].tk], [Tps[6]])
                    ts("dve", MEAN.ap, ps[5][:], 1.0 / 256, None, ALU.mult, None, [Tps[5]], [MEAN.tk])
                    ts("dve", MSQ.ap, ps[6][:], 1.0 / 256, None, ALU.mult, None, [Tps[6]], [MSQ.tk])
                    tt("dve", VAR.ap, MEAN.ap, MEAN.ap, ALU.mult, [MEAN.tk], [VAR.tk])
                    tt("dve", VAR.ap, MSQ.ap, VAR.ap, ALU.subtract, [MSQ.tk, VAR.tk], [VAR.tk])
                    act(RSTD.ap, VAR.ap, AF.Sqrt, [VAR.tk], [RSTD.tk], bias=1e-5, scale=1.0)
                    P.op("dve", lambda e: e.reciprocal(out=RSTD.ap, in_=RSTD.ap), [RSTD.tk], [RSTD.tk])
                    for cc in range(2):
                        tt("dve", SQ2[cc].ap, RO.ap[:, cc, :], MEAN.ap, ALU.subtract, [RO.tk, MEAN.tk], [SQ2[cc].tk])
                        tt("dve", SQ2[cc].ap, SQ2[cc].ap, RSTD.ap, ALU.mult, [SQ2[cc].tk, RSTD.tk], [SQ2[cc].tk])
                        yb = YB[cc]
                        tt("pool", yb.ap, SQ2[cc].ap, RG[b].ap[:, cc, :], ALU.mult, [SQ2[cc].tk, RG[b].tk], [yb.tk])
                        dma("sp", yb_s[2 * h + cc][:, tsl], yb.ap, [yb.tk], [])
                dma("sp", GVt.ap, gvtm_s[r_t:r_t + TT, :].rearrange("(b p) c -> p b c", p=128), [], [GVt.tk])
                dma("sp", GU.ap, gu_s[:, :, tsl].rearrange("k p t -> p k t"), [], [GU.tk])
                for cc in range(16):
                    g = cc // 4
                    pi = cc % 2

                    def f(e, cc=cc, g=g, pi=pi):
                        for blk in range(4):
                            ins = e.matmul(ps[pi][:, blk * 128:(blk + 1) * 128], lhsT=GVt.ap[:, blk, cc * 128:(cc + 1) * 128],
                                           rhs=WST.ap[:, g, :], start=True, stop=True)
                        return ins
                    P.op("pe", f, [GVt.tk, WST.tk], [Tps[pi]])
                    tf = TMPF[cc % 2]
                    tt("dve", tf.ap, ps[pi][:], GMB.ap[:, g, :], ALU.add, [Tps[pi], GMB.tk], [tf.tk])
                    yc = YC[cc % 2]
                    tt("dve", yc.ap, tf.ap, GU.ap[:, cc, :], ALU.mult, [tf.tk, GU.tk], [yc.tk])
                    dma("sp", yc_s[cc][:, tsl], yc.ap, [yc.tk], [])
            P.barrier()

        def m3_phase(l):
            ar.reset(base_off)
            X = Bf(16 * TT, F32, r="p (k t) -> p k t", k=16)
            H = Bf(16 * TT, BF16, r="p (k t) -> p k t", k=16)
            SQ = [Bf(TT, F32) for _ in range(2)]
            RS = Bf(TT, F32)
            YB3 = [Bf(16 * TT, BF16, r="p (k t) -> p k t", k=16) for _ in range(3)]
            M = Bf(16 * TT, F32, r="p (k t) -> p k t", k=16)
            MB = Bf(16 * TT, BF16, r="p (k t) -> p k t", k=16)
            W = [Bf(2048, BF16) for _ in range(4)]
            G = [Bf(TT, F32) for _ in range(2)]
            TM = [Bf(TT, F32) for _ in range(2)]
            srcs = (ya_s, yb_s, yc_s)
            c = {"w": 0, "n": 0}

            def wtile(d, idx):
                w = W[c["w"] % 4]
                c["w"] += 1
                dma("pool", w.ap, d.ap()[idx], [], [w.tk])
                return w

            for t in range(NT):
                tsl = slice(t * TT, (t + 1) * TT)
                load_xh(t, X, H, RS, SQ, l)
                for b in range(3):
                    dma("sp", YB3[b].ap, srcs[b][:, :, tsl].rearrange("k p t -> p k t"), [], [YB3[b].tk])
                for mc in range(16):
                    for b in range(3):
                        n = c["n"]
                        c["n"] += 1
                        wg = wtile(wg_d, l * 48 + b * 16 + mc)
                        wb = wtile(wbr_d, l * 48 + b * 16 + mc)
                        pg, pb = (2 * n) % 4, (2 * n + 1) % 4
                        mmg(ps[pg][:], [(wg.ap[:, kc * 128:(kc + 1) * 128], H.ap[:, kc, :]) for kc in range(16)],
                            [wg.tk, H.tk], [Tps[pg]])
                        mmg(ps[pb][:], [(wb.ap[:, kc * 128:(kc + 1) * 128], YB3[b].ap[:, kc, :]) for kc in range(16)],
                            [wb.tk, YB3[b].tk], [Tps[pb]])
                        g = G[n % 2]
                        act(g.ap, ps[pg][:], AF.Sigmoid, [Tps[pg], Tvecs], [g.tk], bias=vcol(l, V_BG, b * 16 + mc), scale=1.0)
                        if b == 0:
                            tt("dve", M.ap[:, mc, :], g.ap, ps[pb][:], ALU.mult, [g.tk, Tps[pb]], [M.tk])
                        else:
                            tm = TM[n % 2]
                            tt("dve", tm.ap, g.ap, ps[pb][:], ALU.mult, [g.tk, Tps[pb]], [tm.tk])
                            tt("dve", M.ap[:, mc, :], M.ap[:, mc, :], tm.ap, ALU.add, [M.tk, tm.tk], [M.tk])
                    act(MB.ap[:, mc, :], M.ap[:, mc, :], AF.Copy, [M.tk], [MB.tk])
                for mc in range(16):
                    w = wtile(wo2_d, l * 16 + mc)
                    pi = 4 + mc % 2
                    mmg(ps[pi][:], [(w.ap[:, kc * 128:(kc + 1) * 128], MB.ap[:, kc, :]) for kc in range(16)],
                        [w.tk, MB.tk], [Tps[pi]])
                    act(M.ap[:, mc, :], ps[pi][:], AF.Copy, [Tps[pi]], [M.tk])
                rms_stats(M.ap, M.tk, [s.ap for s in SQ], [s.tk for s in SQ], ps[6], Tps[6], RS.ap, RS.tk, 16, D)
                for mc in range(16):
                    tm = TM[mc % 2]
                    stt("dve", tm.ap, M.ap[:, mc, :], vcol(l, V_MIXPOST, mc), RS.ap, ALU.mult, ALU.mult,
                        [M.tk, RS.tk, Tvecs], [tm.tk])
                    tt("dve", X.ap[:, mc, :], X.ap[:, mc, :], tm.ap, ALU.add, [X.tk, tm.tk], [X.tk])
                dma("sp", yT_v[:, :, tsl], X.ap, [X.tk], TY[t])
            P.barrier()

        if "mix" in stages:
            rope_tables_phase()

        for l in range(L):
            if "f1" in stages:
                ffn_phase(l, f1wi, f1wo, V_F1PRE, V_F1POST)
            if "mix" in stages:
                m1a_phase(l)
                m1b_phase(l)
                m2_phase(l)
                m3_phase(l)
            if "f2" in stages:
                ffn_phase(l, f2wi, f2wo, V_F2PRE, V_F2POST)
        P.barrier(["sp"])

        sems = {k: es.enter_context(nc.semaphore(k)) for k in P.cnt.keys()}
        with nc.Block() as block:
            @block.tensor
            def _(e):
                P.replay("pe", e, sems)

            @block.scalar
            def _(e):
                P.replay("act", e, sems)

            @block.vector
            def _(e):
                P.replay("dve", e, sems)

            @block.gpsimd
            def _(e):
                P.replay("pool", e, sems)

            @block.sync
            def _(e):
                P.replay("sp", e, sems)
    return nc


def prep_shared(inp, L):
    out = {}
    vec = np.zeros((128, NV_L * L), np.float32)

    def put(l, base, v):
        n = v.shape[0] // 128
        vec[:, l * NV_L + base:l * NV_L + base + n] = v.reshape(n, 128).T

    for l in range(L):
        put(l, V_F1PRE, inp["ffn1_pre_g"][l])
        put(l, V_F1POST, inp["ffn1_post_g"][l])
        put(l, V_MIXPRE, inp["mix_pre_g"][l])
        put(l, V_QN, inp["q_norm_g"][l])
        put(l, V_KVN, inp["kv_norm_g"][l])
        put(l, V_BG, inp["b_gate"][l])
        put(l, V_MIXPOST, inp["mix_post_g"][l])
        put(l, V_F2PRE, inp["ffn2_pre_g"][l])
        put(l, V_F2POST, inp["ffn2_post_g"][l])
    out["vecs"] = vec

    def wi_tiles(w):
        r = np.empty((L, NHC, 128, 16, 2, 128), np.float32)
        for l in range(L):
            r[l] = w[l].reshape(16, 128, 2, NHC, 128).transpose(3, 1, 0, 2, 4)
        return r.reshape(L * NHC, 128, 4096)

    def wo_tiles(w):
        r = np.empty((L, 16, 128, NHC, 128), np.float32)
        for l in range(L):
            r[l] = w[l].reshape(NHC, 128, 16, 128).transpose(2, 1, 0, 3)
        return r.reshape(L * 16, 128, DFF)

    out["f1wi"] = wi_tiles(inp["ffn1_wi"])
    out["f1wo"] = wo_tiles(inp["ffn1_wo"])
    out["f2wi"] = wi_tiles(inp["ffn2_wi"])
    out["f2wo"] = wo_tiles(inp["ffn2_wo"])
    return out


def tiles128(W):
    K, M = W.shape
    return np.ascontiguousarray(W.reshape(K // 128, 128, M // 128, 128).transpose(2, 1, 0, 3)).reshape(M // 128, 128, K)


def make_consts():
    c = np.zeros((128, NC_TOT), np.float32)
    kp = np.arange(128)[:, None]
    qf = np.arange(512)[None, :]
    for j in range(4):
        c[:, C_MASKA + j * 512:C_MASKA + (j + 1) * 512] = (((j * 128 + kp) // 64) <= (qf // 64)).astype(np.float32)
    i = np.arange(128)[None, :]
    jj = np.arange(128)[:, None]
    for h in range(8):
        lg = np.log1p(-(2.0 ** (-5.0 - h)))
        c[:, C_DT + h * 128:C_DT + (h + 1) * 128] = np.exp(np.abs(i - jj) * lg) * ((i // 64) >= (jj // 64))
        c[:, C_QDEC + h * 128:C_QDEC + (h + 1) * 128] = np.exp((i + 1.0) * lg)
        c[:, C_KDEC + h] = np.exp((127.0 - np.arange(128)) * lg)
    c[:, C_MASKG:C_MASKG + 128] = ((i // 64) >= (jj // 64)).astype(np.float32)
    inv_k = (np.float32(10000.0) ** (-np.arange(0, 256, 2, dtype=np.float32) / np.float32(256))).astype(np.float32)
    inv_r = (np.float32(10000.0) ** (-np.arange(0, 64, 2, dtype=np.float32) / np.float32(64))).astype(np.float32)
    c[:, C_INV] = inv_k
    c[:, C_INV + 1] = np.tile(inv_r, 4)
    c[:, C_INV + 2] = np.where((np.arange(128) % 64) < 32, -1.0, 1.0)
    c[:, C_INVROW:C_INVROW + 128] = inv_k[None, :]
    return c


def prep_mixer(inp, L):
    out = {"consts": make_consts()}
    SPL = np.cumsum([0, 512, 512, 64, 2048, 2048, 2048, 2048, 2048, 2048])
    winfm = np.empty((L, 73, 128, 2048), np.float32)
    wintm = np.empty((L, 12, 128, 16 * 512), np.float32)
    wuq = np.empty((L, 128, 4, 16, 256), np.float32)
    wukv = np.empty((L, 128, 4 * 4096), np.float32)
    swap = (np.arange(64) + 32) % 64
    for l in range(L):
        w = inp["w_in"][l]
        winfm[l, 0:4] = tiles128(w[:, SPL[0]:SPL[1]])
        winfm[l, 4:8] = tiles128(w[:, SPL[1]:SPL[2]])
        kr = w[:, SPL[2]:SPL[3]]
        winfm[l, 8:9] = tiles128(np.concatenate([kr, kr[:, swap]], 1))
        winfm[l, 9:25] = tiles128(w[:, SPL[3]:SPL[4]])
        winfm[l, 25:41] = tiles128(w[:, SPL[4]:SPL[5]])
        winfm[l, 41:57] = tiles128(w[:, SPL[6]:SPL[7]])
        winfm[l, 57:73] = tiles128(w[:, SPL[7]:SPL[8]])
        for i, (lo) in enumerate([SPL[4], SPL[5], SPL[8]]):
            blk = w[:, lo:lo + 2048].reshape(16, 128, 4, 512).transpose(2, 1, 0, 3)
            wintm[l, i * 4:(i + 1) * 4] = blk.reshape(4, 128, 16 * 512)
        q = inp["w_uq"][l].reshape(4, 128, 16, 192).transpose(1, 0, 2, 3)
        wuq[l, :, :, :, 0:192] = q
        wuq[l, :, :, :, 192:256] = q[:, :, :, 128 + swap]
        wukv[l] = inp["w_ukv"][l].reshape(4, 128, 4096).transpose(1, 0, 2).reshape(128, 4 * 4096)
    out["winfm"] = winfm.reshape(L * 73, 128, 2048)
    out["wintm"] = wintm.reshape(L * 12, 128, 16 * 512)
    out["wuq"] = wuq.reshape(L, 128, 4 * 16 * 256)
    out["wukv"] = wukv
    out["wg"] = np.concatenate([tiles128(inp["w_gate"][l]) for l in range(L)], 0)
    out["wbr"] = np.concatenate([tiles128(inp["w_br"][l, b]) for l in range(L) for b in range(3)], 0)
    out["wo2"] = np.concatenate([tiles128(inp["w_o"][l]) for l in range(L)], 0)
    out["lng"] = np.ascontiguousarray(np.broadcast_to(inp["gm_ln_g"][:, None, :], (L, 128, 2048)))
    out["lnb"] = np.ascontiguousarray(np.broadcast_to(inp["gm_ln_b"][:, None, :], (L, 128, 2048)))
    out["wst"] = np.ascontiguousarray(inp["gm_w_s"].transpose(0, 3, 1, 2)).reshape(L, 128, 512)
    gb = np.broadcast_to(inp["gm_b_s"][:, None, :, None, :], (L, 128, 4, 4, 128))
    out["gmb"] = np.ascontiguousarray(gb).reshape(L, 128, 2048)
    return out


def run(inp, S, L, n_seq, stages=("f1", "mix", "f2")):
    nc = build_program(S, L, stages)
    shared = prep_shared(inp, L)
    if "mix" in stages:
        shared.update(prep_mixer(inp, L))
    in_maps = []
    for b in range(n_seq):
        m = dict(shared)
        m["xT"] = np.ascontiguousarray(inp["x"][b, :S].T)
        if "mix" in stages:
            p = np.asarray(inp["pos"][b, :S]).astype(np.int32)
            m["posb"] = np.ascontiguousarray(np.broadcast_to(p[None, :], (128, S)))
            m["post"] = np.ascontiguousarray(p.reshape(S // 128, 128).T)
        in_maps.append(m)
    res = run_bass_kernel_spmd(nc, in_maps, core_ids=list(range(n_seq)))
    return np.stack([np.ascontiguousarray(r["yT"].T) for r in res.results], 0)


def kernel(**inputs):
    inp = {k: np.asarray(v) for k, v in inputs.items()}
    B, S, _ = inp["x"].shape
    L = inp["ffn1_pre_g"].shape[0]
    return run(inp, S, L, B).astype(np.float32)
```

```python
import numpy as np
from contextlib import ExitStack
import concourse.bass as bass
import concourse.mybir as mybir
from concourse.bass_utils import run_bass_kernel_spmd

F32, BF16, I32 = mybir.dt.float32, mybir.dt.bfloat16, mybir.dt.int32
ALU = mybir.AluOpType
AF = mybir.ActivationFunctionType

D = 2048
DFF = 5504
NHC = 43
TT = 512
EPS = 1e-6
NV_L = 152
C_MASKA, C_DT, C_QDEC, C_KDEC, C_MASKG, C_INV, C_INVROW, NC_TOT = 0, 2048, 3072, 4096, 4104, 4232, 4240, 4368
V_F1PRE, V_F1POST, V_MIXPRE, V_QN, V_KVN, V_BG, V_MIXPOST, V_F2PRE, V_F2POST = 0, 16, 32, 48, 52, 56, 104, 120, 136


class Tk:
    __slots__ = ("name", "w", "r")

    def __init__(self, name):
        self.name = name
        self.w = None
        self.r = {}


class Prog:
    ENG = ("pe", "act", "dve", "pool", "sp")

    def __init__(self, ndma=12):
        self.ops = {e: [] for e in self.ENG}
        self.cnt = {}
        self.seen = {e: {} for e in self.ENG}
        self.rr = {e: 0 for e in self.ENG}
        self.ndma = ndma

    def op(self, eng, fn, reads=(), writes=(), dma=False):
        need = {}

        def add(k, v):
            if need.get(k, 0) < v:
                need[k] = v

        for t in reads:
            if t.w is not None:
                add(*t.w)
        for t in writes:
            if t.w is not None:
                add(*t.w)
            for k, v in t.r.items():
                add(k, v)
        own = "c_" + eng
        waits = []
        seen = self.seen[eng]
        for k, v in need.items():
            if k == own and eng == "pe":
                continue
            if seen.get(k, 0) >= v:
                continue
            seen[k] = v
            waits.append((k, v))
        if dma:
            j = self.rr[eng]
            self.rr[eng] = (j + 1) % self.ndma
            k = "d_%s_%d" % (eng, j)
            prev = self.cnt.get(k, 0)
            if prev and seen.get(k, 0) < prev:
                seen[k] = prev
                waits.append((k, prev))
            self.cnt[k] = prev + 16
            tok = (k, prev + 16)
            inc = 16
        else:
            k = own
            self.cnt[k] = self.cnt.get(k, 0) + 1
            tok = (k, self.cnt[k])
            inc = 1
        self.ops[eng].append((waits, fn, k, inc))
        for t in writes:
            t.w = tok
            t.r = {}
        for t in reads:
            if t.r.get(k, 0) < tok[1]:
                t.r[k] = tok[1]
        return tok

    def barrier(self, engines=None):
        for e in (engines or self.ENG):
            waits = []
            seen = self.seen[e]
            for k, v in self.cnt.items():
                if e == "pe" and k == "c_pe":
                    continue
                if seen.get(k, 0) < v:
                    seen[k] = v
                    waits.append((k, v))
            if waits:
                self.ops[e].append((waits, None, None, 0))

    def replay(self, eng, e, sems):
        for waits, fn, k, inc in self.ops[eng]:
            for wk, wv in waits:
                e.wait_ge(sems[wk], wv)
            if fn is not None:
                ins = fn(e)
                ins.then_inc(sems[k], inc)


class Arena:
    def __init__(self, tensor, nbytes):
        self.t32 = tensor
        self.t16 = tensor.bitcast(BF16)
        self.off = 0
        self.nbytes = nbytes

    def reset(self, off=0):
        self.off = off

    def alloc(self, cols, dt, parts=128):
        esz = 4 if dt in (F32, I32) else 2
        nb = (cols * esz + 31) // 32 * 32
        o = self.off
        assert o + nb <= self.nbytes, ("arena overflow", o, nb, self.nbytes)
        self.off = o + nb
        if dt == F32:
            return self.t32[0:parts, o // 4:o // 4 + cols]
        if dt == I32:
            return self.t32.bitcast(I32)[0:parts, o // 4:o // 4 + cols]
        return self.t16[0:parts, o // 2:o // 2 + cols]


def build_program(S, L, stages=("f1", "mix", "f2"), dbg=False):
    NT = S // TT
    nc = bass.Bass("TRN2", target_bir_lowering=False)
    P = Prog()
    di = {}

    def dram_in(name, shape, dt=F32):
        di[name] = nc.dram_tensor(name, list(shape), dt, kind="ExternalInput")
        return di[name]

    xT = dram_in("xT", [D, S])
    vecs_d = dram_in("vecs", [128, NV_L * L])
    f1wi = dram_in("f1wi", [L * NHC, 128, 16 * 256])
    f1wo = dram_in("f1wo", [L * 16, 128, DFF])
    f2wi = dram_in("f2wi", [L * NHC, 128, 16 * 256])
    f2wo = dram_in("f2wo", [L * 16, 128, DFF])
    yT = nc.dram_tensor("yT", [D, S], F32, kind="ExternalOutput")
    NBLK = S // 128
    if "mix" in stages:
        consts_d = dram_in("consts", [128, NC_TOT])
        posb_d = dram_in("posb", [128, S], I32)
        post_d = dram_in("post", [128, NBLK], I32)
        winfm = dram_in("winfm", [L * 73, 128, 2048])
        wintm = dram_in("wintm", [L * 12, 128, 16 * 512])
        wuq_d = dram_in("wuq", [L, 128, 4 * 16 * 256])
        wukv_d = dram_in("wukv", [L, 128, 4 * 4096])
        wg_d = dram_in("wg", [L * 48, 128, 2048])
        wbr_d = dram_in("wbr", [L * 48, 128, 2048])
        wo2_d = dram_in("wo2", [L * 16, 128, 2048])
        lng_d = dram_in("lng", [L, 128, 2048])
        lnb_d = dram_in("lnb", [L, 128, 2048])
        wst_d = dram_in("wst", [L, 128, 512])
        gmb_d = dram_in("gmb", [L, 128, 2048])

        def scr(name, shape, dt=BF16):
            return nc.dram_tensor(name, list(shape), dt).ap()
        cosk_s, sink_s = scr("cosk_s", [128, S], F32), scr("sink_s", [128, S], F32)
        c64_s, s64_s = scr("c64_s", [64, S], F32), scr("s64_s", [64, S], F32)
        costm_s, sintm_s = scr("costm_s", [S, 128], F32), scr("sintm_s", [S, 128], F32)
        qn_s, qr_s, kT_s = scr("qn_s", [16, 128, S]), scr("qr_s", [16, 64, S]), scr("kT_s", [16, 128, S])
        kr_s = scr("kr_s", [64, S])
        v_s = scr("v_s", [S, 2048])
        rq_s, rkT_s = scr("rq_s", [16, 128, S]), scr("rkT_s", [16, 128, S])
        rktm_s, rvtm_s, gvtm_s = scr("rktm_s", [S, 2048]), scr("rvtm_s", [S, 2048]), scr("gvtm_s", [S, 2048])
        rg_s, gu_s = scr("rg_s", [16, 128, S]), scr("gu_s", [16, 128, S])
        ya_s, yb_s, yc_s = scr("ya_s", [16, 128, S]), scr("yb_s", [16, 128, S]), scr("yc_s", [16, 128, S])

    xT_v = xT.ap().rearrange("(kc p) s -> p kc s", p=128)
    yT_v = yT.ap().rearrange("(kc p) s -> p kc s", p=128)

    ARENA_BYTES = 190 * 1024
    with ExitStack() as es:
        arena_t = es.enter_context(nc.sbuf_tensor("arena", [128, ARENA_BYTES // 4], F32))
        ar = Arena(arena_t, ARENA_BYTES)
        ps = [es.enter_context(nc.psum_tensor("ps%d" % i, [128, 512], F32)) for i in range(8)]
        Tps = [Tk("ps%d" % i) for i in range(8)]

        ones = ar.alloc(128, F32)
        Tones = Tk("ones")
        vecs = ar.alloc(NV_L * L, F32)
        Tvecs = Tk("vecs")
        base_off = ar.off

        P.op("pool", lambda e: e.memset(ones, 1.0), writes=[Tones])
        P.op("sp", lambda e: e.dma_start(out=vecs, in_=vecs_d.ap()), writes=[Tvecs], dma=True)

        TY = [[Tk("y%d_%d" % (t, k)) for k in range(16)] for t in range(NT)]
        state = {"first": False}
        for t in range(NT):
            P.op("sp", lambda e, t=t: e.dma_start(out=yT.ap()[:, t * TT:(t + 1) * TT], in_=xT.ap()[:, t * TT:(t + 1) * TT]),
                 writes=TY[t], dma=True)

        def vcol(l, base, kc):
            c = l * NV_L + base + kc
            return vecs[:, c:c + 1]

        def ffn_phase(l, wi_d, wo_d, vpre, vpost):
            ar.reset(base_off)
            O = Bf(16 * TT, F32, r="p (k t) -> p k t", k=16)
            HH = [Bf(16 * TT, BF16, r="p (k t) -> p k t", k=16) for _ in range(2)]
            A = Bf(NHC * TT, BF16, r="p (k t) -> p k t", k=NHC)
            XC = [Bf(TT, F32) for _ in range(4)]
            XR = [Bf(TT, F32) for _ in range(4)]
            RS2 = Bf(TT, F32)
            SQ = [Bf(TT, F32) for _ in range(2)]
            SA = [Bf(TT, F32) for _ in range(2)]
            RS = Bf(TT, F32)
            TMP = [Bf(TT, F32) for _ in range(2)]
            WA = [Bf(4096, BF16) for _ in range(3)]
            WB = [Bf(DFF, BF16) for _ in range(2)]
            c = {"a": 0, "b": 0}
            load_h_chunked(0, HH[0], RS, SQ, XC, l, vpre)
            for t in range(NT):
                tsl = slice(t * TT, (t + 1) * TT)
                H = HH[t % 2]
                for hc in range(NHC):
                    wa = WA[c["a"] % 3]
                    c["a"] += 1
                    dma("pool", wa.ap, wi_d.ap()[l * NHC + hc], [], [wa.tk])
                    ia, ib = (hc % 2) * 2, (hc % 2) * 2 + 1

                    def mm(e, wa=wa, ia=ia, ib=ib, H=H):
                        for kc in range(16):
                            e.matmul(ps[ia][:], lhsT=wa.ap[:, kc * 256:kc * 256 + 128], rhs=H.ap[:, kc, :],
                                     start=(kc == 0), stop=(kc == 15))
                        for kc in range(16):
                            ins = e.matmul(ps[ib][:], lhsT=wa.ap[:, kc * 256 + 128:kc * 256 + 256], rhs=H.ap[:, kc, :],
                                           start=(kc == 0), stop=(kc == 15))
                        return ins
                    P.op("pe", mm, [wa.tk, H.tk], [Tps[ia], Tps[ib]])
                    sa = SA[hc % 2]
                    act(sa.ap, ps[ia][:], AF.Silu, [Tps[ia]], [sa.tk])
                    tt("dve", A.ap[:, hc, :], sa.ap, ps[ib][:], ALU.mult, [sa.tk, Tps[ib]], [A.tk])
                if t + 1 < NT:
                    load_h_chunked(t + 1, HH[(t + 1) % 2], RS, SQ, XC, l, vpre)
                for mc in range(16):
                    wb = WB[c["b"] % 2]
                    c["b"] += 1
                    dma("pool", wb.ap, wo_d.ap()[l * 16 + mc], [], [wb.tk])
                    io = 4 + mc % 2
                    mmg(ps[io][:], [(wb.ap[:, kc * 128:(kc + 1) * 128], A.ap[:, kc, :]) for kc in range(NHC)],
                        [wb.tk, A.tk], [Tps[io]])
                    act(O.ap[:, mc, :], ps[io][:], AF.Copy, [Tps[io]], [O.tk])
                    if mc > 0:
                        stat_mm(SQ, mc - 1, ps[7], Tps[7])
                    stat_square(O.ap[:, mc, :], O.tk, SQ, mc)
                stat_mm(SQ, 15, ps[7], Tps[7])
                stat_finish(ps[7], Tps[7], RS2, D)
                for mc in range(16):
                    xc = XR[mc % 4]
                    dma("sp", xc.ap, yT_v[:, mc, tsl], [TY[t][mc]], [xc.tk])
                    tm = TMP[mc % 2]
                    stt("dve", tm.ap, O.ap[:, mc, :], vcol(l, vpost, mc), RS2.ap, ALU.mult, ALU.mult,
                        [O.tk, RS2.tk, Tvecs], [tm.tk])
                    stt("dve", xc.ap, tm.ap, 0.5, xc.ap, ALU.mult, ALU.add, [tm.tk, xc.tk], [xc.tk])
                    dma("sp", yT_v[:, mc, tsl], xc.ap, [xc.tk], [TY[t][mc]])
            P.barrier()

        def rms_stats(X, TX, SQ, TSQ, pss, Tpss, RS, TRS, nchunk, dim):
            for kc in range(nchunk):
                s = kc % 2
                P.op("act", lambda e, kc=kc, s=s: e.activation(out=SQ[s], in_=X[:, kc, :], func=AF.Square),
                     reads=[TX], writes=[TSQ[s]])
                P.op("pe", lambda e, kc=kc, s=s: e.matmul(pss[:], lhsT=ones, rhs=SQ[s], start=(kc == 0),
                                                          stop=(kc == nchunk - 1)),
                     reads=[TSQ[s], Tones], writes=[Tpss])
            P.op("act", lambda e: e.activation(out=RS, in_=pss[:], func=AF.Sqrt, bias=EPS, scale=1.0 / dim),
                 reads=[Tpss], writes=[TRS])
            P.op("dve", lambda e: e.reciprocal(out=RS, in_=RS), reads=[TRS], writes=[TRS])


        def dma(eng, out, in_, reads=(), writes=()):
            P.op(eng, lambda e: e.dma_start(out=out, in_=in_), reads, writes, dma=True)

        def act(out, in_, func, reads, writes, **kw):
            P.op("act", lambda e: e.activation(out=out, in_=in_, func=func, **kw), reads, writes)

        def tt(eng, out, a, b, op, reads, writes):
            P.op(eng, lambda e: e.tensor_tensor(out=out, in0=a, in1=b, op=op), reads, writes)

        def ts(eng, out, a, s1, s2, op0, op1, reads, writes):
            if s2 is None:
                P.op(eng, lambda e: e.tensor_scalar(out=out, in0=a, scalar1=s1, scalar2=None, op0=op0), reads, writes)
            else:
                P.op(eng, lambda e: e.tensor_scalar(out=out, in0=a, scalar1=s1, scalar2=s2, op0=op0, op1=op1), reads, writes)

        def stt(eng, out, a, s, b, op0, op1, reads, writes):
            P.op(eng, lambda e: e.scalar_tensor_tensor(out=out, in0=a, scalar=s, in1=b, op0=op0, op1=op1), reads, writes)

        def mmg(out, pairs, reads, writes):
            n = len(pairs)

            def f(e):
                for i, (a, b) in enumerate(pairs):
                    ins = e.matmul(out, lhsT=a, rhs=b, start=(i == 0), stop=(i == n - 1))
                return ins
            P.op("pe", f, reads, writes)

        class Bf:
            def __init__(self, cols, dt, parts=128, r=None, **kw):
                self.ap = ar.alloc(cols, dt, parts)
                if r is not None:
                    self.ap = self.ap.rearrange(r, **kw)
                self.tk = Tk("b")

        def stat_square(src_ap, src_tk, SQ, kc):
            s = SQ[kc % 2]
            act(s.ap, src_ap, AF.Square, [src_tk], [s.tk])

        def stat_mm(SQ, kc, pss, Tpss, n=16):
            s = SQ[kc % 2]
            P.op("pe", lambda e, s=s, kc=kc: e.matmul(pss[:], lhsT=ones, rhs=s.ap, start=(kc == 0), stop=(kc == n - 1)),
                 [s.tk, Tones], [Tpss])

        def stat_finish(pss, Tpss, RS, dim):
            act(RS.ap, pss[:], AF.Sqrt, [Tpss], [RS.tk], bias=EPS, scale=1.0 / dim)
            P.op("dve", lambda e: e.reciprocal(out=RS.ap, in_=RS.ap), [RS.tk], [RS.tk])

        def load_h_chunked(t, H, RS, SQ, XC, l, vbase):
            tsl = slice(t * TT, (t + 1) * TT)
            n = len(XC)
            for kc in range(16):
                xc = XC[kc % n]
                dma("sp", xc.ap, yT_v[:, kc, tsl], [TY[t][kc]], [xc.tk])
                s = SQ[kc % 2]
                act(s.ap, xc.ap, AF.Square, [xc.tk], [s.tk])
                P.op("pe", lambda e, kc=kc, s=s: e.matmul(ps[6][:], lhsT=ones, rhs=s.ap, start=(kc == 0), stop=(kc == 15)),
                     [s.tk, Tones], [Tps[6]])
            act(RS.ap, ps[6][:], AF.Sqrt, [Tps[6]], [RS.tk], bias=EPS, scale=1.0 / D)
            P.op("dve", lambda e: e.reciprocal(out=RS.ap, in_=RS.ap), [RS.tk], [RS.tk])
            for kc in range(16):
                xc = XC[(16 + kc) % n]
                dma("sp", xc.ap, yT_v[:, kc, tsl], [TY[t][kc]], [xc.tk])
                stt("dve", H.ap[:, kc, :], xc.ap, vcol(l, vbase, kc), RS.ap, ALU.mult, ALU.mult,
                    [xc.tk, RS.tk, Tvecs], [H.tk])

        def load_xh(t, X, H, RS, SQ, l, first_src=False):
            tsl = slice(t * TT, (t + 1) * TT)
            dma("sp", X.ap, yT_v[:, :, tsl], TY[t], [X.tk])
            rms_stats(X.ap, X.tk, [s.ap for s in SQ], [s.tk for s in SQ], ps[6], Tps[6], RS.ap, RS.tk, 16, D)
            for kc in range(16):
                stt("dve", H.ap[:, kc, :], X.ap[:, kc, :], vcol(l, V_MIXPRE, kc), RS.ap, ALU.mult, ALU.mult,
                    [X.tk, RS.tk, Tvecs], [H.tk])

        def gelu_tanh(eng_sets, out, xin, tmp1, tmp2, reads, writes, shape_tk):
            t1, t2 = tmp1, tmp2
            tt("dve", t1.ap, xin, xin, ALU.mult, reads, [t1.tk])
            ts("dve", t1.ap, t1.ap, 0.044715, 1.0, ALU.mult, ALU.add, [t1.tk], [t1.tk])
            tt("dve", t1.ap, t1.ap, xin, ALU.mult, reads + [t1.tk], [t1.tk])
            act(t2.ap, t1.ap, AF.Sigmoid, [t1.tk], [t2.tk], scale=1.5957691216)
            tt("dve", out, t2.ap, xin, ALU.mult, reads + [t2.tk], writes)

        def rope_tables_phase():
            ar.reset(base_off)
            CI = Bf(8, F32)
            IR = Bf(128, F32)
            dma("sp", CI.ap, consts_d.ap()[:, C_INV:C_INV + 8], [], [CI.tk])
            dma("sp", IR.ap, consts_d.ap()[:, C_INVROW:C_INVROW + 128], [], [IR.tk])
            PI_ = Bf(TT, I32)
            PF = Bf(TT, F32)
            ANG = Bf(TT, F32)
            KI = Bf(TT, I32)
            KF, RR, NR = Bf(TT, F32), Bf(TT, F32), Bf(TT, F32)
            OUT = [Bf(TT, F32) for _ in range(2)]
            HALF_PI = float(np.pi / 2)
            C1 = 6.28125
            C2 = float(2 * np.pi - 6.28125)

            def sincos(ang, n, w, osin, ocos):
                kf, rr, nr, ki = KF.ap[0:n, 0:w], RR.ap[0:n, 0:w], NR.ap[0:n, 0:w], KI.ap[0:n, 0:w]
                ts("dve", kf, ang, float(1.0 / (2 * np.pi)), None, ALU.mult, None, [ANG.tk], [KF.tk])
                P.op("dve", lambda e: e.tensor_copy(out=ki, in_=kf), [KF.tk], [KI.tk])
                P.op("dve", lambda e: e.tensor_copy(out=kf, in_=ki), [KI.tk], [KF.tk])
                stt("dve", rr, kf, -C1, ang, ALU.mult, ALU.add, [KF.tk, ANG.tk], [RR.tk])
                stt("dve", rr, kf, -C2, rr, ALU.mult, ALU.add, [KF.tk, RR.tk], [RR.tk])
                ts("dve", rr, rr, -3.14159, 3.14159, ALU.max, ALU.min, [RR.tk], [RR.tk])
                act(osin.ap[0:n, 0:w], rr, AF.Sin, [RR.tk], [osin.tk])
                ts("dve", nr, rr, -1.0, None, ALU.mult, None, [RR.tk], [NR.tk])
                tt("dve", nr, nr, rr, ALU.max, [NR.tk, RR.tk], [NR.tk])
                ts("dve", nr, nr, -1.0, HALF_PI, ALU.mult, ALU.add, [NR.tk], [NR.tk])
                act(ocos.ap[0:n, 0:w], nr, AF.Sin, [NR.tk], [ocos.tk])

            for t in range(NT):
                tsl = slice(t * TT, (t + 1) * TT)
                dma("sp", PI_.ap, posb_d.ap()[:, tsl], [], [PI_.tk])
                P.op("dve", lambda e: e.tensor_copy(out=PF.ap, in_=PI_.ap), [PI_.tk], [PF.tk])
                for which in range(2):
                    n = 128 if which == 0 else 64
                    ts("dve", ANG.ap[0:n], PF.ap[0:n], CI.ap[0:n, which:which + 1], None, ALU.mult, None,
                       [PF.tk, CI.tk], [ANG.tk])
                    sincos(ANG.ap[0:n], n, TT, OUT[0], OUT[1])
                    if which == 0:
                        dma("sp", sink_s[:, tsl], OUT[0].ap, [OUT[0].tk], [])
                        dma("sp", cosk_s[:, tsl], OUT[1].ap, [OUT[1].tk], [])
                    else:
                        ts("dve", OUT[0].ap[0:64], OUT[0].ap[0:64], CI.ap[0:64, 2:3], None, ALU.mult, None,
                           [OUT[0].tk, CI.tk], [OUT[0].tk])
                        dma("sp", s64_s[:, tsl], OUT[0].ap[0:64], [OUT[0].tk], [])
                        dma("sp", c64_s[:, tsl], OUT[1].ap[0:64], [OUT[1].tk], [])
            PT_ = Bf(NBLK, I32)
            PTF = Bf(NBLK, F32)
            dma("sp", PT_.ap, post_d.ap(), [], [PT_.tk])
            P.op("dve", lambda e: e.tensor_copy(out=PTF.ap, in_=PT_.ap), [PT_.tk], [PTF.tk])
            for n in range(NBLK):
                ts("dve", ANG.ap[:, 0:128], IR.ap, PTF.ap[:, n:n + 1], None, ALU.mult, None, [IR.tk, PTF.tk], [ANG.tk])
                sincos(ANG.ap[:, 0:128], 128, 128, OUT[0], OUT[1])
                dma("sp", sintm_s[n * 128:(n + 1) * 128, :], OUT[0].ap[:, 0:128], [OUT[0].tk], [])
                dma("sp", costm_s[n * 128:(n + 1) * 128, :], OUT[1].ap[:, 0:128], [OUT[1].tk], [])
            P.barrier()

        def m1a_phase(l):
            ar.reset(base_off)
            XC = [Bf(TT, F32) for _ in range(4)]
            H = Bf(16 * TT, BF16, r="p (k t) -> p k t", k=16)
            WUQ = Bf(4 * 16 * 256, BF16, r="p (k h c) -> p k h c", k=4, h=16)
            WUKV = Bf(4 * 4096, BF16, r="p (k h c) -> p k h c", k=4, h=16)
            W = [Bf(2048, BF16) for _ in range(8)]
            ZQ = Bf(4 * TT, F32, r="p (k t) -> p k t", k=4)
            ZKV = Bf(4 * TT, F32, r="p (k t) -> p k t", k=4)
            ZQN = Bf(4 * TT, BF16, r="p (k t) -> p k t", k=4)
            ZKVN = Bf(4 * TT, BF16, r="p (k t) -> p k t", k=4)
            SQ = [Bf(TT, F32) for _ in range(2)]
            RS = Bf(TT, F32)
            CK, SK, C64, S64 = Bf(TT, F32), Bf(TT, F32), Bf(TT, F32), Bf(TT, F32)
            R0, R1 = [Bf(TT, F32) for _ in range(2)], [Bf(TT, F32) for _ in range(2)]
            T0, T1, T2, T3 = Bf(TT, F32), Bf(TT, F32), Bf(TT, F32), Bf(TT, F32)
            ST = [Bf(TT, BF16) for _ in range(4)]
            VT = [Bf(2048, BF16) for _ in range(1)]
            dma("pool", WUQ.ap, wuq_d.ap()[l].rearrange("p (k h c) -> p k h c", k=4, h=16), [], [WUQ.tk])
            dma("pool", WUKV.ap, wukv_d.ap()[l].rearrange("p (k h c) -> p k h c", k=4, h=16), [], [WUKV.tk])
            cnt = {"w": 0, "st": 0, "ps": 0}

            def wtile(idx):
                w = W[cnt["w"] % 8]
                cnt["w"] += 1
                dma("pool", w.ap, winfm.ap()[l * 73 + idx], [], [w.tk])
                return w

            def stage():
                s = ST[cnt["st"] % 4]
                cnt["st"] += 1
                return s

            def proj(w, lo, hi, pi, npart=128):
                mmg(ps[pi][0:npart, :], [(w.ap[:, kc * 128 + lo:kc * 128 + hi], H.ap[:, kc, :]) for kc in range(16)],
                    [w.tk, H.tk], [Tps[pi]])

            for t in range(NT):
                tsl = slice(t * TT, (t + 1) * TT)
                load_h_chunked(t, H, RS, SQ, XC, l, V_MIXPRE)
                dma("sp", CK.ap, cosk_s[:, tsl], [], [CK.tk])
                dma("sp", SK.ap, sink_s[:, tsl], [], [SK.tk])
                dma("sp", C64.ap[0:64], c64_s[:, tsl], [], [C64.tk])
                dma("sp", S64.ap[0:64], s64_s[:, tsl], [], [S64.tk])
                for i in range(8):
                    w = wtile(i)
                    pi = i % 2
                    proj(w, 0, 128, pi)
                    dst = ZQ if i < 4 else ZKV
                    act(dst.ap[:, i % 4, :], ps[pi][:], AF.Copy, [Tps[pi]], [dst.tk])
                for (Z, ZN, vb) in ((ZQ, ZQN, V_QN), (ZKV, ZKVN, V_KVN)):
                    rms_stats(Z.ap, Z.tk, [s.ap for s in SQ], [s.tk for s in SQ], ps[6], Tps[6], RS.ap, RS.tk, 4, 512)
                    for kc in range(4):
                        stt("dve", ZN.ap[:, kc, :], Z.ap[:, kc, :], vcol(l, vb, kc), RS.ap, ALU.mult, ALU.mult,
                            [Z.tk, RS.tk, Tvecs], [ZN.tk])

                def rope64(pa, pb, scale, out_ap, out_tk):
                    act(R0[0].ap[0:64], ps[pa][0:64, :], AF.Copy, [Tps[pa]], [R0[0].tk], scale=scale)
                    act(R1[0].ap[0:64], ps[pb][0:64, :], AF.Copy, [Tps[pb]], [R1[0].tk], scale=scale)
                    tt("dve", T0.ap[0:64], R0[0].ap[0:64], C64.ap[0:64], ALU.mult, [R0[0].tk, C64.tk], [T0.tk])
                    tt("dve", T1.ap[0:64], R1[0].ap[0:64], S64.ap[0:64], ALU.mult, [R1[0].tk, S64.tk], [T1.tk])
                    tt("dve", out_ap, T0.ap[0:64], T1.ap[0:64], ALU.add, [T0.tk, T1.tk], [out_tk])

                w = wtile(8)
                proj(w, 0, 64, 0, 64)
                proj(w, 64, 128, 1, 64)
                s = stage()
                rope64(0, 1, 1.0, s.ap[0:64], s.tk)
                dma("sp", kr_s[:, tsl], s.ap[0:64], [s.tk], [])
                for kind, base, dst_s, scale in (("rq", 9, rq_s, 1.0), ("rk", 25, rkT_s, 1.0 / 16.0)):
                    for h in range(8):
                        pr = h % 2
                        for c in range(2):
                            w = wtile(base + h * 2 + c)
                            proj(w, 0, 128, 2 * pr + c)
                        act(R0[pr].ap, ps[2 * pr][:], AF.Copy, [Tps[2 * pr]], [R0[pr].tk], scale=scale)
                        act(R1[pr].ap, ps[2 * pr + 1][:], AF.Copy, [Tps[2 * pr + 1]], [R1[pr].tk], scale=scale)
                        s0, s1 = stage(), stage()
                        tt("dve", T0.ap, R0[pr].ap, CK.ap, ALU.mult, [R0[pr].tk, CK.tk], [T0.tk])
                        tt("dve", T1.ap, R1[pr].ap, SK.ap, ALU.mult, [R1[pr].tk, SK.tk], [T1.tk])
                        tt("dve", s0.ap, T0.ap, T1.ap, ALU.subtract, [T0.tk, T1.tk], [s0.tk])
                        tt("dve", T2.ap, R1[pr].ap, CK.ap, ALU.mult, [R1[pr].tk, CK.tk], [T2.tk])
                        tt("dve", T3.ap, R0[pr].ap, SK.ap, ALU.mult, [R0[pr].tk, SK.tk], [T3.tk])
                        tt("dve", s1.ap, T2.ap, T3.ap, ALU.add, [T2.tk, T3.tk], [s1.tk])
                        dma("sp", dst_s[2 * h][:, tsl], s0.ap, [s0.tk], [])
                        dma("sp", dst_s[2 * h + 1][:, tsl], s1.ap, [s1.tk], [])
                for i in range(16):
                    w = wtile(41 + i)
                    pi = i % 2
                    proj(w, 0, 128, pi)
                    s = stage()
                    act(s.ap, ps[pi][:], AF.Silu, [Tps[pi]], [s.tk])
                    dma("sp", rg_s[i][:, tsl], s.ap, [s.tk], [])
                for i in range(16):
                    w = wtile(57 + i)
                    pi = i % 2
                    proj(w, 0, 128, pi)
                    act(R0[pi].ap, ps[pi][:], AF.Copy, [Tps[pi]], [R0[pi].tk])
                    s = stage()
                    gelu_tanh(None, s.ap, R0[pi].ap, T0, T1, [R0[pi].tk], [s.tk], None)
                    dma("sp", gu_s[i][:, tsl], s.ap, [s.tk], [])
                QSC = 192.0 ** -0.5
                for h in range(16):
                    pi = 2 + h % 2
                    mmg(ps[pi][:], [(WUQ.ap[:, kc, h, 0:128], ZQN.ap[:, kc, :]) for kc in range(4)],
                        [WUQ.tk, ZQN.tk], [Tps[pi]])
                    s = stage()
                    act(s.ap, ps[pi][:], AF.Copy, [Tps[pi]], [s.tk], scale=QSC)
                    dma("sp", qn_s[h][:, tsl], s.ap, [s.tk], [])
                    mmg(ps[0][0:64, :], [(WUQ.ap[:, kc, h, 128:192], ZQN.ap[:, kc, :]) for kc in range(4)],
                        [WUQ.tk, ZQN.tk], [Tps[0]])
                    mmg(ps[1][0:64, :], [(WUQ.ap[:, kc, h, 192:256], ZQN.ap[:, kc, :]) for kc in range(4)],
                        [WUQ.tk, ZQN.tk], [Tps[1]])
                    s = stage()
                    rope64(0, 1, QSC, s.ap[0:64], s.tk)
                    dma("sp", qr_s[h][:, tsl], s.ap[0:64], [s.tk], [])
                    pi = 4 + h % 2
                    mmg(ps[pi][:], [(WUKV.ap[:, kc, h, 0:128], ZKVN.ap[:, kc, :]) for kc in range(4)],
                        [WUKV.tk, ZKVN.tk], [Tps[pi]])
                    s = stage()
                    act(s.ap, ps[pi][:], AF.Copy, [Tps[pi]], [s.tk])
                    dma("sp", kT_s[h][:, tsl], s.ap, [s.tk], [])
                for blk in range(4):
                    vt = VT[0]
                    for j in range(4):
                        pi = 2 + j % 2
                        mmg(ps[pi][:], [(ZKVN.ap[:, kc, blk * 128:(blk + 1) * 128], WUKV.ap[:, kc, 4 * j:4 * j + 4, 128:256])
                                        for kc in range(4)], [WUKV.tk, ZKVN.tk], [Tps[pi]])
                        act(vt.ap[:, j * 512:(j + 1) * 512], ps[pi][:], AF.Copy, [Tps[pi]], [vt.tk])
                    r0 = t * TT + blk * 128
                    dma("sp", v_s[r0:r0 + 128, :], vt.ap, [vt.tk], [])
            P.barrier()

        def m1b_phase(l):
            ar.reset(base_off)
            XC = [Bf(TT, F32) for _ in range(4)]
            H = Bf(16 * TT, BF16, r="p (k t) -> p k t", k=16)
            W = [Bf(16 * 512, BF16) for _ in range(4)]
            SQ = [Bf(TT, F32) for _ in range(2)]
            RS = Bf(TT, F32)
            GV = Bf(4 * 2048, F32, r="p (b c) -> p b c", b=4)
            LNG, LNB = Bf(2048, F32), Bf(2048, F32)
            CT, STm = Bf(4 * 128, F32, r="p (b f) -> p b f", b=4), Bf(4 * 128, F32, r="p (b f) -> p b f", b=4)
            KS = Bf(512, F32)
            T0, T1, T2 = Bf(512, F32), Bf(512, F32), Bf(512, F32)
            OS = [Bf(512, BF16) for _ in range(3)]
            GO = [Bf(2048, BF16) for _ in range(2)]
            STAT = Bf(8, F32)
            LNscr = Bf(2048, F32)
            dma("sp", LNG.ap, lng_d.ap()[l], [], [LNG.tk])
            dma("sp", LNB.ap, lnb_d.ap()[l], [], [LNB.tk])
            c = {"w": 0, "o": 0}
            for t in range(NT):
                load_h_chunked(t, H, RS, SQ, XC, l, V_MIXPRE)
                r_t = t * TT
                dma("sp", CT.ap, costm_s[r_t:r_t + TT, :].rearrange("(b p) f -> p b f", p=128), [], [CT.tk])
                dma("sp", STm.ap, sintm_s[r_t:r_t + TT, :].rearrange("(b p) f -> p b f", p=128), [], [STm.tk])
                for wi_ in range(12):
                    w = W[c["w"] % 4]
                    c["w"] += 1
                    dma("pool", w.ap, wintm.ap()[l * 12 + wi_], [], [w.tk])
                    kind, j = wi_ // 4, wi_ % 4
                    for blk in range(4):
                        pi = blk % 2
                        mmg(ps[pi][:], [(H.ap[:, kc, blk * 128:(blk + 1) * 128], w.ap[:, kc * 512:(kc + 1) * 512])
                                        for kc in range(16)], [w.tk, H.tk], [Tps[pi]])
                        r0 = r_t + blk * 128
                        if kind == 1:
                            o = OS[c["o"] % 3]
                            c["o"] += 1
                            act(o.ap, ps[pi][:], AF.Copy, [Tps[pi]], [o.tk])
                            dma("sp", rvtm_s[r0:r0 + 128, j * 512:(j + 1) * 512], o.ap, [o.tk], [])
                        elif kind == 0:
                            act(KS.ap, ps[pi][:], AF.Copy, [Tps[pi]], [KS.tk], scale=1.0 / 16.0)
                            o = OS[c["o"] % 3]
                            c["o"] += 1
                            K4 = KS.ap.rearrange("p (h x f) -> p h x f", h=2, x=2)
                            O4 = o.ap.rearrange("p (h x f) -> p h x f", h=2, x=2)
                            for hh in range(2):
                                x1, x2 = K4[:, hh, 0, :], K4[:, hh, 1, :]
                                cb, sb = CT.ap[:, blk, :], STm.ap[:, blk, :]
                                tt("dve", T0.ap[:, 0:128], x1, cb, ALU.mult, [KS.tk, CT.tk], [T0.tk])
                                tt("dve", T1.ap[:, 0:128], x2, sb, ALU.mult, [KS.tk, STm.tk], [T1.tk])
                                tt("dve", O4[:, hh, 0, :], T0.ap[:, 0:128], T1.ap[:, 0:128], ALU.subtract, [T0.tk, T1.tk], [o.tk])
                                tt("dve", T2.ap[:, 0:128], x2, cb, ALU.mult, [KS.tk, CT.tk], [T2.tk])
                                tt("dve", T2.ap[:, 128:256], x1, sb, ALU.mult, [KS.tk, STm.tk], [T2.tk])
                                tt("dve", O4[:, hh, 1, :], T2.ap[:, 0:128], T2.ap[:, 128:256], ALU.add, [T2.tk], [o.tk])
                            dma("sp", rktm_s[r0:r0 + 128, j * 512:(j + 1) * 512], o.ap, [o.tk], [])
                        else:
                            act(KS.ap, ps[pi][:], AF.Copy, [Tps[pi]], [KS.tk])
                            gelu_tanh(None, GV.ap[:, blk, j * 512:(j + 1) * 512], KS.ap, T0, T1, [KS.tk], [GV.tk], None)
                for blk in range(4):
                    g = GV.ap[:, blk, :]
                    go = GO[blk % 2]
                    P.op("dve", lambda e, g=g: e.reduce_sum(out=STAT.ap[:, 0:1], in_=g, axis=mybir.AxisListType.X),
                         [GV.tk], [STAT.tk])
                    tt("dve", LNscr.ap, g, g, ALU.mult, [GV.tk], [LNscr.tk])
                    P.op("dve", lambda e: e.reduce_sum(out=STAT.ap[:, 1:2], in_=LNscr.ap, axis=mybir.AxisListType.X),
                         [LNscr.tk], [STAT.tk])
                    ts("dve", STAT.ap[:, 2:3], STAT.ap[:, 0:1], 1.0 / 2048, None, ALU.mult, None, [STAT.tk], [STAT.tk])
                    ts("dve", STAT.ap[:, 3:4], STAT.ap[:, 1:2], 1.0 / 2048, None, ALU.mult, None, [STAT.tk], [STAT.tk])
                    tt("dve", STAT.ap[:, 4:5], STAT.ap[:, 2:3], STAT.ap[:, 2:3], ALU.mult, [STAT.tk], [STAT.tk])
                    tt("dve", STAT.ap[:, 5:6], STAT.ap[:, 3:4], STAT.ap[:, 4:5], ALU.subtract, [STAT.tk], [STAT.tk])
                    act(STAT.ap[:, 6:7], STAT.ap[:, 5:6], AF.Sqrt, [STAT.tk], [STAT.tk], bias=EPS, scale=1.0)
                    P.op("dve", lambda e: e.reciprocal(out=STAT.ap[:, 6:7], in_=STAT.ap[:, 6:7]), [STAT.tk], [STAT.tk])
                    ts("dve", g, g, STAT.ap[:, 2:3], STAT.ap[:, 6:7], ALU.subtract, ALU.mult, [GV.tk, STAT.tk], [GV.tk])
                    tt("dve", g, g, LNG.ap, ALU.mult, [GV.tk, LNG.tk], [GV.tk])
                    tt("dve", go.ap, g, LNB.ap, ALU.add, [GV.tk, LNB.tk], [go.tk])
                    r0 = r_t + blk * 128
                    dma("sp", gvtm_s[r0:r0 + 128, :], go.ap, [go.tk], [])
            P.barrier()

        LOGG = [float(np.log1p(-(2.0 ** (-5.0 - h)))) for h in range(8)]

        def m2_phase(l):
            ar.reset(base_off)
            MASKA = Bf(2048, F32, r="p (j q) -> p j q", j=4)
            DTc = Bf(1024, F32, r="p (h i) -> p h i", h=8)
            QDEC = Bf(1024, F32, r="p (h i) -> p h i", h=8)
            KDEC = Bf(8, F32)
            MASKG = Bf(128, F32)
            cd = consts_d.ap()
            dma("sp", MASKA.ap, cd[:, C_MASKA:C_MASKA + 2048].rearrange("p (j q) -> p j q", j=4), [], [MASKA.tk])
            dma("sp", DTc.ap, cd[:, C_DT:C_DT + 1024].rearrange("p (h i) -> p h i", h=8), [], [DTc.tk])
            dma("sp", QDEC.ap, cd[:, C_QDEC:C_QDEC + 1024].rearrange("p (h i) -> p h i", h=8), [], [QDEC.tk])
            dma("sp", KDEC.ap, cd[:, C_KDEC:C_KDEC + 8], [], [KDEC.tk])
            dma("sp", MASKG.ap, cd[:, C_MASKG:C_MASKG + 128], [], [MASKG.tk])
            ONESB = Bf(128, BF16)
            P.op("pool", lambda e: e.memset(ONESB.ap, 1.0), [], [ONESB.tk])
            KR = Bf(S, BF16)
            dma("sp", KR.ap[0:64], kr_s, [], [KR.tk])
            QN = [Bf(TT, BF16) for _ in range(2)]
            QR = [Bf(TT, BF16) for _ in range(2)]
            KT = [Bf(TT, BF16) for _ in range(2)]
            VB = [Bf(512, BF16, r="p (j e) -> p j e", j=4) for _ in range(2)]
            PT = [Bf(TT, BF16) for _ in range(3)]
            RDEN = Bf(TT, F32)
            YA = [Bf(TT, BF16) for _ in range(2)]
            SF = Bf(8 * 512, F32, r="p (h c e) -> p h c e", h=8, c=2)
            SB = Bf(8 * 512, BF16, r="p (h c e) -> p h c e", h=8, c=2)
            SFtk = [Tk("sf") for _ in range(8)]
            SBtk = [Tk("sb") for _ in range(8)]
            P.op("pool", lambda e: e.memset(SF.ap, 0.0), [], SFtk)
            P.op("pool", lambda e: e.memset(SB.ap, 0.0), [], SBtk)
            RQ = [Bf(2 * TT, BF16, r="p (c t) -> p c t", c=2) for _ in range(2)]
            RKT = [Bf(2 * TT, BF16, r="p (c t) -> p c t", c=2) for _ in range(2)]
            RG = [Bf(2 * TT, BF16, r="p (c t) -> p c t", c=2) for _ in range(2)]
            RKM = [Bf(4 * 256, BF16, r="p (b d) -> p b d", b=4) for _ in range(2)]
            RVM = [Bf(4 * 256, BF16, r="p (b d) -> p b d", b=4) for _ in range(2)]
            PTR = [Bf(128, BF16) for _ in range(2)]
            QD = [Bf(256, BF16, r="p (c i) -> p c i", c=2) for _ in range(2)]
            KD = [Bf(256, BF16) for _ in range(2)]
            RO = Bf(2 * TT, F32, r="p (c t) -> p c t", c=2)
            SQ2 = [Bf(TT, F32) for _ in range(2)]
            MEAN, MSQ, VAR, RSTD = Bf(TT, F32), Bf(TT, F32), Bf(TT, F32), Bf(TT, F32)
            YB = [Bf(TT, BF16) for _ in range(2)]
            WSTf = Bf(512, F32, r="p (g i) -> p g i", g=4)
            WST = Bf(512, BF16, r="p (g i) -> p g i", g=4)
            GMB = Bf(2048, F32, r="p (g t) -> p g t", g=4)
            GVt = Bf(4 * 2048, BF16, r="p (b c) -> p b c", b=4)
            GU = Bf(16 * TT, BF16, r="p (k t) -> p k t", k=16)
            TMPF = [Bf(TT, F32) for _ in range(2)]
            YC = [Bf(TT, BF16) for _ in range(2)]
            dma("sp", WSTf.ap, wst_d.ap()[l].rearrange("p (g i) -> p g i", g=4), [], [WSTf.tk])
            dma("sp", GMB.ap, gmb_d.ap()[l].rearrange("p (g t) -> p g t", g=4), [], [GMB.tk])
            for g in range(4):
                tt("dve", WST.ap[:, g, :], WSTf.ap[:, g, :], MASKG.ap, ALU.mult, [WSTf.tk, MASKG.tk], [WST.tk])
            c = {"kv": 0, "pt": 0}
            for t in range(NT):
                tsl = slice(t * TT, (t + 1) * TT)
                nk = 4 * (t + 1)
                for h in range(16):
                    b = h % 2
                    dma("sp", QN[b].ap, qn_s[h][:, tsl], [], [QN[b].tk])
                    dma("sp", QR[b].ap[0:64], qr_s[h][:, tsl], [], [QR[b].tk])
                    for kb in range(t + 1):
                        kbuf, vbuf = KT[c["kv"] % 2], VB[c["kv"] % 2]
                        c["kv"] += 1
                        dma("sp", kbuf.ap, kT_s[h][:, kb * 512:(kb + 1) * 512], [], [kbuf.tk])
                        dma("sp", vbuf.ap, v_s[kb * 512:(kb + 1) * 512, h * 128:(h + 1) * 128].rearrange(
                            "(j p) e -> p j e", p=128), [], [vbuf.tk])
                        for j in range(4):
                            kt = kb * 4 + j
                            pi = kt % 2
                            mmg(ps[pi][:], [(kbuf.ap[:, j * 128:(j + 1) * 128], QN[b].ap),
                                            (KR.ap[0:64, kt * 128:(kt + 1) * 128], QR[b].ap[0:64])],
                                [kbuf.tk, QN[b].tk, KR.tk, QR[b].tk], [Tps[pi]])
                            pt = PT[c["pt"] % 3]
                            c["pt"] += 1
                            act(pt.ap, ps[pi][:], AF.Exp, [Tps[pi]], [pt.tk])
                            if kb == t:
                                tt("dve", pt.ap, pt.ap, MASKA.ap[:, j, :], ALU.mult, [pt.tk, MASKA.tk], [pt.tk])
                            P.op("pe", lambda e, b=b, pt=pt, kt=kt: e.matmul(ps[2 + b][:], lhsT=ONESB.ap, rhs=pt.ap,
                                                                             start=(kt == 0), stop=(kt == nk - 1)),
                                 [ONESB.tk, pt.tk], [Tps[2 + b]])
                            P.op("pe", lambda e, b=b, pt=pt, kt=kt, vbuf=vbuf, j=j: e.matmul(
                                ps[4 + b][:], lhsT=vbuf.ap[:, j, :], rhs=pt.ap, start=(kt == 0), stop=(kt == nk - 1)),
                                 [vbuf.tk, pt.tk], [Tps[4 + b]])
                    P.op("dve", lambda e, b=b: e.reciprocal(out=RDEN.ap, in_=ps[2 + b][:]), [Tps[2 + b]], [RDEN.tk])
                    ya = YA[b]
                    tt("dve", ya.ap, ps[4 + b][:], RDEN.ap, ALU.mult, [Tps[4 + b], RDEN.tk], [ya.tk])
                    dma("sp", ya_s[h][:, tsl], ya.ap, [ya.tk], [])
                r_t = t * TT
                for h in range(8):
                    b = h % 2
                    cdh = float(np.exp(128.0 * LOGG[h]))
                    dma("sp", RQ[b].ap, rq_s[2 * h:2 * h + 2, :, tsl].rearrange("c p t -> p c t"), [], [RQ[b].tk])
                    dma("sp", RKT[b].ap, rkT_s[2 * h:2 * h + 2, :, tsl].rearrange("c p t -> p c t"), [], [RKT[b].tk])
                    dma("sp", RG[b].ap, rg_s[2 * h:2 * h + 2, :, tsl].rearrange("c p t -> p c t"), [], [RG[b].tk])
                    dma("sp", RKM[b].ap, rktm_s[r_t:r_t + TT, h * 256:(h + 1) * 256].rearrange("(b p) d -> p b d", p=128),
                        [], [RKM[b].tk])
                    dma("sp", RVM[b].ap, rvtm_s[r_t:r_t + TT, h * 256:(h + 1) * 256].rearrange("(b p) d -> p b d", p=128),
                        [], [RVM[b].tk])
                    for blk in range(4):
                        bs = slice(blk * 128, (blk + 1) * 128)
                        mmg(ps[0][:, 0:128], [(RKT[b].ap[:, cc, bs], RQ[b].ap[:, cc, bs]) for cc in range(2)],
                            [RKT[b].tk, RQ[b].tk], [Tps[0]])
                        ptr = PTR[blk % 2]
                        tt("dve", ptr.ap, ps[0][:, 0:128], DTc.ap[:, h, :], ALU.mult, [Tps[0], DTc.tk], [ptr.tk])
                        qd = QD[blk % 2]
                        for cc in range(2):
                            tt("pool", qd.ap[:, cc, :], RQ[b].ap[:, cc, bs], QDEC.ap[:, h, :], ALU.mult,
                               [RQ[b].tk, QDEC.tk], [qd.tk])
                        for ec in range(2):
                            es_ = slice(ec * 128, (ec + 1) * 128)
                            pairs = [(RVM[b].ap[:, blk, es_], ptr.ap)] + [(SB.ap[:, h, cc, es_], qd.ap[:, cc, :])
                                                                          for cc in range(2)]
                            mmg(ps[1 + ec][:, 0:128], pairs, [RVM[b].tk, ptr.tk, SBtk[h], qd.tk], [Tps[1 + ec]])
                            act(RO.ap[:, ec, bs], ps[1 + ec][:, 0:128], AF.Copy, [Tps[1 + ec]], [RO.tk])
                        kd = KD[blk % 2]
                        ts("pool", kd.ap, RKM[b].ap[:, blk, :], KDEC.ap[:, h:h + 1], None, ALU.mult, None,
                           [RKM[b].tk, KDEC.tk], [kd.tk])
                        for cc in range(2):
                            mmg(ps[3 + cc][:, 0:256], [(kd.ap[:, cc * 128:(cc + 1) * 128], RVM[b].ap[:, blk, :])],
                                [kd.tk, RVM[b].tk], [Tps[3 + cc]])
                            stt("dve", SF.ap[:, h, cc, :], SF.ap[:, h, cc, :], cdh, ps[3 + cc][:, 0:256], ALU.mult, ALU.add,
                                [SFtk[h], Tps[3 + cc]], [SFtk[h]])
                            act(SB.ap[:, h, cc, :], SF.ap[:, h, cc, :], AF.Copy, [SFtk[h]], [SBtk[h]])
                    mmg(ps[5][:], [(ones, RO.ap[:, cc, :]) for cc in range(2)], [Tones, RO.tk], [Tps[5]])
                    for cc in range(2):
                        act(SQ2[cc].ap, RO.ap[:, cc, :], AF.Square, [RO.tk], [SQ2[cc].tk])
                    mmg(ps[6][:], [(ones, SQ2[cc].ap) for cc in range(2)], [Tones, SQ2[0].tk, SQ2[1].tk], [Tps[6]])
                    ts("dve", MEAN.ap, ps[5][:], 1.0 / 256, None, ALU.mult, None, [Tps[5]], [MEAN.tk])
                    ts("dve", MSQ.ap, ps[6][:], 1.0 / 256, None, ALU.mult, None, [Tps[6]], [MSQ.tk])
                    tt("dve", VAR.ap, MEAN.ap, MEAN.ap, ALU.mult, [MEAN.tk], [VAR.tk])
                    tt("dve", VAR.ap, MSQ.ap, VAR.ap, ALU.subtract, [MSQ.tk, VAR.tk], [VAR.tk])
                    act(RSTD.ap, VAR.ap, AF.Sqrt, [VAR.tk], [RSTD.tk], bias=1e-5, scale=1.0)
                    P.op("dve", lambda e: e.reciprocal(out=RSTD.ap, in_=RSTD.ap), [RSTD.tk], [RSTD.tk])
                    for cc in range(2):
                        tt("dve", SQ2[cc].ap, RO.ap[:, cc, :], MEAN.ap, ALU.subtract, [RO.tk, MEAN.tk], [SQ2[cc].tk])
                        tt("dve", SQ2[cc].ap, SQ2[cc].ap, RSTD.ap, ALU.mult, [SQ2[cc].tk, RSTD.tk], [SQ2[cc].tk])
                        yb = YB[cc]
                        tt("pool", yb.ap, SQ2[cc].ap, RG[b].ap[:, cc, :], ALU.mult, [SQ2[cc].tk, RG[b].tk], [yb.tk])
                        dma("sp", yb_s[2 * h + cc][:, tsl], yb.ap, [yb.tk], [])
                dma("sp", GVt.ap, gvtm_s[r_t:r_t + TT, :].rearrange("(b p) c -> p b c", p=128), [], [GVt.tk])
                dma("sp", GU.ap, gu_s[:, :, tsl].rearrange("k p t -> p k t"), [], [GU.tk])
                for cc in range(16):
                    g = cc // 4
                    pi = cc % 2

                    def f(e, cc=cc, g=g, pi=pi):
                        for blk in range(4):
                            ins = e.matmul(ps[pi][:, blk * 128:(blk + 1) * 128], lhsT=GVt.ap[:, blk, cc * 128:(cc + 1) * 128],
                                           rhs=WST.ap[:, g, :], start=True, stop=True)
                        return ins
                    P.op("pe", f, [GVt.tk, WST.tk], [Tps[pi]])
                    tf = TMPF[cc % 2]
                    tt("dve", tf.ap, ps[pi][:], GMB.ap[:, g, :], ALU.add, [Tps[pi], GMB.tk], [tf.tk])
                    yc = YC[cc % 2]
                    tt("dve", yc.ap, tf.ap, GU.ap[:, cc, :], ALU.mult, [tf.tk, GU.tk], [yc.tk])
                    dma("sp", yc_s[cc][:, tsl], yc.ap, [yc.tk], [])
            P.barrier()

        def m3_phase(l):
            ar.reset(base_off)
            X = Bf(16 * TT, F32, r="p (k t) -> p k t", k=16)
            H = Bf(16 * TT, BF16, r="p (k t) -> p k t", k=16)
            SQ = [Bf(TT, F32) for _ in range(2)]
            RS = Bf(TT, F32)
            YB3 = [Bf(16 * TT, BF16, r="p (k t) -> p k t", k=16) for _ in range(3)]
            M = Bf(16 * TT, F32, r="p (k t) -> p k t", k=16)
            MB = Bf(16 * TT, BF16, r="p (k t) -> p k t", k=16)
            W = [Bf(2048, BF16) for _ in range(4)]
            G = [Bf(TT, F32) for _ in range(2)]
            TM = [Bf(TT, F32) for _ in range(2)]
            srcs = (ya_s, yb_s, yc_s)
            c = {"w": 0, "n": 0}

            def wtile(d, idx):
                w = W[c["w"] % 4]
                c["w"] += 1
                dma("pool", w.ap, d.ap()[idx], [], [w.tk])
                return w

            for t in range(NT):
                tsl = slice(t * TT, (t + 1) * TT)
                load_xh(t, X, H, RS, SQ, l)
                for b in range(3):
                    dma("sp", YB3[b].ap, srcs[b][:, :, tsl].rearrange("k p t -> p k t"), [], [YB3[b].tk])
                for mc in range(16):
                    for b in range(3):
                        n = c["n"]
                        c["n"] += 1
                        wg = wtile(wg_d, l * 48 + b * 16 + mc)
                        wb = wtile(wbr_d, l * 48 + b * 16 + mc)
                        pg, pb = (2 * n) % 4, (2 * n + 1) % 4
                        mmg(ps[pg][:], [(wg.ap[:, kc * 128:(kc + 1) * 128], H.ap[:, kc, :]) for kc in range(16)],
                            [wg.tk, H.tk], [Tps[pg]])
                        mmg(ps[pb][:], [(wb.ap[:, kc * 128:(kc + 1) * 128], YB3[b].ap[:, kc, :]) for kc in range(16)],
                            [wb.tk, YB3[b].tk], [Tps[pb]])
                        g = G[n % 2]
                        act(g.ap, ps[pg][:], AF.Sigmoid, [Tps[pg], Tvecs], [g.tk], bias=vcol(l, V_BG, b * 16 + mc), scale=1.0)
                        if b == 0:
                            tt("dve", M.ap[:, mc, :], g.ap, ps[pb][:], ALU.mult, [g.tk, Tps[pb]], [M.tk])
                        else:
                            tm = TM[n % 2]
                            tt("dve", tm.ap, g.ap, ps[pb][:], ALU.mult, [g.tk, Tps[pb]], [tm.tk])
                            tt("dve", M.ap[:, mc, :], M.ap[:, mc, :], tm.ap, ALU.add, [M.tk, tm.tk], [M.tk])
                    act(MB.ap[:, mc, :], M.ap[:, mc, :], AF.Copy, [M.tk], [MB.tk])
                for mc in range(16):
                    w = wtile(wo2_d, l * 16 + mc)
                    pi = 4 + mc % 2
                    mmg(ps[pi][:], [(w.ap[:, kc * 128:(kc + 1) * 128], MB.ap[:, kc, :]) for kc in range(16)],
                        [w.tk, MB.tk], [Tps[pi]])
                    act(M.ap[:, mc, :], ps[pi][:], AF.Copy, [Tps[pi]], [M.tk])
                    if mc > 0:
                        stat_mm(SQ, mc - 1, ps[7], Tps[7])
                    stat_square(M.ap[:, mc, :], M.tk, SQ, mc)
                stat_mm(SQ, 15, ps[7], Tps[7])
                stat_finish(ps[7], Tps[7], RS, D)
                for mc in range(16):
                    tm = TM[mc % 2]
                    stt("dve", tm.ap, M.ap[:, mc, :], vcol(l, V_MIXPOST, mc), RS.ap, ALU.mult, ALU.mult,
                        [M.tk, RS.tk, Tvecs], [tm.tk])
                    tt("dve", X.ap[:, mc, :], X.ap[:, mc, :], tm.ap, ALU.add, [X.tk, tm.tk], [X.tk])
                dma("sp", yT_v[:, :, tsl], X.ap, [X.tk], TY[t])
            P.barrier()

        if "mix" in stages:
            rope_tables_phase()

        for l in range(L):
            if "f1" in stages:
                ffn_phase(l, f1wi, f1wo, V_F1PRE, V_F1POST)
            if "mix" in stages:
                m1a_phase(l)
                m1b_phase(l)
                m2_phase(l)
                m3_phase(l)
            if "f2" in stages:
                ffn_phase(l, f2wi, f2wo, V_F2PRE, V_F2POST)
        P.barrier(["sp"])

        sems = {k: es.enter_context(nc.semaphore(k)) for k in P.cnt.keys()}
        with nc.Block() as block:
            @block.tensor
            def _(e):
                P.replay("pe", e, sems)

            @block.scalar
            def _(e):
                P.replay("act", e, sems)

            @block.vector
            def _(e):
                P.replay("dve", e, sems)

            @block.gpsimd
            def _(e):
                P.replay("pool", e, sems)

            @block.sync
            def _(e):
                P.replay("sp", e, sems)
    return nc


def prep_shared(inp, L):
    out = {}
    vec = np.zeros((128, NV_L * L), np.float32)

    def put(l, base, v):
        n = v.shape[0] // 128
        vec[:, l * NV_L + base:l * NV_L + base + n] = v.reshape(n, 128).T

    for l in range(L):
        put(l, V_F1PRE, inp["ffn1_pre_g"][l])
        put(l, V_F1POST, inp["ffn1_post_g"][l])
        put(l, V_MIXPRE, inp["mix_pre_g"][l])
        put(l, V_QN, inp["q_norm_g"][l])
        put(l, V_KVN, inp["kv_norm_g"][l])
        put(l, V_BG, inp["b_gate"][l])
        put(l, V_MIXPOST, inp["mix_post_g"][l])
        put(l, V_F2PRE, inp["ffn2_pre_g"][l])
        put(l, V_F2POST, inp["ffn2_post_g"][l])
    out["vecs"] = vec

    def wi_tiles(w):
        r = np.empty((L, NHC, 128, 16, 2, 128), np.float32)
        for l in range(L):
            r[l] = w[l].reshape(16, 128, 2, NHC, 128).transpose(3, 1, 0, 2, 4)
        return r.reshape(L * NHC, 128, 4096)

    def wo_tiles(w):
        r = np.empty((L, 16, 128, NHC, 128), np.float32)
        for l in range(L):
            r[l] = w[l].reshape(NHC, 128, 16, 128).transpose(2, 1, 0, 3)
        return r.reshape(L * 16, 128, DFF)

    out["f1wi"] = wi_tiles(inp["ffn1_wi"])
    out["f1wo"] = wo_tiles(inp["ffn1_wo"])
    out["f2wi"] = wi_tiles(inp["ffn2_wi"])
    out["f2wo"] = wo_tiles(inp["ffn2_wo"])
    return out


def tiles128(W):
    K, M = W.shape
    return np.ascontiguousarray(W.reshape(K // 128, 128, M // 128, 128).transpose(2, 1, 0, 3)).reshape(M // 128, 128, K)


def make_consts():
    c = np.zeros((128, NC_TOT), np.float32)
    kp = np.arange(128)[:, None]
    qf = np.arange(512)[None, :]
    for j in range(4):
        c[:, C_MASKA + j * 512:C_MASKA + (j + 1) * 512] = (((j * 128 + kp) // 64) <= (qf // 64)).astype(np.float32)
    i = np.arange(128)[None, :]
    jj = np.arange(128)[:, None]
    for h in range(8):
        lg = np.log1p(-(2.0 ** (-5.0 - h)))
        c[:, C_DT + h * 128:C_DT + (h + 1) * 128] = np.exp(np.abs(i - jj) * lg) * ((i // 64) >= (jj // 64))
        c[:, C_QDEC + h * 128:C_QDEC + (h + 1) * 128] = np.exp((i + 1.0) * lg)
        c[:, C_KDEC + h] = np.exp((127.0 - np.arange(128)) * lg)
    c[:, C_MASKG:C_MASKG + 128] = ((i // 64) >= (jj // 64)).astype(np.float32)
    inv_k = (np.float32(10000.0) ** (-np.arange(0, 256, 2, dtype=np.float32) / np.float32(256))).astype(np.float32)
    inv_r = (np.float32(10000.0) ** (-np.arange(0, 64, 2, dtype=np.float32) / np.float32(64))).astype(np.float32)
    c[:, C_INV] = inv_k
    c[:, C_INV + 1] = np.tile(inv_r, 4)
    c[:, C_INV + 2] = np.where((np.arange(128) % 64) < 32, -1.0, 1.0)
    c[:, C_INVROW:C_INVROW + 128] = inv_k[None, :]
    return c


def prep_mixer(inp, L):
    out = {"consts": make_consts()}
    SPL = np.cumsum([0, 512, 512, 64, 2048, 2048, 2048, 2048, 2048, 2048])
    winfm = np.empty((L, 73, 128, 2048), np.float32)
    wintm = np.empty((L, 12, 128, 16 * 512), np.float32)
    wuq = np.empty((L, 128, 4, 16, 256), np.float32)
    wukv = np.empty((L, 128, 4 * 4096), np.float32)
    swap = (np.arange(64) + 32) % 64
    for l in range(L):
        w = inp["w_in"][l]
        winfm[l, 0:4] = tiles128(w[:, SPL[0]:SPL[1]])
        winfm[l, 4:8] = tiles128(w[:, SPL[1]:SPL[2]])
        kr = w[:, SPL[2]:SPL[3]]
        winfm[l, 8:9] = tiles128(np.concatenate([kr, kr[:, swap]], 1))
        winfm[l, 9:25] = tiles128(w[:, SPL[3]:SPL[4]])
        winfm[l, 25:41] = tiles128(w[:, SPL[4]:SPL[5]])
        winfm[l, 41:57] = tiles128(w[:, SPL[6]:SPL[7]])
        winfm[l, 57:73] = tiles128(w[:, SPL[7]:SPL[8]])
        for i, (lo) in enumerate([SPL[4], SPL[5], SPL[8]]):
            blk = w[:, lo:lo + 2048].reshape(16, 128, 4, 512).transpose(2, 1, 0, 3)
            wintm[l, i * 4:(i + 1) * 4] = blk.reshape(4, 128, 16 * 512)
        q = inp["w_uq"][l].reshape(4, 128, 16, 192).transpose(1, 0, 2, 3)
        wuq[l, :, :, :, 0:192] = q
        wuq[l, :, :, :, 192:256] = q[:, :, :, 128 + swap]
        wukv[l] = inp["w_ukv"][l].reshape(4, 128, 4096).transpose(1, 0, 2).reshape(128, 4 * 4096)
    out["winfm"] = winfm.reshape(L * 73, 128, 2048)
    out["wintm"] = wintm.reshape(L * 12, 128, 16 * 512)
    out["wuq"] = wuq.reshape(L, 128, 4 * 16 * 256)
    out["wukv"] = wukv
    out["wg"] = np.concatenate([tiles128(inp["w_gate"][l]) for l in range(L)], 0)
    out["wbr"] = np.concatenate([tiles128(inp["w_br"][l, b]) for l in range(L) for b in range(3)], 0)
    out["wo2"] = np.concatenate([tiles128(inp["w_o"][l]) for l in range(L)], 0)
    out["lng"] = np.ascontiguousarray(np.broadcast_to(inp["gm_ln_g"][:, None, :], (L, 128, 2048)))
    out["lnb"] = np.ascontiguousarray(np.broadcast_to(inp["gm_ln_b"][:, None, :], (L, 128, 2048)))
    out["wst"] = np.ascontiguousarray(inp["gm_w_s"].transpose(0, 3, 1, 2)).reshape(L, 128, 512)
    gb = np.broadcast_to(inp["gm_b_s"][:, None, :, None, :], (L, 128, 4, 4, 128))
    out["gmb"] = np.ascontiguousarray(gb).reshape(L, 128, 2048)
    return out


def run(inp, S, L, n_seq, stages=("f1", "mix", "f2")):
    nc = build_program(S, L, stages)
    shared = prep_shared(inp, L)
    if "mix" in stages:
        shared.update(prep_mixer(inp, L))
    in_maps = []
    for b in range(n_seq):
        m = dict(shared)
        m["xT"] = np.ascontiguousarray(inp["x"][b, :S].T)
        if "mix" in stages:
            p = np.asarray(inp["pos"][b, :S]).astype(np.int32)
            m["posb"] = np.ascontiguousarray(np.broadcast_to(p[None, :], (128, S)))
            m["post"] = np.ascontiguousarray(p.reshape(S // 128, 128).T)
        in_maps.append(m)
    res = run_bass_kernel_spmd(nc, in_maps, core_ids=list(range(n_seq)))
    return np.stack([np.ascontiguousarray(r["yT"].T) for r in res.results], 0)


def kernel(**inputs):
    inp = {k: np.asarray(v) for k, v in inputs.items()}
    B, S, _ = inp["x"].shape
    L = inp["ffn1_pre_g"].shape[0]
    return run(inp, S, L, B).astype(np.float32)
```
